# Optimizing a Trainium2 kernel written in Bass

```python
import jax, jax.numpy as jnp
from jax import lax
import numpy as np

D_MODEL = 2048
BATCH = 32
SEQ = 256
DEPTH = 2
DEC_BATCH = 4
DEC_SEQ = 2048
PAST_LEN = 256

GRID_W = 64
CHUNK = 128
EPS = 1e-6
M_HEADS = 4
M_DK = 256
M_DV = 256
M_WIDTH = M_HEADS * M_DV
R_HEADS = 8
R_DK = 128
R_DV = 128
R_WIDTH = R_HEADS * R_DV
ROPE_BASE = 10000.0
ROPE_FREQS = R_DK // 4
S_HEADS = 16
S_P = 64
S_GROUPS = 2
S_N = 128
S_WIDTH = S_HEADS * S_P
CONV_W = 4
CONV_CH = S_WIDTH + 2 * S_GROUPS * S_N
D_FF = 4 * D_MODEL
N_BRANCH = 3
IN_SIZES = (M_HEADS * M_DK, M_HEADS * M_DK, M_WIDTH, M_WIDTH, 2 * M_HEADS, 2 * M_HEADS,
            R_HEADS * R_DK, R_HEADS * R_DK, R_WIDTH, R_WIDTH,
            S_WIDTH, CONV_CH, 2 * S_HEADS,
            N_BRANCH * D_MODEL)
IN_COLS = sum(IN_SIZES)

kernel_name = "hybrid_mlstm_retnet_ssd_diffusion_step"


def rmsnorm(x, g):
    xf = x.astype(jnp.float32)
    y = xf * lax.rsqrt(jnp.mean(xf * xf, -1, keepdims=True) + EPS)
    return (y * g.astype(jnp.float32)).astype(x.dtype)


def head_rmsnorm(x, g):
    H, d = x.shape[-2:]
    y = x * lax.rsqrt(jnp.mean(x * x, -1, keepdims=True) + EPS)
    return y * g.astype(jnp.float32).reshape(H, d)


def flip(a):
    return jnp.flip(a, axis=1)


def rope_tables(L):
    n_rows = L // GRID_W
    rows = jnp.repeat(jnp.arange(n_rows, dtype=jnp.float32), GRID_W)
    cols = jnp.tile(jnp.arange(GRID_W, dtype=jnp.float32), n_rows)
    inv = ROPE_BASE ** (-jnp.arange(ROPE_FREQS, dtype=jnp.float32) / ROPE_FREQS)
    ang = jnp.concatenate([rows[:, None] * inv, cols[:, None] * inv], -1)
    return jnp.cos(ang), jnp.sin(ang)


def apply_rope(x, cos, sin):
    x1, x2 = jnp.split(x, 2, axis=-1)
    c = cos[None, :, None, :]
    s = sin[None, :, None, :]
    return jnp.concatenate([x1 * c - x2 * s, x2 * c + x1 * s], -1)


def conv_centred(x, w, b):
    left = (CONV_W - 1) // 2
    y = lax.conv_general_dilated(x, w[:, None, :], window_strides=(1,),
                                 padding=[(left, CONV_W - 1 - left)],
                                 dimension_numbers=('NWC', 'WIO', 'NWC'),
                                 feature_group_count=x.shape[-1])
    return y + b


def mlstm_chunked(q, k, v, i_pre, log_f, state0):
    C0, n0, m0 = state0
    B, L, H, dk = q.shape
    nc = L // CHUNK
    qc = q.reshape(B, nc, CHUNK, H, dk)
    kc = k.reshape(B, nc, CHUNK, H, dk)
    vc = v.reshape(B, nc, CHUNK, H, -1)
    ih = jnp.swapaxes(i_pre.reshape(B, nc, CHUNK, H), 2, 3)
    bh = jnp.cumsum(jnp.swapaxes(log_f.reshape(B, nc, CHUNK, H), 2, 3), -1)
    b_last = bh[..., -1]
    causal = jnp.tril(jnp.ones((CHUNK, CHUNK), dtype=bool))
    logw = jnp.where(causal, bh[..., :, None] - bh[..., None, :] + ih[..., None, :], -jnp.inf)
    m_loc = jnp.max(logw, -1)
    s = jnp.einsum('bcihd,bcjhd->bchij', qc, kc) * jnp.exp(logw - m_loc[..., None])
    a_num = jnp.einsum('bchij,bcjhe->bcihe', s, vc)
    a_den = jnp.swapaxes(jnp.sum(s, -1), 2, 3)
    logw_end = b_last[..., None] - bh + ih
    m_chunk = jnp.max(logw_end, -1)
    w_end = jnp.exp(logw_end - m_chunk[..., None])
    u_C = jnp.einsum('bchj,bcjhd,bcjhe->bchde', w_end, kc, vc)
    u_n = jnp.einsum('bchj,bcjhd->bchd', w_end, kc)

    def step(carry, xs):
        C, n, m = carry
        bl, mc, uC, un = xs
        m_new = jnp.maximum(bl + m, mc)
        a = jnp.exp(bl + m - m_new)
        e = jnp.exp(mc - m_new)
        C_new = a[..., None, None] * C + e[..., None, None] * uC
        n_new = a[..., None] * n + e[..., None] * un
        return (C_new, n_new, m_new), (C, n, m)

    xs = (jnp.moveaxis(b_last, 1, 0), jnp.moveaxis(m_chunk, 1, 0),
          jnp.moveaxis(u_C, 1, 0), jnp.moveaxis(u_n, 1, 0))
    fin, prev = lax.scan(step, (C0, n0, m0), xs)
    C_prev = jnp.moveaxis(prev[0], 0, 1)
    n_prev = jnp.moveaxis(prev[1], 0, 1)
    m_prev = jnp.moveaxis(prev[2], 0, 1)
    g = bh + m_prev[..., None]
    m_tot = jnp.maximum(m_loc, g)
    e_intra = jnp.swapaxes(jnp.exp(m_loc - m_tot), 2, 3)
    e_inter = jnp.swapaxes(jnp.exp(g - m_tot), 2, 3)
    num = (e_intra[..., None] * a_num
           + e_inter[..., None] * jnp.einsum('bcihd,bchde->bcihe', qc, C_prev))
    den = e_intra * a_den + e_inter * jnp.einsum('bcihd,bchd->bcih', qc, n_prev)
    floor = jnp.exp(-jnp.swapaxes(m_tot, 2, 3))
    h = num / jnp.maximum(jnp.abs(den), floor)[..., None]
    return h.reshape(B, L, H, -1), fin


def retention_chunked(q, k, v, log_gamma, s0):
    B, L, H, dk = q.shape
    nc = L // CHUNK
    qc = q.reshape(B, nc, CHUNK, H, dk)
    kc = k.reshape(B, nc, CHUNK, H, dk)
    vc = v.reshape(B, nc, CHUNK, H, -1)
    pos = jnp.arange(CHUNK, dtype=jnp.float32)
    rel = pos[:, None] - pos[None, :]
    decay = jnp.exp(jnp.where(rel >= 0, rel[None] * log_gamma[:, None, None], -jnp.inf))
    scores = jnp.einsum('bcihd,bcjhd->bchij', qc, kc) * decay
    o_intra = jnp.einsum('bchij,bcjhe->bcihe', scores, vc)
    w_k = jnp.exp((CHUNK - 1 - pos)[:, None] * log_gamma[None, :])
    u = jnp.einsum('bcjhd,jh,bcjhe->bchde', kc, w_k, vc)
    chunk_decay = jnp.exp(CHUNK * log_gamma)[:, None, None]

    def step(s, u_c):
        return chunk_decay * s + u_c, s

    s_fin, s_prev = lax.scan(step, s0, jnp.moveaxis(u, 1, 0))
    s_prev = jnp.moveaxis(s_prev, 0, 1)
    w_q = jnp.exp((pos + 1)[:, None] * log_gamma[None, :])
    o_inter = jnp.einsum('bcihd,ih,bchde->bcihe', qc, w_q, s_prev)
    return (o_intra + o_inter).reshape(B, L, H, -1), s_fin


def ssd_chunked(x, dt, A, Bm, Cm, h0):
    B, L, H, P = x.shape
    G, N = Bm.shape[-2:]
    K = H // G
    nc = L // CHUNK
    xc = x.reshape(B, nc, CHUNK, G, K, P)
    dtc = dt.reshape(B, nc, CHUNK, G, K)
    Bc = Bm.reshape(B, nc, CHUNK, G, N)
    Cc = Cm.reshape(B, nc, CHUNK, G, N)
    cs = jnp.cumsum(jnp.moveaxis(dtc * A.reshape(G, K), 2, -1), -1)
    causal = jnp.tril(jnp.ones((CHUNK, CHUNK), dtype=bool))
    decay = jnp.exp(jnp.where(causal, cs[..., :, None] - cs[..., None, :], -jnp.inf))
    cb = jnp.einsum('bcigs,bcjgs->bcgij', Cc, Bc)
    y_intra = jnp.einsum('bcgij,bcgkij,bcjgk,bcjgkp->bcigkp', cb, decay, dtc, xc)
    w_end = jnp.exp(cs[..., -1:] - cs)
    u = jnp.einsum('bcgkj,bcjgk,bcjgkp,bcjgs->bcgkps', w_end, dtc, xc, Bc)
    chunk_decay = jnp.exp(cs[..., -1])

    def step(h, xs):
        a, u_c = xs
        return a[..., None, None] * h + u_c, h

    fin, h_prev = lax.scan(step, h0.reshape(B, G, K, P, N),
                           (jnp.moveaxis(chunk_decay, 1, 0), jnp.moveaxis(u, 1, 0)))
    h_prev = jnp.moveaxis(h_prev, 0, 1)
    y_inter = jnp.einsum('bcigs,bcgkps,bcgki->bcigkp', Cc, h_prev, jnp.exp(cs))
    return (y_intra + y_inter).reshape(B, L, H, P), fin.reshape(B, H, P, N)


def token_mixers(h, lp, st0, rope):
    B, L, _ = h.shape
    f32 = lambda a: a.astype(jnp.float32)
    split_at = [int(s) for s in np.cumsum(IN_SIZES)[:-1]]
    (mq, mk, mv, mo, mi, mf, rq, rk, rv, rg, sz, sxbc, sdt, gl) = jnp.split(h @ lp['w_in'], split_at, axis=-1)
    mC0, mn0, mm0, r0, s0 = (f32(a) for a in st0)

    q = f32(mq).reshape(B, L, M_HEADS, M_DK)
    k = f32(mk).reshape(B, L, M_HEADS, M_DK) * (M_DK ** -0.5)
    v = f32(mv).reshape(B, L, M_HEADS, M_DV)
    ig = f32(mi).reshape(B, L, 2, M_HEADS) + f32(lp['m_igate_b'])
    lf = jax.nn.log_sigmoid(f32(mf).reshape(B, L, 2, M_HEADS) + f32(lp['m_fgate_b']))
    hm_f, ms_f = mlstm_chunked(q, k, v, ig[:, :, 0], lf[:, :, 0], (mC0[:, 0], mn0[:, 0], mm0[:, 0]))
    hm_b, ms_b = mlstm_chunked(flip(q), flip(k), flip(v), flip(ig[:, :, 1]), flip(lf[:, :, 1]),
                               (mC0[:, 1], mn0[:, 1], mm0[:, 1]))
    y_m = head_rmsnorm(hm_f + flip(hm_b), lp['m_norm_g']).reshape(B, L, M_WIDTH) * jax.nn.sigmoid(f32(mo))

    q = f32(rq).reshape(B, L, R_HEADS, R_DK)
    k = f32(rk).reshape(B, L, R_HEADS, R_DK) * (R_DK ** -0.5)
    if rope is not None:
        q = apply_rope(q, *rope)
        k = apply_rope(k, *rope)
    v = f32(rv).reshape(B, L, R_HEADS, R_DV)
    log_gamma = -jnp.exp(f32(lp['r_decay']))
    or_f, rs_f = retention_chunked(q, k, v, log_gamma[0], r0[:, 0])
    or_b, rs_b = retention_chunked(flip(q), flip(k), flip(v), log_gamma[1], r0[:, 1])
    y_r = head_rmsnorm(or_f + flip(or_b), lp['r_norm_g']).reshape(B, L, R_WIDTH) * jax.nn.silu(f32(rg))

    xbc = jax.nn.silu(conv_centred(f32(sxbc), f32(lp['s_conv_w']), f32(lp['s_conv_b'])))
    xs, bs, cs = jnp.split(xbc, [S_WIDTH, S_WIDTH + S_GROUPS * S_N], axis=-1)
    xs = xs.reshape(B, L, S_HEADS, S_P)
    bs = bs.reshape(B, L, S_GROUPS, S_N)
    cs = cs.reshape(B, L, S_GROUPS, S_N)
    dt = jax.nn.softplus(f32(sdt).reshape(B, L, 2, S_HEADS) + f32(lp['s_dt_bias']))
    A = -jnp.exp(f32(lp['s_a_log']))
    ys_f, ss_f = ssd_chunked(xs, dt[:, :, 0], A[0], bs, cs, s0[:, 0])
    ys_b, ss_b = ssd_chunked(flip(xs), flip(dt[:, :, 1]), A[1], flip(bs), flip(cs), s0[:, 1])
    y_s = (ys_f + flip(ys_b) + f32(lp['s_d'])[:, None] * xs).reshape(B, L, S_WIDTH)
    y_s = rmsnorm(y_s * jax.nn.silu(f32(sz)), lp['s_norm_g'])

    dtp = h.dtype
    gates = jax.nn.sigmoid(f32(gl).reshape(B, L, N_BRANCH, D_MODEL)).astype(dtp)
    merged = (gates[:, :, 0] * (y_m.astype(dtp) @ lp['w_br_m'])
              + gates[:, :, 1] * (y_r.astype(dtp) @ lp['w_br_r'])
              + gates[:, :, 2] * (y_s.astype(dtp) @ lp['w_br_s']))
    out = merged @ lp['w_out']
    new_st = (jnp.stack([ms_f[0], ms_b[0]], 1), jnp.stack([ms_f[1], ms_b[1]], 1),
              jnp.stack([ms_f[2], ms_b[2]], 1), jnp.stack([rs_f, rs_b], 1),
              jnp.stack([ss_f, ss_b], 1))
    return out, new_st


def trunk_layer(x, mod, st0, rope, lp):
    sh_a, sc_a, g_a, sh_f, sc_f, g_f = jnp.split(mod[:, None, :], 6, axis=-1)
    h = rmsnorm(x, lp['norm_mix_g']) * (1 + sc_a) + sh_a
    y, st = token_mixers(h, lp, st0, rope)
    x = x + g_a * y
    h = rmsnorm(x, lp['norm_mlp_g']) * (1 + sc_f) + sh_f
    x = x + g_f * (jnp.square(jax.nn.relu(h @ lp['w_ff1'])) @ lp['w_ff2'])
    return x, st


def setup_inputs(seed: int = 0) -> dict:
    key = jax.random.key(seed)
    ks = iter(jax.random.split(key, 48))
    f = jnp.float32
    D = D_MODEL

    def nrm(shape, scale):
        return jax.random.normal(next(ks), shape, f) * scale

    def unif(shape, lo, hi):
        return jax.random.uniform(next(ks), shape, f, lo, hi)

    ret_base = jnp.log(-jnp.log1p(-(2.0 ** (-5.0 - jnp.arange(R_HEADS, dtype=f)))))
    dt0 = jnp.exp(unif((DEPTH, 2, S_HEADS), float(np.log(1e-3)), float(np.log(1e-1))))
    return {
        "x_prompt": nrm((BATCH, SEQ, D), 1.0),
        "x_sample": nrm((DEC_BATCH, DEC_SEQ, D), 1.0),
        "c": nrm((DEC_BATCH, D), 1.0),
        "state_mlstm_C": nrm((DEC_BATCH, DEPTH, 2, M_HEADS, M_DK, M_DV), 0.1),
        "state_mlstm_n": nrm((DEC_BATCH, DEPTH, 2, M_HEADS, M_DK), 0.1),
        "state_mlstm_m": nrm((DEC_BATCH, DEPTH, 2, M_HEADS), 0.5),
        "state_ret": nrm((DEC_BATCH, DEPTH, 2, R_HEADS, R_DK, R_DV), 0.5),
        "state_ssd": nrm((DEC_BATCH, DEPTH, 2, S_HEADS, S_P, S_N), 0.3),
        "c_ctx": nrm((D,), 1.0),
        "w_mod": nrm((DEPTH, D, 6 * D), 0.5 * D ** -0.5),
        "b_mod": nrm((DEPTH, 6 * D), 0.02),
        "norm_mix_g": 1.0 + nrm((DEPTH, D), 0.02),
        "norm_mlp_g": 1.0 + nrm((DEPTH, D), 0.02),
        "w_in": nrm((DEPTH, D, IN_COLS), D ** -0.5),
        "m_igate_b": nrm((DEPTH, 2, M_HEADS), 0.1),
        "m_fgate_b": jnp.linspace(3.0, 6.0, M_HEADS, dtype=f)[None, None] + nrm((DEPTH, 2, M_HEADS), 0.1),
        "m_norm_g": 1.0 + nrm((DEPTH, M_WIDTH), 0.02),
        "r_decay": ret_base[None, None] + nrm((DEPTH, 2, R_HEADS), 0.05),
        "r_norm_g": 1.0 + nrm((DEPTH, R_WIDTH), 0.02),
        "s_conv_w": nrm((DEPTH, CONV_W, CONV_CH), CONV_W ** -0.5),
        "s_conv_b": nrm((DEPTH, CONV_CH), 0.02),
        "s_dt_bias": dt0 + jnp.log(-jnp.expm1(-dt0)),
        "s_a_log": jnp.log(unif((DEPTH, 2, S_HEADS), 1.0, 16.0)),
        "s_d": 1.0 + nrm((DEPTH, S_HEADS), 0.1),
        "s_norm_g": 1.0 + nrm((DEPTH, S_WIDTH), 0.02),
        "w_br_m": nrm((DEPTH, M_WIDTH, D), M_WIDTH ** -0.5),
        "w_br_r": nrm((DEPTH, R_WIDTH, D), R_WIDTH ** -0.5),
        "w_br_s": nrm((DEPTH, S_WIDTH, D), S_WIDTH ** -0.5),
        "w_out": nrm((DEPTH, D, D), D ** -0.5),
        "w_ff1": nrm((DEPTH, D, D_FF), D ** -0.5),
        "w_ff2": nrm((DEPTH, D_FF, D), D_FF ** -0.5),
        "final_norm_g": 1.0 + nrm((D,), 0.02),
    }


def reference(x_prompt, x_sample, c, state_mlstm_C, state_mlstm_n, state_mlstm_m, state_ret, state_ssd,
              c_ctx, w_mod, b_mod, norm_mix_g, norm_mlp_g, w_in, m_igate_b, m_fgate_b, m_norm_g,
              r_decay, r_norm_g, s_conv_w, s_conv_b, s_dt_bias, s_a_log, s_d, s_norm_g,
              w_br_m, w_br_r, w_br_s, w_out, w_ff1, w_ff2, final_norm_g):
    Bp = x_prompt.shape[0]
    Ls = x_sample.shape[1]
    rope = rope_tables(Ls)
    f = jnp.float32
    zero_state = (jnp.zeros((Bp, 2, M_HEADS, M_DK, M_DV), f), jnp.zeros((Bp, 2, M_HEADS, M_DK), f),
                  jnp.zeros((Bp, 2, M_HEADS), f), jnp.zeros((Bp, 2, R_HEADS, R_DK, R_DV), f),
                  jnp.zeros((Bp, 2, S_HEADS, S_P, S_N), f))
    xp, xs = x_prompt, x_sample
    st_mC, st_mn, st_mm, st_r, st_s = [], [], [], [], []
    for l in range(DEPTH):
        lp = dict(w_in=w_in[l], m_igate_b=m_igate_b[l], m_fgate_b=m_fgate_b[l], m_norm_g=m_norm_g[l],
                  r_decay=r_decay[l], r_norm_g=r_norm_g[l], s_conv_w=s_conv_w[l], s_conv_b=s_conv_b[l],
                  s_dt_bias=s_dt_bias[l], s_a_log=s_a_log[l], s_d=s_d[l], s_norm_g=s_norm_g[l],
                  w_br_m=w_br_m[l], w_br_r=w_br_r[l], w_br_s=w_br_s[l], w_out=w_out[l],
                  w_ff1=w_ff1[l], w_ff2=w_ff2[l], norm_mix_g=norm_mix_g[l], norm_mlp_g=norm_mlp_g[l])
        mod_ctx = jax.nn.silu(c_ctx)[None, :] @ w_mod[l] + b_mod[l]
        mod_lat = jax.nn.silu(c) @ w_mod[l] + b_mod[l]
        xp, st = trunk_layer(xp, mod_ctx, zero_state, None, lp)
        st_mC.append(st[0]); st_mn.append(st[1]); st_mm.append(st[2]); st_r.append(st[3]); st_s.append(st[4])
        cache_l = (state_mlstm_C[:, l], state_mlstm_n[:, l], state_mlstm_m[:, l], state_ret[:, l], state_ssd[:, l])
        xs, _ = trunk_layer(xs, mod_lat, cache_l, rope, lp)
    y_prompt = rmsnorm(xp, final_norm_g)
    y_sample = rmsnorm(xs, final_norm_g)
    new_mlstm_C = jnp.stack(st_mC, 1)
    new_mlstm_n = jnp.stack(st_mn, 1)
    new_mlstm_m = jnp.stack(st_mm, 1)
    new_ret = jnp.stack(st_r, 1)
    new_ssd = jnp.stack(st_s, 1)
    return (y_prompt, y_sample, new_mlstm_C, new_mlstm_n, new_mlstm_m, new_ret, new_ssd)
```

```python
import os
from contextlib import ExitStack
import numpy as np
import concourse.bass as bass
import concourse.mybir as mybir
from concourse.bass_utils import run_bass_kernel_spmd

F32 = mybir.dt.float32
BF16 = mybir.dt.bfloat16
AF = mybir.ActivationFunctionType
ALU = mybir.AluOpType
AX = mybir.AxisListType

D = 2048
T = 2048
NCH = 16
KC = 16
DEPTH = 2
EPS = 1e-6
IN_SIZES = (1024, 1024, 1024, 1024, 8, 8, 1024, 1024, 1024, 1024, 1024, 1536, 32, 6144)
IN_OFF = [0]
for _s in IN_SIZES:
    IN_OFF.append(IN_OFF[-1] + _s)
IN_COLS = IN_OFF[-1]
(O_MQ, O_MK, O_MV, O_MO, O_MI, O_MF, O_RQ, O_RK, O_RV, O_RG, O_SZ, O_SX, O_SDT, O_GL) = IN_OFF[:14]
DFF = 8192


class Buf:
    __slots__ = ("name", "w", "r")

    def __init__(self, name=""):
        self.name = name
        self.w = None
        self.r = []


class KB:
    def __init__(self, nc, n_dma_slots=14):
        self.nc = nc
        self.engs = {"pe": nc.tensor, "act": nc.scalar, "dve": nc.vector,
                     "pool": nc.gpsimd, "sp": nc.sync}
        self.sems = {}
        self.cnt = {}
        self.seen = {e: {} for e in self.engs}
        self._cms = []
        for e in ("pe", "act", "dve", "pool"):
            self.sems[e] = self._sem("p_" + e)
            self.cnt[e] = 0
        self.slots = {}
        for q in ("sp", "pool"):
            self.slots[q] = []
            for i in range(n_dma_slots):
                k = f"d_{q}{i}"
                self.sems[k] = self._sem(k)
                self.slots[q].append([k, 0])
        self.slot_i = {q: 0 for q in self.slots}
        self.same_engine_sync = True
        self.n_instr = 0

    def _sem(self, name):
        cm = self.nc.semaphore(name)
        h = cm.__enter__()
        self._cms.append(cm)
        return h

    def _wait(self, e, key, val):
        if key == e and (e == "pe" or not self.same_engine_sync):
            return
        if self.seen[e].get(key, 0) >= val:
            return
        self.engs[e].wait_ge(self.sems[key], val)
        self.seen[e][key] = val

    def _sync(self, e, reads, writes):
        for b in reads:
            if b.w is not None:
                self._wait(e, *b.w)
        for b in writes:
            if b.w is not None:
                self._wait(e, *b.w)
            for r in b.r:
                self._wait(e, *r)

    def _record(self, tok, reads, writes):
        for b in reads:
            b.r = [x for x in b.r if x[0] != tok[0]] + [tok]
        for b in writes:
            b.w = tok
            b.r = []

    def op(self, e, fn, reads=(), writes=(), sig=True):
        self._sync(e, reads, writes)
        ins = fn(self.engs[e])
        self.n_instr += 1
        if sig:
            self.cnt[e] += 1
            ins.then_inc(self.sems[e], 1)
            self._record((e, self.cnt[e]), reads, writes)
        return ins

    def dma(self, q, out, in_, reads=(), writes=(), **kw):
        sl = self.slots[q]
        i = self.slot_i[q]
        self.slot_i[q] = (i + 1) % len(sl)
        sem, n = sl[i]
        if n > 0:
            self._wait(q, sem, 16 * n)
        self._sync(q, reads, writes)
        ins = self.engs[q].dma_start(out=out, in_=in_, **kw)
        ins.then_inc(self.sems[sem], 16)
        sl[i][1] = n + 1
        self.n_instr += 1
        self._record((sem, 16 * (n + 1)), reads, writes)
        return ins

    def barrier(self):
        for e in ("pe", "act", "dve", "pool", "sp"):
            for o in ("pe", "act", "dve", "pool"):
                if self.cnt[o] > 0:
                    self._wait(e, o, self.cnt[o])
            for q in self.slots:
                for sem, n in self.slots[q]:
                    if n > 0:
                        self._wait(e, sem, 16 * n)

    def finish(self):
        for q in self.slots:
            for sem, n in self.slots[q]:
                if n > 0:
                    self._wait("sp", sem, 16 * n)
        for e in ("pe", "act", "dve", "pool"):
            if self.cnt[e] > 0:
                self._wait("sp", e, self.cnt[e])


class Scope(ExitStack):
    def __init__(self, kb):
        super().__init__()
        self._kb = kb

    def __exit__(self, *a):
        if a[0] is None:
            self._kb.barrier()
        return super().__exit__(*a)


class Rot:
    _uid = [0]

    def __init__(self, nc, es, name, shape, dtype, n, psum=False):
        Rot._uid[0] += 1
        name = f"{name}_u{Rot._uid[0]}_"
        self.t = []
        for i in range(n):
            if psum:
                h = es.enter_context(nc.psum_tensor(f"{name}{i}", shape, dtype))
            else:
                h = es.enter_context(nc.sbuf_tensor(f"{name}{i}", shape, dtype))
            self.t.append((h, Buf(f"{name}{i}")))
        self.i = 0

    def get(self):
        r = self.t[self.i]
        self.i = (self.i + 1) % len(self.t)
        return r


def build_program(dbg=None):
    nc = bass.Bass("TRN2", target_bir_lowering=False)
    kb = KB(nc)

    def din(name, shape, dt=F32):
        return nc.dram_tensor(name, list(shape), dt, kind="ExternalInput").ap()

    def dout(name, shape, dt=F32):
        return nc.dram_tensor(name, list(shape), dt, kind="ExternalOutput").ap()

    def dscr(name, shape, dt=BF16):
        if dbg and name in dbg:
            return nc.dram_tensor(name, list(shape), dt, kind="ExternalOutput").ap()
        return nc.dram_tensor(name, list(shape), dt).ap()

    x_in = din("x", [T, D])
    cvec = din("cvec", [D])
    st_mC = din("st_mC", [DEPTH, 2, 4, 256, 256])
    st_mn = din("st_mn", [DEPTH, 2, 4, 256])
    st_mm = din("st_mm", [DEPTH, 2, 4])
    st_r = din("st_r", [DEPTH, 2, 8, 128, 128])
    st_s = din("st_s", [DEPTH, 2, 16, 64, 128])
    flags = din("flags", [128, 2])
    rope_c = din("rope_c", [T, 64])
    rope_s = din("rope_s", [T, 64])
    consts = din("consts", [128, 8, 128])
    posk = din("posk", [128, 16])
    posq = din("posq", [128, 16])
    w_mod = din("w_mod", [DEPTH, D, 6 * D])
    b_mod = din("b_mod", [DEPTH, 6 * D])
    norm_mix_g = din("norm_mix_g", [DEPTH, D])
    norm_mlp_g = din("norm_mlp_g", [DEPTH, D])
    w_in = din("w_in", [DEPTH, D, IN_COLS])
    m_igate_b = din("m_igate_b", [DEPTH, 8])
    m_fgate_b = din("m_fgate_b", [DEPTH, 8])
    m_norm_g = din("m_norm_g", [DEPTH, 1024])
    r_decay = din("r_decay", [DEPTH, 16])
    r_norm_g = din("r_norm_g", [DEPTH, 1024])
    s_conv_w = din("s_conv_w", [DEPTH, 4, 1536])
    s_conv_b = din("s_conv_b", [DEPTH, 1536])
    s_dt_bias = din("s_dt_bias", [DEPTH, 32])
    s_a_log = din("s_a_log", [DEPTH, 32])
    s_d = din("s_d", [DEPTH, 16])
    s_norm_g = din("s_norm_g", [DEPTH, 1024])
    w_br = [din("w_br_m", [DEPTH, 1024, D]), din("w_br_r", [DEPTH, 1024, D]), din("w_br_s", [DEPTH, 1024, D])]
    w_out = din("w_out", [DEPTH, D, D])
    w_ff1 = din("w_ff1", [DEPTH, D, DFF])
    w_ff2 = din("w_ff2", [DEPTH, DFF, D])
    final_g = din("final_norm_g", [D])

    y_out = dout("y", [T, D])
    o_mC = dout("o_mC", [8, DEPTH, 2, 4, 256, 256])
    o_mn = dout("o_mn", [8, DEPTH, 2, 4, 256])
    o_mm = dout("o_mm", [8, DEPTH, 2, 4])
    o_r = dout("o_r", [8, DEPTH, 2, 8, 128, 128])
    o_s = dout("o_s", [8, DEPTH, 2, 16, 64, 128])

    xres = dscr("xres", [T, D], F32)
    ffacc = dscr("ffacc", [T, D], F32)
    modrow = dscr("modrow", [DEPTH, 6, D], F32)
    sc_mq = dscr("sc_mq", [T, 1024]); sc_mk = dscr("sc_mk", [T, 1024]); sc_mv = dscr("sc_mv", [T, 1024])
    sc_mo = dscr("sc_mo", [T, 1024]); sc_mg = dscr("sc_mg", [T, 16], F32)
    sc_rq = dscr("sc_rq", [T, 1024]); sc_rk = dscr("sc_rk", [T, 1024]); sc_rv = dscr("sc_rv", [T, 1024])
    sc_rg = dscr("sc_rg", [T, 1024])
    sc_sz = dscr("sc_sz", [T, 1024]); sc_sdt = dscr("sc_sdt", [T, 32], F32)
    sc_sxT = dscr("sc_sxT", [8, 128, T])
    sc_sBT = dscr("sc_sBT", [2, 128, T]); sc_sCT = dscr("sc_sCT", [2, 128, T])
    sc_gT = dscr("sc_gT", [3, D, T])
    sc_yT = [dscr("sc_ymT", [1024, T]), dscr("sc_yrT", [1024, T]), dscr("sc_ysT", [1024, T])]
    sb_m = dscr("sb_m", [NCH, 4, 2, 128, 257])
    sb_r = dscr("sb_r", [NCH, 8, 128, 128])
    sb_s = dscr("sb_s", [NCH, 2, 128, 512])

    B_xres = [Buf(f"xres{c}") for c in range(NCH)]
    B_ffacc = [Buf(f"ffacc{c}") for c in range(NCH)]
    B_mod = Buf("modrow")
    B_sc = {}

    def bsc(name, c=0):
        k = (name, c)
        if k not in B_sc:
            B_sc[k] = Buf(f"{name}{c}")
        return B_sc[k]

    es0 = ExitStack()
    with es0, nc.allow_non_contiguous_dma(reason="small strided parameter loads"):
        def sb(name, shape, dt=F32, es=es0):
            Rot._uid[0] += 1
            return es.enter_context(nc.sbuf_tensor(f"{name}_v{Rot._uid[0]}", list(shape), dt))

        cst = sb("cst", [128, 8, 128]); B_cst = Buf("cst")
        cstb = sb("cstb", [128, 8, 128], BF16); B_cstb = Buf("cstb")
        flg = sb("flg", [128, 2]); B_flg = Buf("flg")
        kb.dma("sp", cst[:], consts, writes=[B_cst])
        kb.dma("pool", cstb[:], consts, writes=[B_cstb])
        kb.dma("sp", flg[:], flags, writes=[B_flg])
        ident_f = cst[:, 0, :]; maskF = cst[:, 1, :]; maskB = cst[:, 2, :]; ones_f = cst[:, 3, :]
        rel = cst[:, 4, :]; negF = cst[:, 5, :]; negB = cst[:, 6, :]
        ident_b = cstb[:, 0, :]; ones_b = cstb[:, 3, :]
        carry = flg[:, 0:1]

        psF = Rot(nc, es0, "psF", [128, 512], F32, 6, psum=True)
        psT = Rot(nc, es0, "psT", [128, 1024], BF16, 2, psum=True)

        def dump(name, ap, reads):
            if dbg and name in dbg:
                d_ = nc.dram_tensor(name, list(ap.shape), ap.dtype, kind="ExternalOutput").ap()
                kb.dma("sp", d_, ap, reads=reads, writes=[Buf()])

        def MM(out, lhsT, rhs, start, stop, r=(), w=(), sig=None):
            if sig is None:
                sig = stop
            return kb.op("pe", lambda e: e.matmul(out, lhsT, rhs, start=start, stop=stop), r, w, sig=sig)

        def TR(out, in_, r=(), w=(), sig=True):
            return kb.op("pe", lambda e: e.transpose(out, in_, ident_b), list(r) + [B_cstb], w, sig=sig)

        def TT(e, out, in0, in1, op, r=(), w=()):
            return kb.op(e, lambda g: g.tensor_tensor(out=out, in0=in0, in1=in1, op=op), r, w)

        def TS(e, out, in0, s1, s2, op0, op1=None, r=(), w=()):
            if op1 is None:
                return kb.op(e, lambda g: g.tensor_scalar(out=out, in0=in0, scalar1=s1, scalar2=None, op0=op0), r, w)
            return kb.op(e, lambda g: g.tensor_scalar(out=out, in0=in0, scalar1=s1, scalar2=s2, op0=op0, op1=op1), r, w)

        def STT(e, out, in0, scalar, in1, op0, op1, r=(), w=()):
            return kb.op(e, lambda g: g.scalar_tensor_tensor(out=out, in0=in0, scalar=scalar, in1=in1, op0=op0, op1=op1), r, w)

        def ACT(out, in_, func, r=(), w=(), bias=0.0, scale=1.0, accum=None):
            if accum is None:
                return kb.op("act", lambda g: g.activation(out=out, in_=in_, func=func, bias=bias, scale=scale), r, w)
            return kb.op("act", lambda g: g.activation(out=out, in_=in_, func=func, bias=bias, scale=scale, accum_out=accum), r, w)

        def CP(e, out, in_, r=(), w=()):
            if e == "act":
                return ACT(out, in_, AF.Copy, r, w)
            return kb.op(e, lambda g: g.tensor_copy(out=out, in_=in_), r, w)

        def RSTD(out, tmp, ssq, inv_n, b):
            TS("dve", tmp, ssq, inv_n, EPS, ALU.mult, ALU.add, [b], [b])
            ACT(tmp, tmp, AF.Sqrt, [b], [b])
            kb.op("dve", lambda g: g.reciprocal(out=out, in_=tmp), [b], [b])

        cp_rr = [0]

        def CPrr(out, in_, r=(), w=()):
            cp_rr[0] ^= 1
            return CP("act" if cp_rr[0] else "dve", out, in_, r, w)

        cv = sb("cv", [128, KC], F32); B_cv = Buf()
        cvb = sb("cvb", [128, KC], BF16); B_cvb = Buf()
        sg = sb("sg", [128, KC], F32); B_sg = Buf()
        kb.dma("sp", cv[:], cvec.rearrange("(k p) -> p k", p=128), writes=[B_cv])
        ACT(sg[:], cv[:], AF.Sigmoid, [B_cv], [B_sg])
        TT("dve", cvb[:], cv[:], sg[:], ALU.mult, [B_cv, B_sg], [B_cvb])

        def gen_mod(es, l, nbuf):
            wm = Rot(nc, es, "wm", [128, KC, 512], BF16, nbuf)
            brow = Rot(nc, es, "brow", [1, 512], F32, 2)
            grow = Rot(nc, es, "grow", [1, 512], F32, 2)
            mrow = Rot(nc, es, "mrow", [1, 512], F32, 2)
            wv = w_mod[l].rearrange("(k p) n -> p k n", p=128)
            rowmap = {0: 1, 1: 0, 2: 2, 3: 4, 4: 3, 5: 5}
            for cb in range(24):
                seg, off = cb // 4, (cb % 4) * 512
                wt, bw = wm.get()
                for k4 in range(4):
                    kb.dma("pool", wt[:, k4 * 4:(k4 + 1) * 4, :], wv[:, k4 * 4:(k4 + 1) * 4, cb * 512:(cb + 1) * 512], writes=[bw])
                br_, bbr = brow.get()
                kb.dma("sp", br_[:], b_mod[l:l + 1, cb * 512:(cb + 1) * 512], writes=[bbr])
                if seg in (1, 4):
                    gr_, bgr = grow.get()
                    gsrc = norm_mix_g if seg == 1 else norm_mlp_g
                    kb.dma("sp", gr_[:], gsrc[l:l + 1, off:off + 512], writes=[bgr])
                ps, bp = psF.get()
                for k in range(KC):
                    MM(ps[0:1, :], cvb[:, k:k + 1], wt[:, k, :], k == 0, k == KC - 1, [B_cvb, bw], [bp])
                mr, bmr = mrow.get()
                TT("dve", mr[:], ps[0:1, :], br_[:], ALU.add, [bp, bbr], [bmr])
                if seg in (1, 4):
                    STT("dve", mr[:], mr[:], 1.0, gr_[:], ALU.add, ALU.mult, [bmr, bgr], [bmr])
                kb.dma("sp", modrow[l, rowmap[seg]:rowmap[seg] + 1, off:off + 512], mr[:], reads=[bmr], writes=[B_mod])
                yield

        with Scope(kb) as es:
            for _ in gen_mod(es, 0, 3):
                pass

        def load_bcast(es, name, src_row_ap, n, dt=F32, q="sp"):
            t = sb(name, [128, n], dt, es)
            b = Buf(name)
            kb.dma(q, t[:], src_row_ap.partition_broadcast(128), reads=[B_mod], writes=[b])
            return t, b

        def norm_to_hT(xt, bx, gm, bgm, sh, bsh, hT, bhT_c, c, wk):
            junk, bj = wk["junk"].get()
            ss, bss = wk["ss"].get()
            ACT(junk[:], xt[:], AF.Square, [bx], [bj, bss], accum=ss[:, 0:1])
            RSTD(ss[:, 2:3], ss[:, 1:2], ss[:, 0:1], 1.0 / D, bss)
            tmp, bt = wk["tmp"].get()
            STT("dve", tmp[:], xt[:], ss[:, 2:3], gm[:], ALU.mult, ALU.mult, [bx, bss, bgm], [bt])
            hb, bh = wk["hb"].get()
            TT("dve", hb[:], tmp[:], sh[:], ALU.add, [bt, bsh], [bh])
            for half in range(2):
                pt, bp = psT.get()
                for k in range(8):
                    kk = half * 8 + k
                    TR(pt[:, k * 128:(k + 1) * 128], hb[:, kk * 128:(kk + 1) * 128], [bh], [bp], sig=(k == 7))
                CPrr(hT[:, half * 8:(half + 1) * 8, c * 128:(c + 1) * 128],
                     pt[:].rearrange("p (k t) -> p k t", k=8), [bp], [bhT_c])

        def mk_norm_wk(es):
            return {"junk": Rot(nc, es, "nj", [128, D], BF16, 1),
                    "ss": Rot(nc, es, "nss", [128, 4], F32, 3),
                    "tmp": Rot(nc, es, "ntmp", [128, D], F32, 2),
                    "hb": Rot(nc, es, "nhb", [128, D], BF16, 2)}

        stop = dbg.get("stop") if dbg else None
        for l in range(DEPTH if stop != "mod" else 0):
            with Scope(kb) as es:
                hT = sb("hT", [128, KC, T], BF16, es)
                B_hT = [Buf(f"hT{c}") for c in range(NCH)]
                with Scope(kb) as es1:
                    gm, bgm = load_bcast(es1, "gm", modrow[l, 0:1, :], D)
                    sh, bsh = load_bcast(es1, "sh", modrow[l, 1:2, :], D)
                    wk = mk_norm_wk(es1)
                    xr = Rot(nc, es1, "xr", [128, D], F32, 2)
                    for c in range(NCH):
                        xt, bx = xr.get()
                        if l == 0:
                            kb.dma("sp", xt[:], x_in[c * 128:(c + 1) * 128, :], writes=[bx])
                        else:
                            kb.dma("sp", xt[:], xres[c * 128:(c + 1) * 128, :], reads=[B_xres[c]], writes=[bx])
                        norm_to_hT(xt, bx, gm, bgm, sh, bsh, hT, B_hT[c], c, wk)

                if dbg and "dbg_hT" in dbg and l == 0:
                    dh = nc.dram_tensor("dbg_hT", [128, KC, T], BF16, kind="ExternalOutput").ap()
                    kb.dma("sp", dh, hT[:], reads=B_hT, writes=[Buf()])
                with Scope(kb) as es1:
                    if stop == "norm":
                        break
                    wt_rot = Rot(nc, es1, "wi", [128, KC, 512], BF16, 3)
                    ob = Rot(nc, es1, "ob", [128, 512], BF16, 4)
                    of = Rot(nc, es1, "of", [128, 512], F32, 3)
                    rc = sb("rc", [128, NCH, 64], F32, es1); B_rc = Buf()
                    rs_ = sb("rs", [128, NCH, 64], F32, es1); B_rs = Buf()
                    kb.dma("sp", rc[:], rope_c.rearrange("(c p) f -> p c f", p=128), writes=[B_rc])
                    kb.dma("sp", rs_[:], rope_s.rearrange("(c p) f -> p c f", p=128), writes=[B_rs])
                    wv = w_in[l].rearrange("(k p) n -> p k n", p=128)

                    def load_w(col0, ncols):
                        wt, bw = wt_rot.get()
                        for k4 in range(4):
                            kb.dma("pool", wt[:, k4 * 4:(k4 + 1) * 4, 0:ncols], wv[:, k4 * 4:(k4 + 1) * 4, col0:col0 + ncols], writes=[bw])
                        return wt, bw

                    def gemm_tok(col0, ncols, epi):
                        wt, bw = load_w(col0, ncols)
                        for c in range(NCH):
                            ps, bp = psF.get()
                            for k in range(KC):
                                MM(ps[:, 0:ncols], hT[:, k, c * 128:(c + 1) * 128], wt[:, k, 0:ncols],
                                   k == 0, k == KC - 1, [B_hT[c], bw], [bp])
                            epi(c, ps, bp)

                    def epi_store(dst, dcol0, func=None, ncols=512):
                        def f(c, ps, bp):
                            o, bo = ob.get()
                            if func is None:
                                CPrr(o[:, 0:ncols], ps[:, 0:ncols], [bp], [bo])
                            else:
                                ACT(o[:, 0:ncols], ps[:, 0:ncols], func, [bp], [bo])
                            kb.dma("sp", dst[c * 128:(c + 1) * 128, dcol0:dcol0 + ncols], o[:, 0:ncols],
                                   reads=[bo], writes=[bsc(dst.name, c)])
                        return f

                    def epi_store_f32(dst, ncols):
                        def f(c, ps, bp):
                            o, bo = of.get()
                            CP("dve", o[:, 0:ncols], ps[:, 0:ncols], [bp], [bo])
                            kb.dma("sp", dst[c * 128:(c + 1) * 128, 0:ncols], o[:, 0:ncols],
                                   reads=[bo], writes=[bsc(dst.name, c)])
                        return f

                    def epi_rope(dst, dcol0):
                        def f(c, ps, bp):
                            p3 = ps[:].rearrange("p (h d) -> p h d", h=4)
                            x1 = p3[:, :, 0:64]; x2 = p3[:, :, 64:128]
                            cc = rc[:, c:c + 1, :].to_broadcast([128, 4, 64])
                            sn = rs_[:, c:c + 1, :].to_broadcast([128, 4, 64])
                            a, ba = of.get(); b_, bb = of.get()
                            a3 = a[:].rearrange("p (h d) -> p h d", h=4)
                            b3 = b_[:].rearrange("p (h d) -> p h d", h=4)
                            o, bo = ob.get()
                            o3 = o[:].rearrange("p (h d) -> p h d", h=4)
                            TT("dve", a3[:, :, 0:64], x1, cc, ALU.mult, [bp, B_rc], [ba])
                            TT("dve", a3[:, :, 64:128], x2, cc, ALU.mult, [bp, B_rc], [ba])
                            TT("dve", b3[:, :, 0:64], x2, sn, ALU.mult, [bp, B_rs], [bb])
                            TT("dve", b3[:, :, 64:128], x1, sn, ALU.mult, [bp, B_rs], [bb])
                            TT("dve", o3[:, :, 0:64], a3[:, :, 0:64], b3[:, :, 0:64], ALU.subtract, [ba, bb], [bo])
                            TT("dve", o3[:, :, 64:128], a3[:, :, 64:128], b3[:, :, 64:128], ALU.add, [ba, bb], [bo])
                            kb.dma("sp", dst[c * 128:(c + 1) * 128, dcol0:dcol0 + 512], o[:],
                                   reads=[bo], writes=[bsc(dst.name, c)])
                        return f

                    for half in range(2):
                        gemm_tok(O_MQ + half * 512, 512, epi_store(sc_mq, half * 512))
                        gemm_tok(O_MK + half * 512, 512, epi_store(sc_mk, half * 512))
                        gemm_tok(O_MV + half * 512, 512, epi_store(sc_mv, half * 512))
                        gemm_tok(O_MO + half * 512, 512, epi_store(sc_mo, half * 512, AF.Sigmoid))
                    gemm_tok(O_MI, 16, epi_store_f32(sc_mg, 16))
                    for half in range(2):
                        gemm_tok(O_RQ + half * 512, 512, epi_rope(sc_rq, half * 512))
                        gemm_tok(O_RK + half * 512, 512, epi_rope(sc_rk, half * 512))
                        gemm_tok(O_RV + half * 512, 512, epi_store(sc_rv, half * 512))
                        gemm_tok(O_RG + half * 512, 512, epi_store(sc_rg, half * 512, AF.Silu))
                        gemm_tok(O_SZ + half * 512, 512, epi_store(sc_sz, half * 512, AF.Silu))
                    gemm_tok(O_SDT, 32, epi_store_f32(sc_sdt, 32))
                    if stop == "tok":
                        break

                    def gemm_feat(col0, epi):
                        wt, bw = load_w(col0, 512)
                        for j in range(4):
                            for tt in range(4):
                                ps, bp = psF.get()
                                for k in range(KC):
                                    MM(ps[:], wt[:, k, j * 128:(j + 1) * 128], hT[:, k, tt * 512:(tt + 1) * 512],
                                       k == 0, k == KC - 1, B_hT[tt * 4:(tt + 1) * 4] + [bw], [bp])
                                epi(j, tt, ps, bp)

                    for blk in range(dbg.get('gl_blks', 12) if dbg else 12):
                        def epi_g(j, tt, ps, bp, blk=blk):
                            o, bo = ob.get()
                            ACT(o[:], ps[:], AF.Sigmoid, [bp], [bo])
                            row0 = blk * 512 + j * 128
                            br_, fr = divmod(row0, D)
                            kb.dma("sp", sc_gT[br_, fr:fr + 128, tt * 512:(tt + 1) * 512], o[:],
                                   reads=[bo], writes=[bsc("gT", 0)])
                        gemm_feat(O_GL + blk * 512, epi_g)

                    if stop == "gl":
                        break
                    pf_c = flg[:, 1:2]
                    cw = sb("cw", [128, 4, 12], F32, es1); B_cw = Buf()
                    cbias = sb("cbias", [128, 12], F32, es1); B_cb = Buf()
                    with nc.allow_non_contiguous_dma(reason="tiny conv weights"):
                        for k_ in range(4):
                            kb.dma("sp", cw[:, k_, :], s_conv_w[l, k_].rearrange("(b p) -> p b", p=128), writes=[B_cw])
                        kb.dma("sp", cbias[:], s_conv_b[l].rearrange("(b p) -> p b", p=128), writes=[B_cb])
                    cwp = sb("cwp", [128, 4, 12], F32, es1)
                    TS("dve", cwp[:], cw[:], pf_c, None, ALU.mult, None, [B_cw, B_flg], [B_cw])
                    conv_lvl = dbg.get("conv_lvl", 6) if dbg else 6
                    xrow = Rot(nc, es1, "xrow", [128, T], F32, 2)
                    acc = Rot(nc, es1, "cacc", [128, T], F32, 2)
                    cvo = Rot(nc, es1, "cvo", [128, T], BF16, 2)
                    tok = Rot(nc, es1, "ctok", [128, 128], BF16, 4)
                    pf_ = flg[:, 1:2]
                    for blk in range(3):
                        rows = {}

                        def epi_x(j, tt, ps, bp, rows=rows):
                            if tt == 0:
                                rows[j] = xrow.get()
                            xr_, bxr = rows[j]
                            CPrr(xr_[:, tt * 512:(tt + 1) * 512], ps[:], [bp], [bxr])
                            if tt < 3:
                                return
                            cbk = blk * 4 + j
                            if conv_lvl < 1:
                                return
                            a, ba = acc.get()
                            TS("dve", a[:], xr_[:], cw[:, 1, cbk:cbk + 1], None, ALU.mult, None, [bxr, B_cw], [ba])
                            STT("dve", a[:, 1:T], xr_[:, 0:T - 1], cw[:, 0, cbk:cbk + 1], a[:, 1:T], ALU.mult, ALU.add, [bxr, B_cw, ba], [ba])
                            STT("dve", a[:, 0:T - 1], xr_[:, 1:T], cw[:, 2, cbk:cbk + 1], a[:, 0:T - 1], ALU.mult, ALU.add, [bxr, B_cw, ba], [ba])
                            STT("dve", a[:, 0:T - 2], xr_[:, 2:T], cw[:, 3, cbk:cbk + 1], a[:, 0:T - 2], ALU.mult, ALU.add, [bxr, B_cw, ba], [ba])
                            t3, b3 = wk_small.get()
                            a_v = a[:].rearrange("p (s t) -> p s t", t=256)
                            x_v = xr_[:].rearrange("p (s t) -> p s t", t=256)
                            fixes = ((a_v[:, 1:8, 0], x_v[:, 0:7, 255], 0),
                                     (a_v[:, 0:7, 255], x_v[:, 1:8, 0], 2),
                                     (a_v[:, 0:7, 255], x_v[:, 1:8, 1], 3),
                                     (a_v[:, 0:7, 254], x_v[:, 1:8, 0], 3))
                            for (dst_, src_, kk) in (fixes if conv_lvl >= 2 else ()):
                                TS("dve", t3[:, 0:7], src_, cwp[:, kk, cbk:cbk + 1], None, ALU.mult, None, [bxr, B_cw], [b3])
                                TT("dve", dst_, dst_, t3[:, 0:7], ALU.subtract, [ba, b3], [ba])
                            if conv_lvl < 3:
                                return
                            o, bo = cvo.get()
                            ACT(o[:], a[:], AF.Silu, [ba, B_cb], [bo], bias=cbias[:, cbk:cbk + 1])
                            if conv_lvl < 4:
                                return
                            if cbk < 8:
                                kb.dma("sp", sc_sxT[cbk], o[:], reads=[bo], writes=[bsc("sxT", 0)])
                            if cbk in (8, 9):
                                kb.dma("sp", sc_sBT[cbk - 8], o[:], reads=[bo], writes=[bsc("sBT", 0)])
                            if cbk in (10, 11):
                                kb.dma("sp", sc_sCT[cbk - 10], o[:], reads=[bo], writes=[bsc("sCT", 0)])

                        wk_small = Rot(nc, es1, f"wks{blk}", [128, 8], F32, 2)
                        gemm_feat(O_SX + blk * 512, epi_x)

            def yT_store(es_, y, by, which, c, nblk, wkp):
                dstv = sc_yT[which].rearrange("(k p) t -> p k t", p=128)
                pt, bpt = psT.get()
                for k in range(nblk):
                    TR(pt[:, k * 128:(k + 1) * 128], y[:, k * 128:(k + 1) * 128], [by], [bpt], sig=(k == nblk - 1))
                yt, byt = wkp.get()
                CPrr(yt[:, 0:nblk, :], pt[:, 0:nblk * 128].rearrange("p (k t) -> p k t", k=nblk), [bpt], [byt])
                kb.dma("sp", dstv[:, 0:nblk, c * 128:(c + 1) * 128], yt[:, 0:nblk, :], reads=[byt],
                       writes=[bsc(f"yT{which}", 0)])

            def gen_mlstm(es):
                G = sb("m_G", [128, NCH, 16], F32, es); B_G = Buf()
                bias = sb("m_bias", [128, 16], F32, es); B_bias = Buf()
                for c_ in range(NCH):
                    kb.dma("sp", G[:, c_, :], sc_mg[c_ * 128:(c_ + 1) * 128, :], reads=[bsc("sc_mg", c_)], writes=[B_G])
                kb.dma("sp", bias[:, 0:8], m_igate_b[l:l + 1, :].partition_broadcast(128), writes=[B_bias])
                kb.dma("sp", bias[:, 8:16], m_fgate_b[l:l + 1, :].partition_broadcast(128), writes=[B_bias])
                TT("dve", G[:], G[:], bias[:].unsqueeze(1).to_broadcast([128, NCH, 16]), ALU.add, [B_G, B_bias], [B_G])
                LF = sb("m_LF", [128, NCH, 8], F32, es); B_LF = Buf()
                ACT(LF[:], G[:, :, 8:16], AF.Exp, [B_G], [B_LF], scale=-1.0)
                ACT(LF[:], LF[:], AF.Ln, [B_LF], [B_LF], bias=1.0)
                TS("dve", LF[:], LF[:], -1.0, None, ALU.mult, None, [B_LF], [B_LF])
                BH = sb("m_BH", [128, NCH, 16], F32, es); B_BH = Buf()
                ps, bp = psF.get()
                for c in range(NCH):
                    MM(ps[:, c * 16:c * 16 + 4], maskF, LF[:, c, 0:4], True, True, [B_cst, B_LF], [bp], sig=False)
                    MM(ps[:, c * 16 + 4:c * 16 + 8], maskB, LF[:, c, 4:8], True, True, [B_cst, B_LF], [bp], sig=False)
                    MM(ps[:, c * 16 + 8:c * 16 + 16], ones_f, LF[:, c, :], True, True, [B_cst, B_LF], [bp], sig=(c == NCH - 1))
                CP("dve", BH[:], ps[:, 0:256].rearrange("p (c g) -> p c g", c=NCH), [bp], [B_BH])
                A_ = sb("m_A", [128, NCH, 8], F32, es); B_A = Buf()
                TT("dve", A_[:], G[:, :, 0:8], BH[:, :, 0:8], ALU.subtract, [B_G, B_BH], [B_A])
                AM = sb("m_AM", [8, NCH], F32, es); B_AM = Buf()
                for c4 in range(4):
                    ps, bp = psF.get()
                    for cc in range(4):
                        c = c4 * 4 + cc
                        MM(ps[0:8, cc * 128:(cc + 1) * 128], A_[:, c, :], ident_f, True, True, [B_A, B_cst], [bp], sig=(cc == 3))
                    kb.op("dve", lambda g, ps=ps, c4=c4: g.tensor_reduce(out=AM[:, c4 * 4:(c4 + 1) * 4],
                          in_=ps[0:8, :].rearrange("p (c t) -> p c t", c=4), axis=AX.X, op=ALU.max), [bp], [B_AM])
                R8 = sb("m_R8", [8, NCH, 8], F32, es); B_R8 = Buf()
                TT("dve", R8[:], AM[:].unsqueeze(2).to_broadcast([8, NCH, 8]),
                   cst[0:8, 0, 0:8].unsqueeze(1).to_broadcast([8, NCH, 8]), ALU.mult, [B_AM, B_cst], [B_R8])
                ps, bp = psF.get()
                MM(ps[:, 0:128], cst[0:8, 3, :], R8[:].rearrange("p c g -> p (c g)"), True, True, [B_cst, B_R8], [bp])
                AR = sb("m_AR", [128, NCH, 8], F32, es); B_AR = Buf()
                CP("dve", AR[:], ps[:, 0:128].rearrange("p (c g) -> p c g", c=NCH), [bp], [B_AR])
                MU = sb("m_MU", [128, NCH, 8], F32, es); DEC = sb("m_DEC", [128, NCH, 8], F32, es); B_MU = Buf()
                mcur = sb("m_mcur", [128, 8], F32, es); B_mc = Buf()
                kb.dma("sp", mcur[:], st_mm[l:l + 1].rearrange("o d h -> o (d h)").partition_broadcast(128), writes=[B_mc])
                for d_ in range(2):
                    sl = slice(d_ * 4, d_ * 4 + 4)
                    order = range(NCH) if d_ == 0 else range(NCH - 1, -1, -1)
                    for n_, c in enumerate(order):
                        if n_ > 0 and n_ % 2 == 0:
                            TS("dve", mcur[:, sl], mcur[:, sl], carry, None, ALU.mult, None, [B_mc, B_flg], [B_mc])
                        TT("dve", MU[:, c, sl], AR[:, c, sl], mcur[:, sl], ALU.max, [B_AR, B_mc], [B_MU])
                        TT("dve", DEC[:, c, sl], mcur[:, sl], MU[:, c, sl], ALU.subtract, [B_mc, B_MU], [B_MU])
                        TT("dve", mcur[:, sl], BH[:, c, 8 + d_ * 4:12 + d_ * 4], MU[:, c, sl], ALU.add, [B_BH, B_MU], [B_mc])
                        if n_ % 2 == 1:
                            seg = c // 2
                            kb.dma("sp", o_mm[seg, l, d_:d_ + 1, :], mcur[0:1, sl], reads=[B_mc], writes=[Buf()])
                ACT(DEC[:], DEC[:], AF.Exp, [B_MU], [B_MU])
                OM = sb("m_OM", [128, NCH, 8], F32, es); FL = sb("m_FL", [128, NCH, 8], F32, es)
                TT("dve", OM[:], A_[:], MU[:], ALU.subtract, [B_A, B_MU], [B_MU])
                ACT(OM[:], OM[:], AF.Exp, [B_MU], [B_MU])
                TS("dve", OM[:], OM[:], 1.0 / 16.0, None, ALU.mult, None, [B_MU], [B_MU])
                TT("dve", FL[:], BH[:, :, 0:8], MU[:], ALU.add, [B_BH, B_MU], [B_MU])
                ACT(FL[:], FL[:], AF.Exp, [B_MU], [B_MU], scale=-1.0)
                B_gs = B_MU
                if l == 0:
                    dump("d_BH", BH[:], [B_BH]); dump("d_MU", MU[:], [B_MU]); dump("d_DEC", DEC[:], [B_MU])
                    dump("d_OM", OM[:], [B_MU]); dump("d_FL", FL[:], [B_MU]); dump("d_AR", AR[:], [B_AR]); dump("d_A", A_[:], [B_A])
                if stop == "m_gates":
                    return
                yield

                gb, bgb = load_bcast(es, "m_g", m_norm_g[l:l + 1, :], 1024)
                S = [sb(f"m_S{d_}", [128, 4, 2, 257], F32, es) for d_ in range(2)]
                B_S = [[Buf() for _ in range(4)] for _ in range(2)]
                S16 = Rot(nc, es, "m_S16", [128, 4, 2, 257], BF16, 3)
                ldq = Rot(nc, es, "m_ldq", [128, 4, 256], BF16, 2)
                ldk = Rot(nc, es, "m_ldk", [128, 4, 256], BF16, 2)
                ldo = Rot(nc, es, "m_ldo", [128, 4, 256], BF16, 2)
                ldv = Rot(nc, es, "m_ldv", [128, 4, 257], BF16, 2)
                for (t_, b_) in ldv.t:
                    kb.op("pool", lambda g, t_=t_: g.memset(t_[:, :, 256:257], 1.0), [], [b_])
                kwr = Rot(nc, es, "m_kw", [128, 4, 256], BF16, 2)
                qkT = Rot(nc, es, "m_qkT", [128, 8, 128], BF16, 4)
                sTr = Rot(nc, es, "m_sT", [128, 4, 128], BF16, 4)
                hfr = Rot(nc, es, "m_hf", [128, 256], F32, 3)
                smr = Rot(nc, es, "m_sm", [128, 8], F32, 4)
                jkr = Rot(nc, es, "m_jk", [128, 256], BF16, 2)
                yb = Rot(nc, es, "m_y", [128, 1024], BF16, 2)
                ytp = Rot(nc, es, "m_yt", [128, 8, 128], BF16, 2)

                def m_load(rot, src, c, w=256):
                    t_, b_ = rot.get()
                    kb.dma("sp", t_[:, :, 0:256], src[c * 128:(c + 1) * 128, :].rearrange("p (h d) -> p h d", h=4),
                           reads=[bsc(src.name, c)], writes=[b_])
                    return t_, b_

                def m_state_init(d_, first):
                    for h in range(4):
                        if first:
                            kb.dma("sp", S[d_][:, h, :, 0:256], st_mC[l, d_, h].rearrange("(dc p) e -> p dc e", p=128), writes=[B_S[d_][h]])
                            kb.dma("sp", S[d_][:, h, :, 256], st_mn[l, d_, h].rearrange("(dc p) -> p dc", p=128), writes=[B_S[d_][h]])
                        else:
                            TS("dve", S[d_][:, h], S[d_][:, h], carry, None, ALU.mult, None, [B_S[d_][h], B_flg], [B_S[d_][h]])

                def m_dec16(d_, c):
                    s16, bs16 = S16.get()
                    for h in range(4):
                        ACT(s16[:, h], S[d_][:, h], AF.Copy, [B_S[d_][h], B_gs], [bs16], scale=DEC[:, c, d_ * 4 + h:d_ * 4 + h + 1])
                    return s16, bs16

                def m_state_update(d_, c, kt, bk, vt, bv):
                    kw, bkw = kwr.get()
                    TT("pool", kw[:], kt[:], OM[:, c, d_ * 4:d_ * 4 + 4].unsqueeze(2).to_broadcast([128, 4, 256]), ALU.mult, [bk, B_gs], [bkw])
                    for h in range(4):
                        for dc in range(2):
                            ps, bp = psF.get()
                            MM(ps[:, 0:257], kw[:, h, dc * 128:(dc + 1) * 128], vt[:, h, :], True, True, [bkw, bv], [bp])
                            STT("dve", S[d_][:, h, dc, :], S[d_][:, h, dc, :], DEC[:, c, d_ * 4 + h:d_ * 4 + h + 1], ps[:, 0:257],
                                ALU.mult, ALU.add, [B_S[d_][h], B_gs, bp], [B_S[d_][h]])

                def m_state_out(d_, seg):
                    for h in range(4):
                        kb.dma("sp", o_mC[seg, l, d_, h].rearrange("(dc p) e -> p dc e", p=128), S[d_][:, h, :, 0:256],
                               reads=[B_S[d_][h]], writes=[Buf()])
                        kb.dma("sp", o_mn[seg, l, d_, h].rearrange("(dc p) -> p dc", p=128), S[d_][:, h, :, 256],
                               reads=[B_S[d_][h]], writes=[Buf()])

                for c in range(NCH - 1, -1, -1):
                    kt, bk = m_load(ldk, sc_mk, c)
                    vt, bv = m_load(ldv, sc_mv, c)
                    if c == NCH - 1:
                        m_state_init(1, True)
                    elif c % 2 == 1:
                        m_state_init(1, False)
                    s16, bs16 = m_dec16(1, c)
                    kb.dma("sp", sb_m[c].rearrange("h dc p e -> p h dc e"), s16[:], reads=[bs16], writes=[bsc("sb_m", c)])
                    m_state_update(1, c, kt, bk, vt, bv)
                    if c % 2 == 0:
                        m_state_out(1, c // 2)
                    yield
                for c in range(NCH):
                    qt, bq = m_load(ldq, sc_mq, c)
                    kt, bk = m_load(ldk, sc_mk, c)
                    vt, bv = m_load(ldv, sc_mv, c)
                    ot, bo_ = m_load(ldo, sc_mo, c)
                    sbk, bsbk = S16.get()
                    kb.dma("sp", sbk[:], sb_m[c].rearrange("h dc p e -> p h dc e"), reads=[bsc("sb_m", c)], writes=[bsbk])
                    if c == 0:
                        m_state_init(0, True)
                    elif c % 2 == 0:
                        m_state_init(0, False)
                    sfk, bsfk = m_dec16(0, c)
                    qT, bqT = qkT.get(); kT, bkT = qkT.get()
                    for (src, bsrc, dstT, bdst) in ((qt, bq, qT, bqT), (kt, bk, kT, bkT)):
                        pt, bpt = psT.get()
                        for k in range(8):
                            TR(pt[:, k * 128:(k + 1) * 128], src[:, k // 2, (k % 2) * 128:(k % 2 + 1) * 128], [bsrc], [bpt], sig=(k == 7))
                        CPrr(dstT[:], pt[:].rearrange("p (k t) -> p k t", k=8), [bpt], [bdst])
                    psS, bpS = psF.get()
                    for h in range(4):
                        for dc in range(2):
                            MM(psS[:, h * 128:(h + 1) * 128], kT[:, h * 2 + dc, :], qT[:, h * 2 + dc, :], dc == 0, dc == 1,
                               [bkT, bqT], [bpS], sig=(h == 3 and dc == 1))
                    sTf, bsTf = sTr.get(); sTb, bsTb = sTr.get()
                    for h in range(4):
                        STT("dve", sTf[:, h, :], psS[:, h * 128:(h + 1) * 128], OM[:, c, h:h + 1], maskF, ALU.mult, ALU.mult,
                            [bpS, B_gs, B_cst], [bsTf])
                        STT("dve", sTb[:, h, :], psS[:, h * 128:(h + 1) * 128], OM[:, c, 4 + h:5 + h], maskB, ALU.mult, ALU.mult,
                            [bpS, B_gs, B_cst], [bsTb])
                    y, by = yb.get()
                    for h in range(4):
                        hf, bhf = hfr.get()
                        for d_, (sT_, bsT_, s16_, bs16_) in enumerate(((sTf, bsTf, sfk, bsfk), (sTb, bsTb, sbk, bsbk))):
                            ps, bp = psF.get()
                            MM(ps[:, 0:257], sT_[:, h, :], vt[:, h, :], True, False, [bsT_, bv], [bp], sig=False)
                            for dc in range(2):
                                MM(ps[:, 0:257], qT[:, h * 2 + dc, :], s16_[:, h, dc, :], False, dc == 1, [bqT, bs16_], [bp])
                            sm, bsm = smr.get()
                            g_ = d_ * 4 + h
                            CP("dve", sm[:, 3:4], ps[:, 256:257], [bp], [bsm])
                            STT("dve", sm[:, 0:1], sm[:, 3:4], -1.0, sm[:, 3:4], ALU.mult, ALU.max, [bsm], [bsm])
                            TT("dve", sm[:, 0:1], sm[:, 0:1], FL[:, c, g_:g_ + 1], ALU.max, [bsm, B_gs], [bsm])
                            kb.op("dve", lambda g, sm=sm: g.reciprocal(out=sm[:, 1:2], in_=sm[:, 0:1]), [bsm], [bsm])
                            if d_ == 0:
                                ACT(hf[:], ps[:, 0:256], AF.Copy, [bp, bsm], [bhf], scale=sm[:, 1:2])
                            else:
                                STT("dve", hf[:], ps[:, 0:256], sm[:, 1:2], hf[:], ALU.mult, ALU.add, [bp, bsm, bhf], [bhf])
                        sm, bsm = smr.get()
                        jk, bjk = jkr.get()
                        ACT(jk[:], hf[:], AF.Square, [bhf], [bjk, bsm], accum=sm[:, 0:1])
                        RSTD(sm[:, 2:3], sm[:, 1:2], sm[:, 0:1], 1.0 / 256, bsm)
                        STT("dve", hf[:], hf[:], sm[:, 2:3], gb[:, h * 256:(h + 1) * 256], ALU.mult, ALU.mult, [bhf, bsm, bgb], [bhf])
                        TT("pool", y[:, h * 256:(h + 1) * 256], hf[:], ot[:, h, :], ALU.mult, [bhf, bo_], [by])
                    yT_store(es, y, by, 0, c, 8, ytp)
                    m_state_update(0, c, kt, bk, vt, bv)
                    if c % 2 == 1:
                        m_state_out(0, c // 2)
                    yield

            def gen_ret(es):
                RS = 128.0 ** -0.5
                lg = sb("r_lg", [128, 16], F32, es); B_lg = Buf()
                nlg = sb("r_nlg", [128, 16], F32, es)
                wkq = sb("r_wkq", [128, 3, 16], F32, es)
                pk = sb("r_pk", [128, 2, 16], F32, es); B_pk = Buf()
                kb.dma("sp", lg[:], r_decay[l:l + 1, :].partition_broadcast(128), writes=[B_lg])
                kb.dma("sp", pk[:, 0, :], posk, writes=[B_pk])
                kb.dma("sp", pk[:, 1, :], posq, writes=[B_pk])
                ACT(lg[:], lg[:], AF.Exp, [B_lg], [B_lg])
                TS("dve", nlg[:], lg[:], 1.0, None, ALU.mult, None, [B_lg], [B_lg])
                TS("dve", lg[:], lg[:], -1.0, None, ALU.mult, None, [B_lg], [B_lg])
                TT("dve", wkq[:, 0, :], pk[:, 0, :], lg[:], ALU.mult, [B_pk, B_lg], [B_lg])
                TT("dve", wkq[:, 1, :], pk[:, 1, :], lg[:], ALU.mult, [B_pk, B_lg], [B_lg])
                TS("dve", wkq[:, 2, :], lg[:], 128.0, None, ALU.mult, None, [B_lg], [B_lg])
                ACT(wkq[:], wkq[:], AF.Exp, [B_lg], [B_lg])
                TS("dve", wkq[:, 0, :], wkq[:, 0, :], RS, None, ALU.mult, None, [B_lg], [B_lg])
                wk_ = wkq[:, 0, :]; wq_ = wkq[:, 1, :]; cd_ = wkq[:, 2, :]
                DTm = sb("r_DT", [128, 8, 128], F32, es); B_DT = Buf()
                dtmp = sb("r_dtmp", [128, 128], F32, es); B_dtmp = Buf()
                for h in range(8):
                    ACT(DTm[:, h, :], rel, AF.Exp, [B_cst, B_lg], [B_DT], scale=lg[:, h:h + 1])
                    TT("dve", DTm[:, h, :], DTm[:, h, :], maskF, ALU.mult, [B_DT, B_cst], [B_DT])
                    ACT(dtmp[:], rel, AF.Exp, [B_cst, B_lg], [B_dtmp], scale=nlg[:, 8 + h:9 + h])
                    TT("dve", dtmp[:], dtmp[:], maskB, ALU.mult, [B_dtmp, B_cst], [B_dtmp])
                    STT("dve", DTm[:, h, :], DTm[:, h, :], 1.0, dtmp[:], ALU.mult, ALU.add, [B_DT, B_dtmp], [B_DT])
                TS("dve", DTm[:], DTm[:], RS, None, ALU.mult, None, [B_DT], [B_DT])
                gb, bgb = load_bcast(es, "r_g", r_norm_g[l:l + 1, :], 1024)
                S = [sb(f"r_S{d_}", [128, 8, 128], F32, es) for d_ in range(2)]
                B_S = [Buf(), Buf()]
                S16 = Rot(nc, es, "r_S16", [128, 8, 128], BF16, 3)
                ld = {n: Rot(nc, es, "r_ld" + n, [128, 8, 128], BF16, 2) for n in ("q", "k", "v", "g")}
                kwr = Rot(nc, es, "r_kw", [128, 8, 128], BF16, 2)
                qkT = Rot(nc, es, "r_qkT", [128, 8, 128], BF16, 4)
                sTm = Rot(nc, es, "r_sTm", [128, 8, 128], BF16, 2)
                t32 = Rot(nc, es, "r_t32", [128, 4, 128], F32, 4)
                o32 = Rot(nc, es, "r_o32", [128, 8, 128], F32, 2)
                ssr = Rot(nc, es, "r_ss", [128, 24], F32, 2)
                yb = Rot(nc, es, "r_y", [128, 1024], BF16, 2)
                ytp = Rot(nc, es, "r_yt", [128, 8, 128], BF16, 2)

                def r_load(name, src, c):
                    t_, b_ = ld[name].get()
                    kb.dma("sp", t_[:], src[c * 128:(c + 1) * 128, :].rearrange("p (h d) -> p h d", h=8),
                           reads=[bsc(src.name, c)], writes=[b_])
                    return t_, b_

                def r_state_update(d_, kt, bk, vt, bv):
                    kw, bkw = kwr.get()
                    TT("dve", kw[:], kt[:], wk_[:, d_ * 8:(d_ + 1) * 8].to_broadcast([128, 8, 128]) if False else
                       wkq[:, 0, d_ * 8:(d_ + 1) * 8].unsqueeze(2).to_broadcast([128, 8, 128]), ALU.mult, [bk, B_lg], [bkw])
                    TT("pool", S[d_][:], S[d_][:], wkq[:, 2, d_ * 8:(d_ + 1) * 8].unsqueeze(2).to_broadcast([128, 8, 128]),
                       ALU.mult, [B_S[d_], B_lg], [B_S[d_]])
                    for hh in range(2):
                        ps, bp = psF.get()
                        for h4 in range(4):
                            h = hh * 4 + h4
                            MM(ps[:, h4 * 128:(h4 + 1) * 128], kw[:, h, :], vt[:, h, :], True, True, [bkw, bv], [bp], sig=(h4 == 3))
                        TT("dve", S[d_][:, hh * 4:(hh + 1) * 4, :], S[d_][:, hh * 4:(hh + 1) * 4, :],
                           ps[:].rearrange("p (h e) -> p h e", h=4), ALU.add, [B_S[d_], bp], [B_S[d_]])

                def r_state_init(d_, c, first):
                    if first:
                        kb.dma("sp", S[d_][:], st_r[l, d_].rearrange("h d e -> d h e"), writes=[B_S[d_]])
                    else:
                        TS("dve", S[d_][:], S[d_][:], carry, None, ALU.mult, None, [B_S[d_], B_flg], [B_S[d_]])

                def r_state_out(d_, seg):
                    kb.dma("sp", o_r[seg, l, d_].rearrange("h d e -> d h e"), S[d_][:], reads=[B_S[d_]], writes=[Buf()])

                yield
                for c in range(NCH - 1, -1, -1):
                    kt, bk = r_load("k", sc_rk, c)
                    vt, bv = r_load("v", sc_rv, c)
                    if c == NCH - 1:
                        r_state_init(1, c, True)
                    elif c % 2 == 1:
                        r_state_init(1, c, False)
                    s16, bs16 = S16.get()
                    CP("act", s16[:], S[1][:], [B_S[1]], [bs16])
                    kb.dma("sp", sb_r[c].rearrange("h d e -> d h e"), s16[:], reads=[bs16], writes=[bsc("sb_r", c)])
                    r_state_update(1, kt, bk, vt, bv)
                    if c % 2 == 0:
                        r_state_out(1, c // 2)
                    yield
                for c in range(NCH):
                    qt, bq = r_load("q", sc_rq, c)
                    kt, bk = r_load("k", sc_rk, c)
                    vt, bv = r_load("v", sc_rv, c)
                    gt, bg = r_load("g", sc_rg, c)
                    sbk, bsbk = S16.get()
                    kb.dma("sp", sbk[:], sb_r[c].rearrange("h d e -> d h e"), reads=[bsc("sb_r", c)], writes=[bsbk])
                    if c == 0:
                        r_state_init(0, c, True)
                    elif c % 2 == 0:
                        r_state_init(0, c, False)
                    sfk, bsfk = S16.get()
                    CP("act", sfk[:], S[0][:], [B_S[0]], [bsfk])
                    qT, bqT = qkT.get(); kT, bkT = qkT.get()
                    for (src, bsrc, dstT, bdst) in ((qt, bq, qT, bqT), (kt, bk, kT, bkT)):
                        pt, bpt = psT.get()
                        for h in range(8):
                            TR(pt[:, h * 128:(h + 1) * 128], src[:, h, :], [bsrc], [bpt], sig=(h == 7))
                        CPrr(dstT[:], pt[:].rearrange("p (h t) -> p h t", h=8), [bpt], [bdst])
                    stm, bstm = sTm.get()
                    for hh in range(2):
                        ps, bp = psF.get()
                        for h4 in range(4):
                            h = hh * 4 + h4
                            MM(ps[:, h4 * 128:(h4 + 1) * 128], kT[:, h, :], qT[:, h, :], True, True, [bkT, bqT], [bp], sig=(h4 == 3))
                        TT("dve", stm[:, hh * 4:(hh + 1) * 4, :], ps[:].rearrange("p (h e) -> p h e", h=4),
                           DTm[:, hh * 4:(hh + 1) * 4, :], ALU.mult, [bp, B_DT], [bstm])
                    o, bo = o32.get()
                    for hh in range(2):
                        psI, bpI = psF.get(); psA, bpA = psF.get(); psB, bpB = psF.get()
                        for h4 in range(4):
                            h = hh * 4 + h4
                            sl = slice(h4 * 128, (h4 + 1) * 128)
                            MM(psI[:, sl], stm[:, h, :], vt[:, h, :], True, True, [bstm, bv], [bpI], sig=(h4 == 3))
                            MM(psA[:, sl], qT[:, h, :], sfk[:, h, :], True, True, [bqT, bsfk], [bpA], sig=(h4 == 3))
                            MM(psB[:, sl], qT[:, h, :], sbk[:, h, :], True, True, [bqT, bsbk], [bpB], sig=(h4 == 3))
                        t1, bt1 = t32.get(); t2, bt2 = t32.get()
                        r4 = lambda p_: p_[:].rearrange("p (h e) -> p h e", h=4)
                        TT("dve", t1[:], r4(psA), wkq[:, 1, hh * 4:hh * 4 + 4].unsqueeze(2).to_broadcast([128, 4, 128]),
                           ALU.mult, [bpA, B_lg], [bt1])
                        TT("dve", t2[:], r4(psB), wkq[:, 1, 8 + hh * 4:8 + hh * 4 + 4].unsqueeze(2).to_broadcast([128, 4, 128]),
                           ALU.mult, [bpB, B_lg], [bt2])
                        TT("pool", t1[:], t1[:], t2[:], ALU.add, [bt1, bt2], [bt1])
                        TT("dve", o[:, hh * 4:(hh + 1) * 4, :], r4(psI), t1[:], ALU.add, [bpI, bt1], [bo])
                    sq, bsq = o32.get()
                    TT("pool", sq[:], o[:], o[:], ALU.mult, [bo], [bsq])
                    ss, bss = ssr.get()
                    kb.op("dve", lambda g: g.tensor_reduce(out=ss[:, 0:8], in_=sq[:], axis=AX.X, op=ALU.add), [bsq], [bss])
                    RSTD(ss[:, 16:24], ss[:, 8:16], ss[:, 0:8], 1.0 / 128, bss)
                    TT("dve", o[:], o[:], ss[:, 16:24].unsqueeze(2).to_broadcast([128, 8, 128]), ALU.mult, [bo, bss], [bo])
                    TT("pool", o[:], o[:], gb[:].rearrange("p (h e) -> p h e", h=8), ALU.mult, [bo, bgb], [bo])
                    y, by = yb.get()
                    TT("dve", y[:].rearrange("p (h e) -> p h e", h=8), o[:], gt[:], ALU.mult, [bo, bg], [by])
                    yT_store(es, y, by, 1, c, 8, ytp)
                    r_state_update(0, kt, bk, vt, bv)
                    if c % 2 == 1:
                        r_state_out(0, c // 2)
                    yield

            if stop != "inproj":
                with Scope(kb) as es:
                    gens = []
                    if not (dbg and dbg.get("skip_mlstm")):
                        gens.append(gen_mlstm(es))
                    if not (dbg and dbg.get("skip_ret")):
                        gens.append(gen_ret(es))
                    while gens:
                        for g_ in list(gens):
                            try:
                                next(g_)
                            except StopIteration:
                                gens.remove(g_)
            if stop in ("r_full", "m_full", "m_gates"):
                break
            def gen_ssd(es):
                DTt = sb("s_DT", [128, NCH, 32], F32, es); B_DTt = Buf()
                for c_ in range(NCH):
                    kb.dma("sp", DTt[:, c_, :], sc_sdt[c_ * 128:(c_ + 1) * 128, :], reads=[bsc("sc_sdt", c_)], writes=[B_DTt])
                sbias = sb("s_bias", [128, 32], F32, es); B_sb = Buf()
                Arep = sb("s_A", [128, 32], F32, es)
                Dbc = sb("s_D", [128, 16], F32, es)
                kb.dma("sp", sbias[:], s_dt_bias[l:l + 1, :].partition_broadcast(128), writes=[B_sb])
                kb.dma("sp", Arep[:], s_a_log[l:l + 1, :].partition_broadcast(128), writes=[B_sb])
                kb.dma("sp", Dbc[:], s_d[l:l + 1, :].partition_broadcast(128), writes=[B_sb])
                ACT(Arep[:], Arep[:], AF.Exp, [B_sb], [B_sb])
                TS("dve", Arep[:], Arep[:], -1.0, None, ALU.mult, None, [B_sb], [B_sb])
                TT("dve", DTt[:], DTt[:], sbias[:].unsqueeze(1).to_broadcast([128, NCH, 32]), ALU.add, [B_DTt, B_sb], [B_DTt])
                ACT(DTt[:], DTt[:], AF.Exp, [B_DTt], [B_DTt])
                ACT(DTt[:], DTt[:], AF.Ln, [B_DTt], [B_DTt], bias=1.0)
                dA = sb("s_dA", [128, NCH, 32], F32, es); B_dA = Buf()
                TT("dve", dA[:], DTt[:], Arep[:].unsqueeze(1).to_broadcast([128, NCH, 32]), ALU.mult, [B_DTt, B_sb], [B_dA])
                CS = sb("s_CS", [128, NCH, 64], F32, es); B_CS = Buf()
                for hf in range(2):
                    ps, bp = psF.get()
                    for cc in range(8):
                        c = hf * 8 + cc
                        MM(ps[:, cc * 64:cc * 64 + 16], maskF, dA[:, c, 0:16], True, True, [B_cst, B_dA], [bp], sig=False)
                        MM(ps[:, cc * 64 + 16:cc * 64 + 32], maskB, dA[:, c, 16:32], True, True, [B_cst, B_dA], [bp], sig=False)
                        MM(ps[:, cc * 64 + 32:cc * 64 + 64], ones_f, dA[:, c, :], True, True, [B_cst, B_dA], [bp], sig=(cc == 7))
                    CP("dve", CS[:, hf * 8:(hf + 1) * 8, :], ps[:].rearrange("p (c g) -> p c g", c=8), [bp], [B_CS])
                ECS = sb("s_ECS", [128, NCH, 32], F32, es); WEND = sb("s_WEND", [128, NCH, 32], F32, es)
                ETOT = sb("s_ETOT", [128, NCH, 32], F32, es); B_E = Buf()
                ACT(ECS[:], CS[:, :, 0:32], AF.Exp, [B_CS], [B_E])
                TT("dve", WEND[:], CS[:, :, 32:64], CS[:, :, 0:32], ALU.subtract, [B_CS], [B_E])
                ACT(WEND[:], WEND[:], AF.Exp, [B_E], [B_E])
                ACT(ETOT[:], CS[:, :, 32:64], AF.Exp, [B_CS], [B_E])
                nm4 = sb("s_nm4", [128, 2, 4, 128], F32, es); B_nm4 = Buf()
                CP("dve", nm4[:, 0], negF.unsqueeze(1).to_broadcast([128, 4, 128]), [B_cst], [B_nm4])
                CP("dve", nm4[:, 1], negB.unsqueeze(1).to_broadcast([128, 4, 128]), [B_cst], [B_nm4])
                gb, bgb = load_bcast(es, "s_g", s_norm_g[l:l + 1, :], 1024)
                H = [sb(f"s_H{d_}", [128, 16, 64], F32, es) for d_ in range(2)]
                B_H = [Buf(), Buf()]
                H16 = Rot(nc, es, "s_H16", [128, 1024], BF16, 3)
                xTr = Rot(nc, es, "s_xT", [128, 8, 128], BF16, 2)
                xtk = Rot(nc, es, "s_xtok", [128, 16, 64], BF16, 2)
                bcr = Rot(nc, es, "s_BC", [128, 4, 128], BF16, 2)
                btk = Rot(nc, es, "s_Btok", [128, 2, 128], BF16, 2)
                zr = Rot(nc, es, "s_z", [128, 1024], BF16, 2)
                xdr = Rot(nc, es, "s_xdt", [128, 16, 64], BF16, 4)
                xwr = Rot(nc, es, "s_xw", [128, 16, 64], BF16, 2)
                dgr = Rot(nc, es, "s_dg", [128, 8, 128], F32, 2)
                rrr = Rot(nc, es, "s_rr", [128, 8, 128], F32, 2)
                dcr = Rot(nc, es, "s_dc", [128, 8, 128], F32, 2)
                Gr = Rot(nc, es, "s_G", [128, 16, 128], BF16, 3)
                cbr = Rot(nc, es, "s_cb", [128, 2, 128], F32, 2)
                t32 = Rot(nc, es, "s_t32", [128, 16, 64], F32, 3)
                yar = Rot(nc, es, "s_yacc", [128, 16, 64], F32, 2)
                ssr = Rot(nc, es, "s_ss", [128, 4], F32, 2)
                jkr = Rot(nc, es, "s_jk", [128, 1024], BF16, 1)
                yb = Rot(nc, es, "s_y", [128, 1024], BF16, 2)
                ytp = Rot(nc, es, "s_yt", [128, 8, 128], BF16, 2)
                stg = Rot(nc, es, "s_stg", [128, 8, 128], F32, 2)

                def s_load_x(c):
                    xT_, bxT = xTr.get()
                    kb.dma("sp", xT_[:], sc_sxT.rearrange("b p t -> p b t")[:, :, c * 128:(c + 1) * 128], reads=[bsc("sxT", 0)], writes=[bxT])
                    pt, bpt = psT.get()
                    for k in range(8):
                        TR(pt[:, k * 128:(k + 1) * 128], xT_[:, k, :], [bxT], [bpt], sig=(k == 7))
                    xt_, bxt = xtk.get()
                    CPrr(xt_[:].rearrange("p h d -> p (h d)"), pt[:], [bpt], [bxt])
                    return xt_, bxt

                def s_load_bc(c):
                    bc, bbc = bcr.get()
                    kb.dma("sp", bc[:, 0:2, :], sc_sBT.rearrange("g p t -> p g t")[:, :, c * 128:(c + 1) * 128], reads=[bsc("sBT", 0)], writes=[bbc])
                    kb.dma("sp", bc[:, 2:4, :], sc_sCT.rearrange("g p t -> p g t")[:, :, c * 128:(c + 1) * 128], reads=[bsc("sCT", 0)], writes=[bbc])
                    pt, bpt = psT.get()
                    for g in range(2):
                        TR(pt[:, g * 128:(g + 1) * 128], bc[:, g, :], [bbc], [bpt], sig=(g == 1))
                    bt_, bbt = btk.get()
                    CPrr(bt_[:].rearrange("p g s -> p (g s)"), pt[:, 0:256], [bpt], [bbt])
                    return bc, bbc, bt_, bbt

                def s_xdt(d_, c, xt_, bxt):
                    xd, bxd = xdr.get()
                    TT("pool", xd[:], xt_[:], DTt[:, c, d_ * 16:(d_ + 1) * 16].unsqueeze(2).to_broadcast([128, 16, 64]), ALU.mult, [bxt, B_DTt], [bxd])
                    return xd, bxd

                def s_state_init(d_, first):
                    if first:
                        st_, bst = stg.get()
                        kb.dma("sp", st_[:], st_s[l, d_].rearrange("h p s -> (h p) s").rearrange("(b r) s -> r b s", r=128), writes=[bst])
                        for hf in range(2):
                            ps, bp = psF.get()
                            for b4 in range(4):
                                b = hf * 4 + b4
                                MM(ps[:, b4 * 128:(b4 + 1) * 128], st_[:, b, :], ident_f, True, True, [bst, B_cst], [bp], sig=(b4 == 3))
                            CP("dve", H[d_][:, hf * 8:(hf + 1) * 8, :].rearrange("p h d -> p (h d)"), ps[:], [bp], [B_H[d_]])
                    else:
                        TS("dve", H[d_][:], H[d_][:], carry, None, ALU.mult, None, [B_H[d_], B_flg], [B_H[d_]])

                def s_state_out(d_, seg):
                    st_, bst = stg.get()
                    Hf = H[d_][:].rearrange("p h d -> p (h d)")
                    for hf in range(2):
                        ps, bp = psF.get()
                        for b4 in range(4):
                            b = hf * 4 + b4
                            MM(ps[:, b4 * 128:(b4 + 1) * 128], Hf[:, b * 128:(b + 1) * 128], ident_f, True, True, [B_H[d_], B_cst], [bp], sig=(b4 == 3))
                        CP("dve", st_[:, hf * 4:(hf + 1) * 4, :], ps[:].rearrange("p (b s) -> p b s", b=4), [bp], [bst])
                    kb.dma("sp", o_s[seg, l, d_].rearrange("h p s -> (h p) s").rearrange("(b r) s -> r b s", r=128), st_[:], reads=[bst], writes=[Buf()])

                def s_state_update(d_, c, xd, bxd, bt_, bbt):
                    xw, bxw = xwr.get()
                    TT("pool", xw[:], xd[:], WEND[:, c, d_ * 16:(d_ + 1) * 16].unsqueeze(2).to_broadcast([128, 16, 64]), ALU.mult, [bxd, B_E], [bxw])
                    TT("dve", H[d_][:], H[d_][:], ETOT[:, c, d_ * 16:(d_ + 1) * 16].unsqueeze(2).to_broadcast([128, 16, 64]), ALU.mult,
                       [B_H[d_], B_E], [B_H[d_]])
                    for g in range(2):
                        ps, bp = psF.get()
                        MM(ps[:], bt_[:, g, :], xw[:, g * 8:(g + 1) * 8, :].rearrange("p h d -> p (h d)"), True, True, [bbt, bxw], [bp])
                        TT("dve", H[d_][:, g * 8:(g + 1) * 8, :].rearrange("p h d -> p (h d)"),
                           H[d_][:, g * 8:(g + 1) * 8, :].rearrange("p h d -> p (h d)"), ps[:], ALU.add, [B_H[d_], bp], [B_H[d_]])

                for c in range(NCH - 1, -1, -1):
                    xt_, bxt = s_load_x(c)
                    bc, bbc, bt_, bbt = s_load_bc(c)
                    if c == NCH - 1:
                        s_state_init(1, True)
                    elif c % 2 == 1:
                        s_state_init(1, False)
                    h16, bh16 = H16.get()
                    CP("act", h16[:], H[1][:].rearrange("p h d -> p (h d)"), [B_H[1]], [bh16])
                    kb.dma("sp", sb_s[c].rearrange("g s n -> s g n"), h16[:].rearrange("p (g n) -> p g n", g=2), reads=[bh16], writes=[bsc("sb_s", c)])
                    xd, bxd = s_xdt(1, c, xt_, bxt)
                    s_state_update(1, c, xd, bxd, bt_, bbt)
                    if c % 2 == 0:
                        s_state_out(1, c // 2)
                    yield
                for c in range(NCH):
                    xt_, bxt = s_load_x(c)
                    bc, bbc, bt_, bbt = s_load_bc(c)
                    z_, bz = zr.get()
                    kb.dma("sp", z_[:], sc_sz[c * 128:(c + 1) * 128, :], reads=[bsc("sc_sz", c)], writes=[bz])
                    hb16, bhb16 = H16.get()
                    kb.dma("sp", hb16[:].rearrange("p (g n) -> p g n", g=2), sb_s[c].rearrange("g s n -> s g n"), reads=[bsc("sb_s", c)], writes=[bhb16])
                    if c == 0:
                        s_state_init(0, True)
                    elif c % 2 == 0:
                        s_state_init(0, False)
                    hf16, bhf16 = H16.get()
                    CP("act", hf16[:], H[0][:].rearrange("p h d -> p (h d)"), [B_H[0]], [bhf16])
                    psc, bpc = psF.get()
                    for g in range(2):
                        MM(psc[:, g * 128:(g + 1) * 128], bc[:, g, :], bc[:, 2 + g, :], True, True, [bbc], [bpc], sig=(g == 1))
                    cb, bcb = cbr.get()
                    CP("dve", cb[:].rearrange("p g i -> p (g i)"), psc[:, 0:256], [bpc], [bcb])
                    xds = []; Gs = []
                    for d_ in range(2):
                        xds.append(s_xdt(d_, c, xt_, bxt))
                        G_, bG = Gr.get()
                        st = []
                        for g in range(2):
                            cs_ = CS[:, c, d_ * 16 + g * 8:d_ * 16 + (g + 1) * 8]
                            dg, bdg = dgr.get(); rr, brr = rrr.get()
                            TT("dve", dg[:], cst[:, 0:1, :].to_broadcast([128, 8, 128]), cs_.unsqueeze(2).to_broadcast([128, 8, 128]),
                               ALU.mult, [B_cst, B_CS], [bdg])
                            CP("pool", rr[:], cs_.unsqueeze(2).to_broadcast([128, 8, 128]), [B_CS], [brr])
                            dc, bdc = dcr.get()
                            st.append((dg, bdg, rr, brr, dc, bdc))
                        pss_ = []
                        for g in range(2):
                            dg, bdg, rr, brr, dc, bdc = st[g]
                            for hf in range(2):
                                ps, bp = psF.get()
                                MM(ps[:], ones_f, dg[:, hf * 4:(hf + 1) * 4, :].rearrange("p h i -> p (h i)"), True, False, [B_cst, bdg], [bp], sig=False)
                                MM(ps[:], cst[:, 7, :], rr[:, hf * 4:(hf + 1) * 4, :].rearrange("p h i -> p (h i)"), False, True, [B_cst, brr], [bp])
                                pss_.append((ps, bp))
                        for g in range(2):
                            dc, bdc = st[g][4], st[g][5]
                            for hf in range(2):
                                ps, bp = pss_[g * 2 + hf]
                                dch = dc[:, hf * 4:(hf + 1) * 4, :].rearrange("p h i -> p (h i)")
                                STT("dve", dch, ps[:], 0.0, nm4[:, d_].rearrange("p h i -> p (h i)"), ALU.min, ALU.add, [bp, B_nm4], [bdc])
                        for g in range(2):
                            dc, bdc = st[g][4], st[g][5]
                            ACT(dc[:], dc[:], AF.Exp, [bdc], [bdc])
                        for g in range(2):
                            dc, bdc = st[g][4], st[g][5]
                            TT("dve", G_[:, g * 8:(g + 1) * 8, :], dc[:], cb[:, g:g + 1, :].to_broadcast([128, 8, 128]), ALU.mult, [bdc, bcb], [bG])
                        Gs.append((G_, bG))
                    yacc, bya = yar.get()
                    for g in range(2):
                        psI, bpI = psF.get(); psA, bpA = psF.get(); psB, bpB = psF.get()
                        for h8 in range(8):
                            h = g * 8 + h8
                            MM(psI[:, h8 * 64:(h8 + 1) * 64], Gs[0][0][:, h, :], xds[0][0][:, h, :], True, False, [Gs[0][1], xds[0][1]], [bpI], sig=False)
                            MM(psI[:, h8 * 64:(h8 + 1) * 64], Gs[1][0][:, h, :], xds[1][0][:, h, :], False, True, [Gs[1][1], xds[1][1]], [bpI], sig=(h8 == 7))
                        MM(psA[:], bc[:, 2 + g, :], hf16[:, g * 512:(g + 1) * 512], True, True, [bbc, bhf16], [bpA])
                        MM(psB[:], bc[:, 2 + g, :], hb16[:, g * 512:(g + 1) * 512], True, True, [bbc, bhb16], [bpB])
                        t1, bt1 = t32.get(); t2, bt2 = t32.get()
                        r8 = lambda p_: p_[:].rearrange("p (h d) -> p h d", h=8)
                        TT("dve", t1[:, 0:8, :], r8(psA), ECS[:, c, g * 8:(g + 1) * 8].unsqueeze(2).to_broadcast([128, 8, 64]), ALU.mult, [bpA, B_E], [bt1])
                        TT("dve", t2[:, 0:8, :], r8(psB), ECS[:, c, 16 + g * 8:16 + (g + 1) * 8].unsqueeze(2).to_broadcast([128, 8, 64]), ALU.mult, [bpB, B_E], [bt2])
                        TT("pool", t1[:, 0:8, :], t1[:, 0:8, :], t2[:, 0:8, :], ALU.add, [bt1, bt2], [bt1])
                        TT("dve", yacc[:, g * 8:(g + 1) * 8, :], r8(psI), t1[:, 0:8, :], ALU.add, [bpI, bt1], [bya])
                    xD, bxD = t32.get()
                    TT("pool", xD[:], xt_[:], Dbc[:].unsqueeze(2).to_broadcast([128, 16, 64]), ALU.mult, [bxt, B_sb], [bxD])
                    TT("pool", yacc[:], yacc[:], xD[:], ALU.add, [bya, bxD], [bya])
                    if l == 0 and c == 0:
                        dump("d_yacc", yacc[:], [bya]); dump("d_xtok", xt_[:], [bxt])
                    yf = yacc[:].rearrange("p h d -> p (h d)")
                    TT("dve", yf, yf, z_[:], ALU.mult, [bya, bz], [bya])
                    ss, bss = ssr.get(); jk, bjk = jkr.get()
                    ACT(jk[:], yf, AF.Square, [bya], [bjk, bss], accum=ss[:, 0:1])
                    RSTD(ss[:, 2:3], ss[:, 1:2], ss[:, 0:1], 1.0 / 1024, bss)
                    y, by = yb.get()
                    STT("dve", y[:], yf, ss[:, 2:3], gb[:], ALU.mult, ALU.mult, [bya, bss, bgb], [by])
                    yT_store(es, y, by, 2, c, 8, ytp)
                    s_state_update(0, c, xds[0][0], xds[0][1], bt_, bbt)
                    if c % 2 == 1:
                        s_state_out(0, c // 2)
                    yield
            if not (dbg and dbg.get("skip_ssd")):
                with Scope(kb) as es:
                    gens = [gen_ssd(es)]
                    if l + 1 < DEPTH:
                        gens.append(gen_mod(es, l + 1, 2))
                    while gens:
                        for g_ in list(gens):
                            try:
                                next(g_)
                            except StopIteration:
                                gens.remove(g_)
            if stop == "s_full":
                break
            if stop in ("inproj", "mix"):
                break
            with Scope(kb) as es:
                mT = sb("mT", [128, KC, T], BF16, es)
                B_mT = [Buf(f"mT{t_}") for t_ in range(4)]
                with Scope(kb) as es1:
                    yT = []
                    for b in range(3):
                        t_ = sb(f"yT{b}", [128, 8, T], BF16, es1); bb = Buf()
                        for hf in range(2):
                            kb.dma("sp", t_[:, hf * 4:(hf + 1) * 4, :],
                                   sc_yT[b].rearrange("(k p) t -> p k t", p=128)[:, hf * 4:(hf + 1) * 4, :],
                                   reads=[bsc(f"yT{b}", 0)], writes=[bb])
                        yT.append((t_, bb))
                    wbr = Rot(nc, es1, "wbr", [128, 8, 512], BF16, 3)
                    gt_ = Rot(nc, es1, "gtile", [128, 512], BF16, 3)
                    acc = Rot(nc, es1, "macc", [128, 512], F32, 3)
                    tm = Rot(nc, es1, "mtmp", [128, 512], F32, 3)
                    for cb4 in range(4):
                        wts = []
                        for b in range(3):
                            wt, bw = wbr.get()
                            for k4 in range(2):
                                kb.dma("pool", wt[:, k4 * 4:(k4 + 1) * 4, :], w_br[b][l].rearrange("(k p) n -> p k n", p=128)[:, k4 * 4:(k4 + 1) * 4, cb4 * 512:(cb4 + 1) * 512],
                                       writes=[bw])
                            wts.append((wt, bw))
                        for j in range(4):
                            kc_ = cb4 * 4 + j
                            for tt in range(4):
                                a, ba = acc.get()
                                for b in range(3):
                                    g_, bg_ = gt_.get()
                                    kb.dma("sp", g_[:], sc_gT[b, kc_ * 128:(kc_ + 1) * 128, tt * 512:(tt + 1) * 512],
                                           reads=[bsc("gT", 0)], writes=[bg_])
                                    ps, bp = psF.get()
                                    wt, bw = wts[b]
                                    for k in range(8):
                                        MM(ps[:], wt[:, k, j * 128:(j + 1) * 128], yT[b][0][:, k, tt * 512:(tt + 1) * 512],
                                           k == 0, k == 7, [bw, yT[b][1]], [bp])
                                    if b == 0:
                                        TT("dve", a[:], ps[:], g_[:], ALU.mult, [bp, bg_], [ba])
                                    else:
                                        t_, bt_ = tm.get()
                                        TT("dve", t_[:], ps[:], g_[:], ALU.mult, [bp, bg_], [bt_])
                                        if b == 1:
                                            TT("dve", a[:], a[:], t_[:], ALU.add, [ba, bt_], [ba])
                                        else:
                                            TT("dve", mT[:, kc_, tt * 512:(tt + 1) * 512], a[:], t_[:], ALU.add, [ba, bt_], [B_mT[tt]])
                with Scope(kb) as es1:
                    ga, bga = load_bcast(es1, "ga", modrow[l, 2:3, :], D)
                    wo = Rot(nc, es1, "wo", [128, KC, 512], BF16, 2)
                    xin = Rot(nc, es1, "xin", [128, 512], F32, 3)
                    xo = Rot(nc, es1, "xo", [128, 512], F32, 3)
                    for n4 in range(4):
                        wt, bw = wo.get()
                        for k4 in range(4):
                            kb.dma("pool", wt[:, k4 * 4:(k4 + 1) * 4, :], w_out[l].rearrange("(k p) n -> p k n", p=128)[:, k4 * 4:(k4 + 1) * 4, n4 * 512:(n4 + 1) * 512], writes=[bw])
                        for c in range(NCH):
                            xi, bxi = xin.get()
                            src = x_in if l == 0 else xres
                            kb.dma("sp", xi[:], src[c * 128:(c + 1) * 128, n4 * 512:(n4 + 1) * 512],
                                   reads=([] if l == 0 else [B_xres[c]]), writes=[bxi])
                            ps, bp = psF.get()
                            for k in range(KC):
                                MM(ps[:], mT[:, k, c * 128:(c + 1) * 128], wt[:, k, :], k == 0, k == KC - 1, [B_mT[c // 4], bw], [bp])
                            x2, bx2 = xo.get()
                            TT("dve", x2[:], ps[:], ga[:, n4 * 512:(n4 + 1) * 512], ALU.mult, [bp, bga], [bx2])
                            TT("dve", x2[:], x2[:], xi[:], ALU.add, [bx2, bxi], [bx2])
                            kb.dma("sp", xres[c * 128:(c + 1) * 128, n4 * 512:(n4 + 1) * 512], x2[:], reads=[bx2], writes=[B_xres[c]])
            if stop == "merge":
                break
            with Scope(kb) as es:
                h2T = sb("h2T", [128, KC, T], BF16, es)
                B_h2T = [Buf(f"h2T{c}") for c in range(NCH)]
                with Scope(kb) as es1:
                    gm, bgm = load_bcast(es1, "gm2", modrow[l, 3:4, :], D)
                    sh, bsh = load_bcast(es1, "sh2", modrow[l, 4:5, :], D)
                    wk = mk_norm_wk(es1)
                    xr = Rot(nc, es1, "xr2", [128, D], F32, 2)
                    for c in range(NCH):
                        xt, bx = xr.get()
                        kb.dma("sp", xt[:], xres[c * 128:(c + 1) * 128, :], reads=[B_xres[c]], writes=[bx])
                        norm_to_hT(xt, bx, gm, bgm, sh, bsh, h2T, B_h2T[c], c, wk)
                with Scope(kb) as es1:
                    gf, bgf = load_bcast(es1, "gf", modrow[l, 5:6, :], D)
                    uT = sb("uT", [128, KC, T], BF16, es1)
                    B_uT = [Buf(f"uT{t_}") for t_ in range(4)]
                    wf = Rot(nc, es1, "wf", [128, KC, 512], BF16, 3)
                    rl = Rot(nc, es1, "rl", [128, 512], F32, 2)
                    xin = Rot(nc, es1, "xin2", [128, 512], F32, 3)
                    xo = Rot(nc, es1, "xo2", [128, 512], F32, 3)
                    w1v = w_ff1[l].rearrange("(k p) n -> p k n", p=128)
                    w2v = w_ff2[l].rearrange("(q k p) n -> q p k n", p=128, k=16)
                    for hb in range(4):
                        for fb in range(4):
                            wt, bw = wf.get()
                            col0 = (hb * 4 + fb) * 512
                            for k4 in range(4):
                                kb.dma("pool", wt[:, k4 * 4:(k4 + 1) * 4, :], w1v[:, k4 * 4:(k4 + 1) * 4, col0:col0 + 512], writes=[bw])
                            for j in range(4):
                                for tt in range(4):
                                    ps, bp = psF.get()
                                    for k in range(KC):
                                        MM(ps[:], wt[:, k, j * 128:(j + 1) * 128], h2T[:, k, tt * 512:(tt + 1) * 512],
                                           k == 0, k == KC - 1, B_h2T[tt * 4:(tt + 1) * 4] + [bw], [bp])
                                    r_, br_ = rl.get()
                                    ACT(r_[:], ps[:], AF.Relu, [bp], [br_])
                                    TT("dve", uT[:, fb * 4 + j, tt * 512:(tt + 1) * 512], r_[:], r_[:], ALU.mult, [br_], [B_uT[tt]])
                        for n4 in range(4):
                            wt, bw = wf.get()
                            for k4 in range(4):
                                kb.dma("pool", wt[:, k4 * 4:(k4 + 1) * 4, :], w2v[hb][:, k4 * 4:(k4 + 1) * 4, n4 * 512:(n4 + 1) * 512], writes=[bw])
                            for c in range(NCH):
                                rs_, cs_ = slice(c * 128, (c + 1) * 128), slice(n4 * 512, (n4 + 1) * 512)
                                if hb > 0:
                                    xi, bxi = xin.get()
                                    kb.dma("sp", xi[:], ffacc[rs_, cs_], reads=[B_ffacc[c]], writes=[bxi])
                                if hb == 3:
                                    xr_, bxr_ = xin.get()
                                    kb.dma("sp", xr_[:], xres[rs_, cs_], reads=[B_xres[c]], writes=[bxr_])
                                ps, bp = psF.get()
                                for k in range(KC):
                                    MM(ps[:], uT[:, k, c * 128:(c + 1) * 128], wt[:, k, :], k == 0, k == KC - 1, [B_uT[c // 4], bw], [bp])
                                x2, bx2 = xo.get()
                                if hb == 0:
                                    CPrr(x2[:], ps[:], [bp], [bx2])
                                else:
                                    TT("dve", x2[:], ps[:], xi[:], ALU.add, [bp, bxi], [bx2])
                                if hb < 3:
                                    kb.dma("sp", ffacc[rs_, cs_], x2[:], reads=[bx2], writes=[B_ffacc[c]])
                                else:
                                    TT("dve", x2[:], x2[:], gf[:, cs_], ALU.mult, [bx2, bgf], [bx2])
                                    TT("dve", x2[:], x2[:], xr_[:], ALU.add, [bx2, bxr_], [bx2])
                                    kb.dma("sp", xres[rs_, cs_], x2[:], reads=[bx2], writes=[B_xres[c]])

        if stop is None or stop == "final":
            with Scope(kb) as es:
                fg = sb("fg", [128, D], F32, es); bfg = Buf()
                kb.dma("sp", fg[:], final_g.partition_broadcast(128), writes=[bfg])
                xr = Rot(nc, es, "xr3", [128, D], F32, 2)
                junk = Rot(nc, es, "fj", [128, D], BF16, 1)
                ssr = Rot(nc, es, "fss", [128, 4], F32, 2)
                yo = Rot(nc, es, "fy", [128, D], F32, 2)
                for c in range(NCH):
                    xt, bx = xr.get()
                    kb.dma("sp", xt[:], xres[c * 128:(c + 1) * 128, :], reads=[B_xres[c]], writes=[bx])
                    jk, bj = junk.get(); ss, bss = ssr.get()
                    ACT(jk[:], xt[:], AF.Square, [bx], [bj, bss], accum=ss[:, 0:1])
                    RSTD(ss[:, 2:3], ss[:, 1:2], ss[:, 0:1], 1.0 / D, bss)
                    y_, by_ = yo.get()
                    STT("dve", y_[:], xt[:], ss[:, 2:3], fg[:], ALU.mult, ALU.mult, [bx, bss, bfg], [by_])
                    kb.dma("sp", y_out[c * 128:(c + 1) * 128, :], y_[:], reads=[by_], writes=[Buf()])

        kb.finish()
    return nc, kb


def _consts():
    j = np.arange(128)[:, None].astype(np.float32)
    i = np.arange(128)[None, :].astype(np.float32)
    c = np.zeros((128, 8, 128), np.float32)
    c[:, 0] = (j == i)
    c[:, 1] = (j <= i)
    c[:, 2] = (j >= i)
    c[:, 3] = 1.0
    c[:, 4] = i - j
    c[:, 5] = np.where(j <= i, 0.0, -1000.0)
    c[:, 6] = np.where(j >= i, 0.0, -1000.0)
    c[:, 7] = -(j == i).astype(np.float32)
    pos = np.arange(128, dtype=np.float32)[:, None]
    posk = np.concatenate([np.repeat(127.0 - pos, 8, 1), np.repeat(pos, 8, 1)], 1)
    posq = np.concatenate([np.repeat(pos + 1.0, 8, 1), np.repeat(128.0 - pos, 8, 1)], 1)
    return c, posk.astype(np.float32), posq.astype(np.float32)


def _rope_tables(L):
    GRID_W = 64
    n_rows = L // GRID_W
    rows = np.repeat(np.arange(n_rows, dtype=np.float32), GRID_W)
    cols = np.tile(np.arange(GRID_W, dtype=np.float32), n_rows)
    inv = (np.float32(10000.0) ** (-np.arange(32, dtype=np.float32) / np.float32(32))).astype(np.float32)
    ang = np.concatenate([rows[:, None] * inv, cols[:, None] * inv], -1).astype(np.float32)
    return np.cos(ang).astype(np.float32), np.sin(ang).astype(np.float32)


def make_in_maps(inp):
    f = lambda a: np.ascontiguousarray(np.asarray(a, dtype=np.float32))
    c3, posk, posq = _consts()
    rc, rs = _rope_tables(T)
    shared = {
        "consts": c3, "posk": posk, "posq": posq,
        "w_mod": f(inp["w_mod"]), "b_mod": f(inp["b_mod"]),
        "norm_mix_g": f(inp["norm_mix_g"]), "norm_mlp_g": f(inp["norm_mlp_g"]),
        "w_in": f(inp["w_in"]),
        "m_igate_b": f(inp["m_igate_b"]).reshape(DEPTH, 8), "m_fgate_b": f(inp["m_fgate_b"]).reshape(DEPTH, 8),
        "m_norm_g": f(inp["m_norm_g"]), "r_decay": f(inp["r_decay"]).reshape(DEPTH, 16),
        "r_norm_g": f(inp["r_norm_g"]), "s_conv_w": f(inp["s_conv_w"]), "s_conv_b": f(inp["s_conv_b"]),
        "s_dt_bias": f(inp["s_dt_bias"]).reshape(DEPTH, 32), "s_a_log": f(inp["s_a_log"]).reshape(DEPTH, 32),
        "s_d": f(inp["s_d"]), "s_norm_g": f(inp["s_norm_g"]),
        "w_br_m": f(inp["w_br_m"]), "w_br_r": f(inp["w_br_r"]), "w_br_s": f(inp["w_br_s"]),
        "w_out": f(inp["w_out"]), "w_ff1": f(inp["w_ff1"]), "w_ff2": f(inp["w_ff2"]),
        "final_norm_g": f(inp["final_norm_g"]),
    }
    maps = []
    xs = f(inp["x_sample"]); xp = f(inp["x_prompt"])
    for core in range(8):
        m = dict(shared)
        if core < 4:
            b = core
            m["x"] = xs[b]
            m["cvec"] = f(inp["c"])[b]
            m["st_mC"] = f(inp["state_mlstm_C"])[b]
            m["st_mn"] = f(inp["state_mlstm_n"])[b]
            m["st_mm"] = f(inp["state_mlstm_m"])[b]
            m["st_r"] = f(inp["state_ret"])[b]
            m["st_s"] = f(inp["state_ssd"])[b]
            fl = np.zeros((128, 2), np.float32); fl[:, 0] = 1.0
            m["flags"] = fl
            m["rope_c"] = rc; m["rope_s"] = rs
        else:
            p = core - 4
            m["x"] = np.ascontiguousarray(xp[8 * p:8 * p + 8].reshape(T, D))
            m["cvec"] = f(inp["c_ctx"])
            m["st_mC"] = np.zeros((DEPTH, 2, 4, 256, 256), np.float32)
            m["st_mn"] = np.zeros((DEPTH, 2, 4, 256), np.float32)
            m["st_mm"] = np.zeros((DEPTH, 2, 4), np.float32)
            m["st_r"] = np.zeros((DEPTH, 2, 8, 128, 128), np.float32)
            m["st_s"] = np.zeros((DEPTH, 2, 16, 64, 128), np.float32)
            fl = np.zeros((128, 2), np.float32); fl[:, 1] = 1.0
            m["flags"] = fl
            m["rope_c"] = np.ones((T, 64), np.float32); m["rope_s"] = np.zeros((T, 64), np.float32)
        maps.append(m)
    return maps


def kernel(**inputs):
    nc, kb = build_program()
    maps = make_in_maps(inputs)
    res = run_bass_kernel_spmd(nc, maps, core_ids=list(range(8)))
    r = res.results
    y_sample = np.stack([r[b]["y"] for b in range(4)], 0)
    y_prompt = np.concatenate([r[4 + p]["y"].reshape(8, 256, D) for p in range(4)], 0)
    cat = lambda k: np.concatenate([r[4 + p][k] for p in range(4)], 0)
    return (y_prompt.astype(np.float32), y_sample.astype(np.float32), cat("o_mC"), cat("o_mn"), cat("o_mm"),
            cat("o_r"), cat("o_s"))
```

```python
import os
from contextlib import ExitStack
import numpy as np
import concourse.bass as bass
import concourse.mybir as mybir
from concourse.bass_utils import run_bass_kernel_spmd

F32 = mybir.dt.float32
BF16 = mybir.dt.bfloat16
AF = mybir.ActivationFunctionType
ALU = mybir.AluOpType
AX = mybir.AxisListType

D = 2048
T = 2048
NCH = 16
KC = 16
DEPTH = 2
EPS = 1e-6
IN_SIZES = (1024, 1024, 1024, 1024, 8, 8, 1024, 1024, 1024, 1024, 1024, 1536, 32, 6144)
IN_OFF = [0]
for _s in IN_SIZES:
    IN_OFF.append(IN_OFF[-1] + _s)
IN_COLS = IN_OFF[-1]
(O_MQ, O_MK, O_MV, O_MO, O_MI, O_MF, O_RQ, O_RK, O_RV, O_RG, O_SZ, O_SX, O_SDT, O_GL) = IN_OFF[:14]
DFF = 8192


class Buf:
    __slots__ = ("name", "w", "r")

    def __init__(self, name=""):
        self.name = name
        self.w = None
        self.r = []


class KB:
    def __init__(self, nc, n_dma_slots=14):
        self.nc = nc
        self.engs = {"pe": nc.tensor, "act": nc.scalar, "dve": nc.vector,
                     "pool": nc.gpsimd, "sp": nc.sync}
        self.sems = {}
        self.cnt = {}
        self.seen = {e: {} for e in self.engs}
        self._cms = []
        for e in ("pe", "act", "dve", "pool"):
            self.sems[e] = self._sem("p_" + e)
            self.cnt[e] = 0
        self.slots = {}
        for q in ("sp", "pool"):
            self.slots[q] = []
            for i in range(n_dma_slots):
                k = f"d_{q}{i}"
                self.sems[k] = self._sem(k)
                self.slots[q].append([k, 0])
        self.slot_i = {q: 0 for q in self.slots}
        self.same_engine_sync = True
        self.n_instr = 0

    def _sem(self, name):
        cm = self.nc.semaphore(name)
        h = cm.__enter__()
        self._cms.append(cm)
        return h

    def _wait(self, e, key, val):
        if key == e and (e == "pe" or not self.same_engine_sync):
            return
        if self.seen[e].get(key, 0) >= val:
            return
        self.engs[e].wait_ge(self.sems[key], val)
        self.seen[e][key] = val

    def _sync(self, e, reads, writes):
        for b in reads:
            if b.w is not None:
                self._wait(e, *b.w)
        for b in writes:
            if b.w is not None and b.w[0] != e:
                self._wait(e, *b.w)
            for r in b.r:
                if r[0] != e:
                    self._wait(e, *r)

    def _record(self, tok, reads, writes):
        for b in reads:
            b.r = [x for x in b.r if x[0] != tok[0]] + [tok]
        for b in writes:
            b.w = tok
            b.r = []

    def op(self, e, fn, reads=(), writes=(), sig=True):
        self._sync(e, reads, writes)
        ins = fn(self.engs[e])
        self.n_instr += 1
        if sig:
            self.cnt[e] += 1
            ins.then_inc(self.sems[e], 1)
            self._record((e, self.cnt[e]), reads, writes)
        return ins

    def dma(self, q, out, in_, reads=(), writes=(), **kw):
        sl = self.slots[q]
        i = self.slot_i[q]
        self.slot_i[q] = (i + 1) % len(sl)
        sem, n = sl[i]
        if n > 0:
            self._wait(q, sem, 16 * n)
        self._sync(q, reads, writes)
        ins = self.engs[q].dma_start(out=out, in_=in_, **kw)
        ins.then_inc(self.sems[sem], 16)
        sl[i][1] = n + 1
        self.n_instr += 1
        self._record((sem, 16 * (n + 1)), reads, writes)
        return ins

    def barrier(self):
        for e in ("pe", "act", "dve", "pool", "sp"):
            for o in ("pe", "act", "dve", "pool"):
                if self.cnt[o] > 0:
                    self._wait(e, o, self.cnt[o])
            for q in self.slots:
                for sem, n in self.slots[q]:
                    if n > 0:
                        self._wait(e, sem, 16 * n)

    def finish(self):
        for q in self.slots:
            for sem, n in self.slots[q]:
                if n > 0:
                    self._wait("sp", sem, 16 * n)
        for e in ("pe", "act", "dve", "pool"):
            if self.cnt[e] > 0:
                self._wait("sp", e, self.cnt[e])


class Scope(ExitStack):
    def __init__(self, kb):
        super().__init__()
        self._kb = kb

    def __exit__(self, *a):
        if a[0] is None:
            self._kb.barrier()
        return super().__exit__(*a)


class Rot:
    _uid = [0]

    def __init__(self, nc, es, name, shape, dtype, n, psum=False):
        Rot._uid[0] += 1
        name = f"{name}_u{Rot._uid[0]}_"
        self.t = []
        for i in range(n):
            if psum:
                h = es.enter_context(nc.psum_tensor(f"{name}{i}", shape, dtype))
            else:
                h = es.enter_context(nc.sbuf_tensor(f"{name}{i}", shape, dtype))
            self.t.append((h, Buf(f"{name}{i}")))
        self.i = 0

    def get(self):
        r = self.t[self.i]
        self.i = (self.i + 1) % len(self.t)
        return r


def build_program(dbg=None):
    nc = bass.Bass("TRN2", target_bir_lowering=False)
    kb = KB(nc)

    def din(name, shape, dt=F32):
        return nc.dram_tensor(name, list(shape), dt, kind="ExternalInput").ap()

    def dout(name, shape, dt=F32):
        return nc.dram_tensor(name, list(shape), dt, kind="ExternalOutput").ap()

    def dscr(name, shape, dt=BF16):
        if dbg and name in dbg:
            return nc.dram_tensor(name, list(shape), dt, kind="ExternalOutput").ap()
        return nc.dram_tensor(name, list(shape), dt).ap()

    x_in = din("x", [T, D])
    cvec = din("cvec", [D])
    st_mC = din("st_mC", [DEPTH, 2, 4, 256, 256])
    st_mn = din("st_mn", [DEPTH, 2, 4, 256])
    st_mm = din("st_mm", [DEPTH, 2, 4])
    st_r = din("st_r", [DEPTH, 2, 8, 128, 128])
    st_s = din("st_s", [DEPTH, 2, 16, 64, 128])
    flags = din("flags", [128, 2])
    rope_c = din("rope_c", [T, 64])
    rope_s = din("rope_s", [T, 64])
    consts = din("consts", [128, 8, 128])
    posk = din("posk", [128, 16])
    posq = din("posq", [128, 16])
    w_mod = din("w_mod", [DEPTH, D, 6 * D])
    b_mod = din("b_mod", [DEPTH, 6 * D])
    norm_mix_g = din("norm_mix_g", [DEPTH, D])
    norm_mlp_g = din("norm_mlp_g", [DEPTH, D])
    w_in = din("w_in", [DEPTH, D, IN_COLS])
    m_igate_b = din("m_igate_b", [DEPTH, 8])
    m_fgate_b = din("m_fgate_b", [DEPTH, 8])
    m_norm_g = din("m_norm_g", [DEPTH, 1024])
    r_decay = din("r_decay", [DEPTH, 16])
    r_norm_g = din("r_norm_g", [DEPTH, 1024])
    s_conv_w = din("s_conv_w", [DEPTH, 4, 1536])
    s_conv_b = din("s_conv_b", [DEPTH, 1536])
    s_dt_bias = din("s_dt_bias", [DEPTH, 32])
    s_a_log = din("s_a_log", [DEPTH, 32])
    s_d = din("s_d", [DEPTH, 16])
    s_norm_g = din("s_norm_g", [DEPTH, 1024])
    w_br = [din("w_br_m", [DEPTH, 1024, D]), din("w_br_r", [DEPTH, 1024, D]), din("w_br_s", [DEPTH, 1024, D])]
    w_out = din("w_out", [DEPTH, D, D])
    w_ff1 = din("w_ff1", [DEPTH, D, DFF])
    w_ff2 = din("w_ff2", [DEPTH, DFF, D])
    final_g = din("final_norm_g", [D])

    y_out = dout("y", [T, D])
    o_mC = dout("o_mC", [8, DEPTH, 2, 4, 256, 256])
    o_mn = dout("o_mn", [8, DEPTH, 2, 4, 256])
    o_mm = dout("o_mm", [8, DEPTH, 2, 4])
    o_r = dout("o_r", [8, DEPTH, 2, 8, 128, 128])
    o_s = dout("o_s", [8, DEPTH, 2, 16, 64, 128])

    xres = dscr("xres", [T, D], F32)
    ffacc = dscr("ffacc", [T, D], F32)
    modrow = dscr("modrow", [DEPTH, 6, D], F32)
    sc_mq = dscr("sc_mq", [T, 1024]); sc_mk = dscr("sc_mk", [T, 1024]); sc_mv = dscr("sc_mv", [T, 1024])
    sc_mo = dscr("sc_mo", [T, 1024]); sc_mg = dscr("sc_mg", [T, 16], F32)
    sc_rq = dscr("sc_rq", [T, 1024]); sc_rk = dscr("sc_rk", [T, 1024]); sc_rv = dscr("sc_rv", [T, 1024])
    sc_rg = dscr("sc_rg", [T, 1024])
    sc_sz = dscr("sc_sz", [T, 1024]); sc_sdt = dscr("sc_sdt", [T, 32], F32)
    sc_sxT = dscr("sc_sxT", [8, 128, T])
    sc_sBT = dscr("sc_sBT", [2, 128, T]); sc_sCT = dscr("sc_sCT", [2, 128, T])
    sc_gT = dscr("sc_gT", [3, D, T])
    sc_yT = [dscr("sc_ymT", [1024, T]), dscr("sc_yrT", [1024, T]), dscr("sc_ysT", [1024, T])]
    sb_m = dscr("sb_m", [NCH, 4, 2, 128, 257])
    sb_r = dscr("sb_r", [NCH, 8, 128, 128])
    sb_s = dscr("sb_s", [NCH, 2, 128, 512])

    B_xres = [Buf(f"xres{c}") for c in range(NCH)]
    B_ffacc = [Buf(f"ffacc{c}") for c in range(NCH)]
    B_mod = Buf("modrow")
    B_sc = {}

    def bsc(name, c=0):
        k = (name, c)
        if k not in B_sc:
            B_sc[k] = Buf(f"{name}{c}")
        return B_sc[k]

    es0 = ExitStack()
    with es0, nc.allow_non_contiguous_dma(reason="small strided parameter loads"):
        def sb(name, shape, dt=F32, es=es0):
            Rot._uid[0] += 1
            return es.enter_context(nc.sbuf_tensor(f"{name}_v{Rot._uid[0]}", list(shape), dt))

        cst = sb("cst", [128, 8, 128]); B_cst = Buf("cst")
        cstb = sb("cstb", [128, 8, 128], BF16); B_cstb = Buf("cstb")
        flg = sb("flg", [128, 2]); B_flg = Buf("flg")
        kb.dma("sp", cst[:], consts, writes=[B_cst])
        kb.dma("pool", cstb[:], consts, writes=[B_cstb])
        kb.dma("sp", flg[:], flags, writes=[B_flg])
        ident_f = cst[:, 0, :]; maskF = cst[:, 1, :]; maskB = cst[:, 2, :]; ones_f = cst[:, 3, :]
        rel = cst[:, 4, :]; negF = cst[:, 5, :]; negB = cst[:, 6, :]
        ident_b = cstb[:, 0, :]; ones_b = cstb[:, 3, :]
        carry = flg[:, 0:1]

        psF = Rot(nc, es0, "psF", [128, 512], F32, 6, psum=True)
        psT = Rot(nc, es0, "psT", [128, 1024], BF16, 2, psum=True)

        def dump(name, ap, reads):
            if dbg and name in dbg:
                d_ = nc.dram_tensor(name, list(ap.shape), ap.dtype, kind="ExternalOutput").ap()
                kb.dma("sp", d_, ap, reads=reads, writes=[Buf()])

        def MM(out, lhsT, rhs, start, stop, r=(), w=(), sig=None):
            if sig is None:
                sig = stop
            return kb.op("pe", lambda e: e.matmul(out, lhsT, rhs, start=start, stop=stop), r, w, sig=sig)

        def TR(out, in_, r=(), w=(), sig=True):
            return kb.op("pe", lambda e: e.transpose(out, in_, ident_b), list(r) + [B_cstb], w, sig=sig)

        def TT(e, out, in0, in1, op, r=(), w=()):
            return kb.op(e, lambda g: g.tensor_tensor(out=out, in0=in0, in1=in1, op=op), r, w)

        def TS(e, out, in0, s1, s2, op0, op1=None, r=(), w=()):
            if op1 is None:
                return kb.op(e, lambda g: g.tensor_scalar(out=out, in0=in0, scalar1=s1, scalar2=None, op0=op0), r, w)
            return kb.op(e, lambda g: g.tensor_scalar(out=out, in0=in0, scalar1=s1, scalar2=s2, op0=op0, op1=op1), r, w)

        def STT(e, out, in0, scalar, in1, op0, op1, r=(), w=()):
            return kb.op(e, lambda g: g.scalar_tensor_tensor(out=out, in0=in0, scalar=scalar, in1=in1, op0=op0, op1=op1), r, w)

        def ACT(out, in_, func, r=(), w=(), bias=0.0, scale=1.0, accum=None):
            if accum is None:
                return kb.op("act", lambda g: g.activation(out=out, in_=in_, func=func, bias=bias, scale=scale), r, w)
            return kb.op("act", lambda g: g.activation(out=out, in_=in_, func=func, bias=bias, scale=scale, accum_out=accum), r, w)

        def CP(e, out, in_, r=(), w=()):
            if e == "act":
                return ACT(out, in_, AF.Copy, r, w)
            return kb.op(e, lambda g: g.tensor_copy(out=out, in_=in_), r, w)

        def RSTD(out, tmp, ssq, inv_n, b):
            TS("dve", tmp, ssq, inv_n, EPS, ALU.mult, ALU.add, [b], [b])
            ACT(tmp, tmp, AF.Sqrt, [b], [b])
            kb.op("dve", lambda g: g.reciprocal(out=out, in_=tmp), [b], [b])

        cp_rr = [0]

        def CPrr(out, in_, r=(), w=()):
            cp_rr[0] ^= 1
            return CP("act" if cp_rr[0] else "dve", out, in_, r, w)

        cv = sb("cv", [128, KC], F32); B_cv = Buf()
        cvb = sb("cvb", [128, KC], BF16); B_cvb = Buf()
        sg = sb("sg", [128, KC], F32); B_sg = Buf()
        kb.dma("sp", cv[:], cvec.rearrange("(k p) -> p k", p=128), writes=[B_cv])
        ACT(sg[:], cv[:], AF.Sigmoid, [B_cv], [B_sg])
        TT("dve", cvb[:], cv[:], sg[:], ALU.mult, [B_cv, B_sg], [B_cvb])

        def gen_mod(es, l, nbuf):
            wm = Rot(nc, es, "wm", [128, KC, 512], BF16, nbuf)
            brow = Rot(nc, es, "brow", [1, 512], F32, 2)
            grow = Rot(nc, es, "grow", [1, 512], F32, 2)
            mrow = Rot(nc, es, "mrow", [1, 512], F32, 2)
            wv = w_mod[l].rearrange("(k p) n -> p k n", p=128)
            rowmap = {0: 1, 1: 0, 2: 2, 3: 4, 4: 3, 5: 5}
            for cb in range(24):
                seg, off = cb // 4, (cb % 4) * 512
                wt, bw = wm.get()
                for k4 in range(4):
                    kb.dma("pool", wt[:, k4 * 4:(k4 + 1) * 4, :], wv[:, k4 * 4:(k4 + 1) * 4, cb * 512:(cb + 1) * 512], writes=[bw])
                br_, bbr = brow.get()
                kb.dma("sp", br_[:], b_mod[l:l + 1, cb * 512:(cb + 1) * 512], writes=[bbr])
                if seg in (1, 4):
                    gr_, bgr = grow.get()
                    gsrc = norm_mix_g if seg == 1 else norm_mlp_g
                    kb.dma("sp", gr_[:], gsrc[l:l + 1, off:off + 512], writes=[bgr])
                ps, bp = psF.get()
                for k in range(KC):
                    MM(ps[0:1, :], cvb[:, k:k + 1], wt[:, k, :], k == 0, k == KC - 1, [B_cvb, bw], [bp])
                mr, bmr = mrow.get()
                TT("dve", mr[:], ps[0:1, :], br_[:], ALU.add, [bp, bbr], [bmr])
                if seg in (1, 4):
                    STT("dve", mr[:], mr[:], 1.0, gr_[:], ALU.add, ALU.mult, [bmr, bgr], [bmr])
                kb.dma("sp", modrow[l, rowmap[seg]:rowmap[seg] + 1, off:off + 512], mr[:], reads=[bmr], writes=[B_mod])
                yield

        with Scope(kb) as es:
            for _ in gen_mod(es, 0, 3):
                pass

        def load_bcast(es, name, src_row_ap, n, dt=F32, q="sp"):
            t = sb(name, [128, n], dt, es)
            b = Buf(name)
            kb.dma(q, t[:], src_row_ap.partition_broadcast(128), reads=[B_mod], writes=[b])
            return t, b

        def norm_to_hT(xt, bx, gm, bgm, sh, bsh, hT, bhT_c, c, wk):
            junk, bj = wk["junk"].get()
            ss, bss = wk["ss"].get()
            ACT(junk[:], xt[:], AF.Square, [bx], [bj, bss], accum=ss[:, 0:1])
            RSTD(ss[:, 2:3], ss[:, 1:2], ss[:, 0:1], 1.0 / D, bss)
            tmp, bt = wk["tmp"].get()
            STT("dve", tmp[:], xt[:], ss[:, 2:3], gm[:], ALU.mult, ALU.mult, [bx, bss, bgm], [bt])
            hb, bh = wk["hb"].get()
            TT("dve", hb[:], tmp[:], sh[:], ALU.add, [bt, bsh], [bh])
            for half in range(2):
                pt, bp = psT.get()
                for k in range(8):
                    kk = half * 8 + k
                    TR(pt[:, k * 128:(k + 1) * 128], hb[:, kk * 128:(kk + 1) * 128], [bh], [bp], sig=(k == 7))
                CPrr(hT[:, half * 8:(half + 1) * 8, c * 128:(c + 1) * 128],
                     pt[:].rearrange("p (k t) -> p k t", k=8), [bp], [bhT_c])

        def mk_norm_wk(es):
            return {"junk": Rot(nc, es, "nj", [128, D], BF16, 1),
                    "ss": Rot(nc, es, "nss", [128, 4], F32, 3),
                    "tmp": Rot(nc, es, "ntmp", [128, D], F32, 2),
                    "hb": Rot(nc, es, "nhb", [128, D], BF16, 2)}

        stop = dbg.get("stop") if dbg else None
        for l in range(DEPTH if stop != "mod" else 0):
            with Scope(kb) as es:
                hT = sb("hT", [128, KC, T], BF16, es)
                B_hT = [Buf(f"hT{c}") for c in range(NCH)]
                with Scope(kb) as es1:
                    gm, bgm = load_bcast(es1, "gm", modrow[l, 0:1, :], D)
                    sh, bsh = load_bcast(es1, "sh", modrow[l, 1:2, :], D)
                    wk = mk_norm_wk(es1)
                    xr = Rot(nc, es1, "xr", [128, D], F32, 2)
                    for c in range(NCH):
                        xt, bx = xr.get()
                        if l == 0:
                            kb.dma("sp", xt[:], x_in[c * 128:(c + 1) * 128, :], writes=[bx])
                        else:
                            kb.dma("sp", xt[:], xres[c * 128:(c + 1) * 128, :], reads=[B_xres[c]], writes=[bx])
                        norm_to_hT(xt, bx, gm, bgm, sh, bsh, hT, B_hT[c], c, wk)

                if dbg and "dbg_hT" in dbg and l == 0:
                    dh = nc.dram_tensor("dbg_hT", [128, KC, T], BF16, kind="ExternalOutput").ap()
                    kb.dma("sp", dh, hT[:], reads=B_hT, writes=[Buf()])
                with Scope(kb) as es1:
                    if stop == "norm":
                        break
                    wt_rot = Rot(nc, es1, "wi", [128, KC, 512], BF16, 3)
                    ob = Rot(nc, es1, "ob", [128, 512], BF16, 4)
                    of = Rot(nc, es1, "of", [128, 512], F32, 3)
                    rc = sb("rc", [128, NCH, 64], F32, es1); B_rc = Buf()
                    rs_ = sb("rs", [128, NCH, 64], F32, es1); B_rs = Buf()
                    kb.dma("sp", rc[:], rope_c.rearrange("(c p) f -> p c f", p=128), writes=[B_rc])
                    kb.dma("sp", rs_[:], rope_s.rearrange("(c p) f -> p c f", p=128), writes=[B_rs])
                    wv = w_in[l].rearrange("(k p) n -> p k n", p=128)

                    def load_w(col0, ncols):
                        wt, bw = wt_rot.get()
                        for k4 in range(4):
                            kb.dma("pool", wt[:, k4 * 4:(k4 + 1) * 4, 0:ncols], wv[:, k4 * 4:(k4 + 1) * 4, col0:col0 + ncols], writes=[bw])
                        return wt, bw

                    def gemm_tok(col0, ncols, epi):
                        wt, bw = load_w(col0, ncols)
                        for c in range(NCH):
                            ps, bp = psF.get()
                            for k in range(KC):
                                MM(ps[:, 0:ncols], hT[:, k, c * 128:(c + 1) * 128], wt[:, k, 0:ncols],
                                   k == 0, k == KC - 1, [B_hT[c], bw], [bp])
                            epi(c, ps, bp)

                    def epi_store(dst, dcol0, func=None, ncols=512):
                        def f(c, ps, bp):
                            o, bo = ob.get()
                            if func is None:
                                CPrr(o[:, 0:ncols], ps[:, 0:ncols], [bp], [bo])
                            else:
                                ACT(o[:, 0:ncols], ps[:, 0:ncols], func, [bp], [bo])
                            kb.dma("sp", dst[c * 128:(c + 1) * 128, dcol0:dcol0 + ncols], o[:, 0:ncols],
                                   reads=[bo], writes=[bsc(dst.name, c)])
                        return f

                    def epi_store_f32(dst, ncols):
                        def f(c, ps, bp):
                            o, bo = of.get()
                            CP("dve", o[:, 0:ncols], ps[:, 0:ncols], [bp], [bo])
                            kb.dma("sp", dst[c * 128:(c + 1) * 128, 0:ncols], o[:, 0:ncols],
                                   reads=[bo], writes=[bsc(dst.name, c)])
                        return f

                    def epi_rope(dst, dcol0):
                        def f(c, ps, bp):
                            p3 = ps[:].rearrange("p (h d) -> p h d", h=4)
                            x1 = p3[:, :, 0:64]; x2 = p3[:, :, 64:128]
                            cc = rc[:, c:c + 1, :].to_broadcast([128, 4, 64])
                            sn = rs_[:, c:c + 1, :].to_broadcast([128, 4, 64])
                            a, ba = of.get(); b_, bb = of.get()
                            a3 = a[:].rearrange("p (h d) -> p h d", h=4)
                            b3 = b_[:].rearrange("p (h d) -> p h d", h=4)
                            o, bo = ob.get()
                            o3 = o[:].rearrange("p (h d) -> p h d", h=4)
                            TT("dve", a3[:, :, 0:64], x1, cc, ALU.mult, [bp, B_rc], [ba])
                            TT("dve", a3[:, :, 64:128], x2, cc, ALU.mult, [bp, B_rc], [ba])
                            TT("dve", b3[:, :, 0:64], x2, sn, ALU.mult, [bp, B_rs], [bb])
                            TT("dve", b3[:, :, 64:128], x1, sn, ALU.mult, [bp, B_rs], [bb])
                            TT("dve", o3[:, :, 0:64], a3[:, :, 0:64], b3[:, :, 0:64], ALU.subtract, [ba, bb], [bo])
                            TT("dve", o3[:, :, 64:128], a3[:, :, 64:128], b3[:, :, 64:128], ALU.add, [ba, bb], [bo])
                            kb.dma("sp", dst[c * 128:(c + 1) * 128, dcol0:dcol0 + 512], o[:],
                                   reads=[bo], writes=[bsc(dst.name, c)])
                        return f

                    for half in range(2):
                        gemm_tok(O_MQ + half * 512, 512, epi_store(sc_mq, half * 512))
                        gemm_tok(O_MK + half * 512, 512, epi_store(sc_mk, half * 512))
                        gemm_tok(O_MV + half * 512, 512, epi_store(sc_mv, half * 512))
                        gemm_tok(O_MO + half * 512, 512, epi_store(sc_mo, half * 512, AF.Sigmoid))
                    gemm_tok(O_MI, 16, epi_store_f32(sc_mg, 16))
                    for half in range(2):
                        gemm_tok(O_RQ + half * 512, 512, epi_rope(sc_rq, half * 512))
                        gemm_tok(O_RK + half * 512, 512, epi_rope(sc_rk, half * 512))
                        gemm_tok(O_RV + half * 512, 512, epi_store(sc_rv, half * 512))
                        gemm_tok(O_RG + half * 512, 512, epi_store(sc_rg, half * 512, AF.Silu))
                        gemm_tok(O_SZ + half * 512, 512, epi_store(sc_sz, half * 512, AF.Silu))
                    gemm_tok(O_SDT, 32, epi_store_f32(sc_sdt, 32))
                    if stop == "tok":
                        break

                    def gemm_feat(col0, epi):
                        wt, bw = load_w(col0, 512)
                        for j in range(4):
                            for tt in range(4):
                                ps, bp = psF.get()
                                for k in range(KC):
                                    MM(ps[:], wt[:, k, j * 128:(j + 1) * 128], hT[:, k, tt * 512:(tt + 1) * 512],
                                       k == 0, k == KC - 1, B_hT[tt * 4:(tt + 1) * 4] + [bw], [bp])
                                epi(j, tt, ps, bp)

                    for blk in range(dbg.get('gl_blks', 12) if dbg else 12):
                        def epi_g(j, tt, ps, bp, blk=blk):
                            o, bo = ob.get()
                            ACT(o[:], ps[:], AF.Sigmoid, [bp], [bo])
                            row0 = blk * 512 + j * 128
                            br_, fr = divmod(row0, D)
                            kb.dma("sp", sc_gT[br_, fr:fr + 128, tt * 512:(tt + 1) * 512], o[:],
                                   reads=[bo], writes=[bsc("gT", 0)])
                        gemm_feat(O_GL + blk * 512, epi_g)

                    if stop == "gl":
                        break
                    pf_c = flg[:, 1:2]
                    cw = sb("cw", [128, 4, 12], F32, es1); B_cw = Buf()
                    cbias = sb("cbias", [128, 12], F32, es1); B_cb = Buf()
                    with nc.allow_non_contiguous_dma(reason="tiny conv weights"):
                        for k_ in range(4):
                            kb.dma("sp", cw[:, k_, :], s_conv_w[l, k_].rearrange("(b p) -> p b", p=128), writes=[B_cw])
                        kb.dma("sp", cbias[:], s_conv_b[l].rearrange("(b p) -> p b", p=128), writes=[B_cb])
                    cwp = sb("cwp", [128, 4, 12], F32, es1)
                    TS("dve", cwp[:], cw[:], pf_c, None, ALU.mult, None, [B_cw, B_flg], [B_cw])
                    conv_lvl = dbg.get("conv_lvl", 6) if dbg else 6
                    xrow = Rot(nc, es1, "xrow", [128, T], F32, 2)
                    acc = Rot(nc, es1, "cacc", [128, T], F32, 2)
                    cvo = Rot(nc, es1, "cvo", [128, T], BF16, 2)
                    tok = Rot(nc, es1, "ctok", [128, 128], BF16, 4)
                    pf_ = flg[:, 1:2]
                    for blk in range(3):
                        rows = {}

                        def epi_x(j, tt, ps, bp, rows=rows):
                            if tt == 0:
                                rows[j] = xrow.get()
                            xr_, bxr = rows[j]
                            CPrr(xr_[:, tt * 512:(tt + 1) * 512], ps[:], [bp], [bxr])
                            if tt < 3:
                                return
                            cbk = blk * 4 + j
                            if conv_lvl < 1:
                                return
                            a, ba = acc.get()
                            TS("dve", a[:], xr_[:], cw[:, 1, cbk:cbk + 1], None, ALU.mult, None, [bxr, B_cw], [ba])
                            STT("dve", a[:, 1:T], xr_[:, 0:T - 1], cw[:, 0, cbk:cbk + 1], a[:, 1:T], ALU.mult, ALU.add, [bxr, B_cw, ba], [ba])
                            STT("dve", a[:, 0:T - 1], xr_[:, 1:T], cw[:, 2, cbk:cbk + 1], a[:, 0:T - 1], ALU.mult, ALU.add, [bxr, B_cw, ba], [ba])
                            STT("dve", a[:, 0:T - 2], xr_[:, 2:T], cw[:, 3, cbk:cbk + 1], a[:, 0:T - 2], ALU.mult, ALU.add, [bxr, B_cw, ba], [ba])
                            t3, b3 = wk_small.get()
                            a_v = a[:].rearrange("p (s t) -> p s t", t=256)
                            x_v = xr_[:].rearrange("p (s t) -> p s t", t=256)
                            fixes = ((a_v[:, 1:8, 0], x_v[:, 0:7, 255], 0),
                                     (a_v[:, 0:7, 255], x_v[:, 1:8, 0], 2),
                                     (a_v[:, 0:7, 255], x_v[:, 1:8, 1], 3),
                                     (a_v[:, 0:7, 254], x_v[:, 1:8, 0], 3))
                            for (dst_, src_, kk) in (fixes if conv_lvl >= 2 else ()):
                                TS("dve", t3[:, 0:7], src_, cwp[:, kk, cbk:cbk + 1], None, ALU.mult, None, [bxr, B_cw], [b3])
                                TT("dve", dst_, dst_, t3[:, 0:7], ALU.subtract, [ba, b3], [ba])
                            if conv_lvl < 3:
                                return
                            o, bo = cvo.get()
                            ACT(o[:], a[:], AF.Silu, [ba, B_cb], [bo], bias=cbias[:, cbk:cbk + 1])
                            if conv_lvl < 4:
                                return
                            if cbk < 8:
                                kb.dma("sp", sc_sxT[cbk], o[:], reads=[bo], writes=[bsc("sxT", 0)])
                            if cbk in (8, 9):
                                kb.dma("sp", sc_sBT[cbk - 8], o[:], reads=[bo], writes=[bsc("sBT", 0)])
                            if cbk in (10, 11):
                                kb.dma("sp", sc_sCT[cbk - 10], o[:], reads=[bo], writes=[bsc("sCT", 0)])

                        wk_small = Rot(nc, es1, f"wks{blk}", [128, 8], F32, 2)
                        gemm_feat(O_SX + blk * 512, epi_x)

            def yT_store(es_, y, by, which, c, nblk, wkp):
                dstv = sc_yT[which].rearrange("(k p) t -> p k t", p=128)
                pt, bpt = psT.get()
                for k in range(nblk):
                    TR(pt[:, k * 128:(k + 1) * 128], y[:, k * 128:(k + 1) * 128], [by], [bpt], sig=(k == nblk - 1))
                yt, byt = wkp.get()
                CPrr(yt[:, 0:nblk, :], pt[:, 0:nblk * 128].rearrange("p (k t) -> p k t", k=nblk), [bpt], [byt])
                kb.dma("sp", dstv[:, 0:nblk, c * 128:(c + 1) * 128], yt[:, 0:nblk, :], reads=[byt],
                       writes=[bsc(f"yT{which}", 0)])

            def gen_mlstm(es):
                G = sb("m_G", [128, NCH, 16], F32, es); B_G = Buf()
                bias = sb("m_bias", [128, 16], F32, es); B_bias = Buf()
                for c_ in range(NCH):
                    kb.dma("sp", G[:, c_, :], sc_mg[c_ * 128:(c_ + 1) * 128, :], reads=[bsc("sc_mg", c_)], writes=[B_G])
                kb.dma("sp", bias[:, 0:8], m_igate_b[l:l + 1, :].partition_broadcast(128), writes=[B_bias])
                kb.dma("sp", bias[:, 8:16], m_fgate_b[l:l + 1, :].partition_broadcast(128), writes=[B_bias])
                TT("dve", G[:], G[:], bias[:].unsqueeze(1).to_broadcast([128, NCH, 16]), ALU.add, [B_G, B_bias], [B_G])
                LF = sb("m_LF", [128, NCH, 8], F32, es); B_LF = Buf()
                ACT(LF[:], G[:, :, 8:16], AF.Exp, [B_G], [B_LF], scale=-1.0)
                ACT(LF[:], LF[:], AF.Ln, [B_LF], [B_LF], bias=1.0)
                TS("dve", LF[:], LF[:], -1.0, None, ALU.mult, None, [B_LF], [B_LF])
                BH = sb("m_BH", [128, NCH, 16], F32, es); B_BH = Buf()
                ps, bp = psF.get()
                for c in range(NCH):
                    MM(ps[:, c * 16:c * 16 + 4], maskF, LF[:, c, 0:4], True, True, [B_cst, B_LF], [bp], sig=False)
                    MM(ps[:, c * 16 + 4:c * 16 + 8], maskB, LF[:, c, 4:8], True, True, [B_cst, B_LF], [bp], sig=False)
                    MM(ps[:, c * 16 + 8:c * 16 + 16], ones_f, LF[:, c, :], True, True, [B_cst, B_LF], [bp], sig=(c == NCH - 1))
                CP("dve", BH[:], ps[:, 0:256].rearrange("p (c g) -> p c g", c=NCH), [bp], [B_BH])
                A_ = sb("m_A", [128, NCH, 8], F32, es); B_A = Buf()
                TT("dve", A_[:], G[:, :, 0:8], BH[:, :, 0:8], ALU.subtract, [B_G, B_BH], [B_A])
                AM = sb("m_AM", [8, NCH], F32, es); B_AM = Buf()
                for c4 in range(4):
                    ps, bp = psF.get()
                    for cc in range(4):
                        c = c4 * 4 + cc
                        MM(ps[0:8, cc * 128:(cc + 1) * 128], A_[:, c, :], ident_f, True, True, [B_A, B_cst], [bp], sig=(cc == 3))
                    kb.op("dve", lambda g, ps=ps, c4=c4: g.tensor_reduce(out=AM[:, c4 * 4:(c4 + 1) * 4],
                          in_=ps[0:8, :].rearrange("p (c t) -> p c t", c=4), axis=AX.X, op=ALU.max), [bp], [B_AM])
                R8 = sb("m_R8", [8, NCH, 8], F32, es); B_R8 = Buf()
                TT("dve", R8[:], AM[:].unsqueeze(2).to_broadcast([8, NCH, 8]),
                   cst[0:8, 0, 0:8].unsqueeze(1).to_broadcast([8, NCH, 8]), ALU.mult, [B_AM, B_cst], [B_R8])
                ps, bp = psF.get()
                MM(ps[:, 0:128], cst[0:8, 3, :], R8[:].rearrange("p c g -> p (c g)"), True, True, [B_cst, B_R8], [bp])
                AR = sb("m_AR", [128, NCH, 8], F32, es); B_AR = Buf()
                CP("dve", AR[:], ps[:, 0:128].rearrange("p (c g) -> p c g", c=NCH), [bp], [B_AR])
                MU = sb("m_MU", [128, NCH, 8], F32, es); DEC = sb("m_DEC", [128, NCH, 8], F32, es); B_MU = Buf()
                mcur = sb("m_mcur", [128, 8], F32, es); B_mc = Buf()
                kb.dma("sp", mcur[:], st_mm[l:l + 1].rearrange("o d h -> o (d h)").partition_broadcast(128), writes=[B_mc])
                for d_ in range(2):
                    sl = slice(d_ * 4, d_ * 4 + 4)
                    order = range(NCH) if d_ == 0 else range(NCH - 1, -1, -1)
                    for n_, c in enumerate(order):
                        if n_ > 0 and n_ % 2 == 0:
                            TS("dve", mcur[:, sl], mcur[:, sl], carry, None, ALU.mult, None, [B_mc, B_flg], [B_mc])
                        TT("dve", MU[:, c, sl], AR[:, c, sl], mcur[:, sl], ALU.max, [B_AR, B_mc], [B_MU])
                        TT("dve", DEC[:, c, sl], mcur[:, sl], MU[:, c, sl], ALU.subtract, [B_mc, B_MU], [B_MU])
                        TT("dve", mcur[:, sl], BH[:, c, 8 + d_ * 4:12 + d_ * 4], MU[:, c, sl], ALU.add, [B_BH, B_MU], [B_mc])
                        if n_ % 2 == 1:
                            seg = c // 2
                            kb.dma("sp", o_mm[seg, l, d_:d_ + 1, :], mcur[0:1, sl], reads=[B_mc], writes=[Buf()])
                ACT(DEC[:], DEC[:], AF.Exp, [B_MU], [B_MU])
                OM = sb("m_OM", [128, NCH, 8], F32, es); FL = sb("m_FL", [128, NCH, 8], F32, es)
                TT("dve", OM[:], A_[:], MU[:], ALU.subtract, [B_A, B_MU], [B_MU])
                ACT(OM[:], OM[:], AF.Exp, [B_MU], [B_MU])
                TS("dve", OM[:], OM[:], 1.0 / 16.0, None, ALU.mult, None, [B_MU], [B_MU])
                TT("dve", FL[:], BH[:, :, 0:8], MU[:], ALU.add, [B_BH, B_MU], [B_MU])
                ACT(FL[:], FL[:], AF.Exp, [B_MU], [B_MU], scale=-1.0)
                B_gs = B_MU
                if l == 0:
                    dump("d_BH", BH[:], [B_BH]); dump("d_MU", MU[:], [B_MU]); dump("d_DEC", DEC[:], [B_MU])
                    dump("d_OM", OM[:], [B_MU]); dump("d_FL", FL[:], [B_MU]); dump("d_AR", AR[:], [B_AR]); dump("d_A", A_[:], [B_A])
                if stop == "m_gates":
                    return
                yield

                gb, bgb = load_bcast(es, "m_g", m_norm_g[l:l + 1, :], 1024)
                S = [sb(f"m_S{d_}", [128, 4, 2, 257], F32, es) for d_ in range(2)]
                B_S = [[Buf() for _ in range(4)] for _ in range(2)]
                S16 = Rot(nc, es, "m_S16", [128, 4, 2, 257], BF16, 3)
                ldq = Rot(nc, es, "m_ldq", [128, 4, 256], BF16, 2)
                ldk = Rot(nc, es, "m_ldk", [128, 4, 256], BF16, 2)
                ldo = Rot(nc, es, "m_ldo", [128, 4, 256], BF16, 2)
                ldv = Rot(nc, es, "m_ldv", [128, 4, 257], BF16, 2)
                for (t_, b_) in ldv.t:
                    kb.op("pool", lambda g, t_=t_: g.memset(t_[:, :, 256:257], 1.0), [], [b_])
                kwr = Rot(nc, es, "m_kw", [128, 4, 256], BF16, 2)
                qkT = Rot(nc, es, "m_qkT", [128, 8, 128], BF16, 4)
                sTr = Rot(nc, es, "m_sT", [128, 4, 128], BF16, 4)
                hfr = Rot(nc, es, "m_hf", [128, 256], F32, 3)
                smr = Rot(nc, es, "m_sm", [128, 8], F32, 4)
                jkr = Rot(nc, es, "m_jk", [128, 256], BF16, 2)
                yb = Rot(nc, es, "m_y", [128, 1024], BF16, 2)
                ytp = Rot(nc, es, "m_yt", [128, 8, 128], BF16, 2)

                def m_load(rot, src, c, w=256):
                    t_, b_ = rot.get()
                    kb.dma("sp", t_[:, :, 0:256], src[c * 128:(c + 1) * 128, :].rearrange("p (h d) -> p h d", h=4),
                           reads=[bsc(src.name, c)], writes=[b_])
                    return t_, b_

                def m_state_init(d_, first):
                    for h in range(4):
                        if first:
                            kb.dma("sp", S[d_][:, h, :, 0:256], st_mC[l, d_, h].rearrange("(dc p) e -> p dc e", p=128), writes=[B_S[d_][h]])
                            kb.dma("sp", S[d_][:, h, :, 256], st_mn[l, d_, h].rearrange("(dc p) -> p dc", p=128), writes=[B_S[d_][h]])
                        else:
                            TS("dve", S[d_][:, h], S[d_][:, h], carry, None, ALU.mult, None, [B_S[d_][h], B_flg], [B_S[d_][h]])

                def m_dec16(d_, c):
                    s16, bs16 = S16.get()
                    for h in range(4):
                        ACT(s16[:, h], S[d_][:, h], AF.Copy, [B_S[d_][h], B_gs], [bs16], scale=DEC[:, c, d_ * 4 + h:d_ * 4 + h + 1])
                    return s16, bs16

                def m_state_update(d_, c, kt, bk, vt, bv):
                    kw, bkw = kwr.get()
                    TT("pool", kw[:], kt[:], OM[:, c, d_ * 4:d_ * 4 + 4].unsqueeze(2).to_broadcast([128, 4, 256]), ALU.mult, [bk, B_gs], [bkw])
                    for h in range(4):
                        for dc in range(2):
                            ps, bp = psF.get()
                            MM(ps[:, 0:257], kw[:, h, dc * 128:(dc + 1) * 128], vt[:, h, :], True, True, [bkw, bv], [bp])
                            STT("dve", S[d_][:, h, dc, :], S[d_][:, h, dc, :], DEC[:, c, d_ * 4 + h:d_ * 4 + h + 1], ps[:, 0:257],
                                ALU.mult, ALU.add, [B_S[d_][h], B_gs, bp], [B_S[d_][h]])

                def m_state_out(d_, seg):
                    for h in range(4):
                        kb.dma("sp", o_mC[seg, l, d_, h].rearrange("(dc p) e -> p dc e", p=128), S[d_][:, h, :, 0:256],
                               reads=[B_S[d_][h]], writes=[Buf()])
                        kb.dma("sp", o_mn[seg, l, d_, h].rearrange("(dc p) -> p dc", p=128), S[d_][:, h, :, 256],
                               reads=[B_S[d_][h]], writes=[Buf()])

                for c in range(NCH - 1, -1, -1):
                    kt, bk = m_load(ldk, sc_mk, c)
                    vt, bv = m_load(ldv, sc_mv, c)
                    if c == NCH - 1:
                        m_state_init(1, True)
                    elif c % 2 == 1:
                        m_state_init(1, False)
                    s16, bs16 = m_dec16(1, c)
                    kb.dma("sp", sb_m[c].rearrange("h dc p e -> p h dc e"), s16[:], reads=[bs16], writes=[bsc("sb_m", c)])
                    m_state_update(1, c, kt, bk, vt, bv)
                    if c % 2 == 0:
                        m_state_out(1, c // 2)
                    yield
                for c in range(NCH):
                    qt, bq = m_load(ldq, sc_mq, c)
                    kt, bk = m_load(ldk, sc_mk, c)
                    vt, bv = m_load(ldv, sc_mv, c)
                    ot, bo_ = m_load(ldo, sc_mo, c)
                    sbk, bsbk = S16.get()
                    kb.dma("sp", sbk[:], sb_m[c].rearrange("h dc p e -> p h dc e"), reads=[bsc("sb_m", c)], writes=[bsbk])
                    if c == 0:
                        m_state_init(0, True)
                    elif c % 2 == 0:
                        m_state_init(0, False)
                    sfk, bsfk = m_dec16(0, c)
                    qT, bqT = qkT.get(); kT, bkT = qkT.get()
                    for (src, bsrc, dstT, bdst) in ((qt, bq, qT, bqT), (kt, bk, kT, bkT)):
                        pt, bpt = psT.get()
                        for k in range(8):
                            TR(pt[:, k * 128:(k + 1) * 128], src[:, k // 2, (k % 2) * 128:(k % 2 + 1) * 128], [bsrc], [bpt], sig=(k == 7))
                        CPrr(dstT[:], pt[:].rearrange("p (k t) -> p k t", k=8), [bpt], [bdst])
                    psS, bpS = psF.get()
                    for h in range(4):
                        for dc in range(2):
                            MM(psS[:, h * 128:(h + 1) * 128], kT[:, h * 2 + dc, :], qT[:, h * 2 + dc, :], dc == 0, dc == 1,
                               [bkT, bqT], [bpS], sig=(h == 3 and dc == 1))
                    sTf, bsTf = sTr.get(); sTb, bsTb = sTr.get()
                    for h in range(4):
                        STT("dve", sTf[:, h, :], psS[:, h * 128:(h + 1) * 128], OM[:, c, h:h + 1], maskF, ALU.mult, ALU.mult,
                            [bpS, B_gs, B_cst], [bsTf])
                        STT("dve", sTb[:, h, :], psS[:, h * 128:(h + 1) * 128], OM[:, c, 4 + h:5 + h], maskB, ALU.mult, ALU.mult,
                            [bpS, B_gs, B_cst], [bsTb])
                    y, by = yb.get()
                    for h in range(4):
                        hf, bhf = hfr.get()
                        for d_, (sT_, bsT_, s16_, bs16_) in enumerate(((sTf, bsTf, sfk, bsfk), (sTb, bsTb, sbk, bsbk))):
                            ps, bp = psF.get()
                            MM(ps[:, 0:257], sT_[:, h, :], vt[:, h, :], True, False, [bsT_, bv], [bp], sig=False)
                            for dc in range(2):
                                MM(ps[:, 0:257], qT[:, h * 2 + dc, :], s16_[:, h, dc, :], False, dc == 1, [bqT, bs16_], [bp])
                            sm, bsm = smr.get()
                            g_ = d_ * 4 + h
                            CP("dve", sm[:, 3:4], ps[:, 256:257], [bp], [bsm])
                            STT("dve", sm[:, 0:1], sm[:, 3:4], -1.0, sm[:, 3:4], ALU.mult, ALU.max, [bsm], [bsm])
                            TT("dve", sm[:, 0:1], sm[:, 0:1], FL[:, c, g_:g_ + 1], ALU.max, [bsm, B_gs], [bsm])
                            kb.op("dve", lambda g, sm=sm: g.reciprocal(out=sm[:, 1:2], in_=sm[:, 0:1]), [bsm], [bsm])
                            if d_ == 0:
                                ACT(hf[:], ps[:, 0:256], AF.Copy, [bp, bsm], [bhf], scale=sm[:, 1:2])
                            else:
                                STT("dve", hf[:], ps[:, 0:256], sm[:, 1:2], hf[:], ALU.mult, ALU.add, [bp, bsm, bhf], [bhf])
                        sm, bsm = smr.get()
                        jk, bjk = jkr.get()
                        ACT(jk[:], hf[:], AF.Square, [bhf], [bjk, bsm], accum=sm[:, 0:1])
                        RSTD(sm[:, 2:3], sm[:, 1:2], sm[:, 0:1], 1.0 / 256, bsm)
                        STT("dve", hf[:], hf[:], sm[:, 2:3], gb[:, h * 256:(h + 1) * 256], ALU.mult, ALU.mult, [bhf, bsm, bgb], [bhf])
                        TT("pool", y[:, h * 256:(h + 1) * 256], hf[:], ot[:, h, :], ALU.mult, [bhf, bo_], [by])
                    yT_store(es, y, by, 0, c, 8, ytp)
                    m_state_update(0, c, kt, bk, vt, bv)
                    if c % 2 == 1:
                        m_state_out(0, c // 2)
                    yield

            def gen_ret(es):
                RS = 128.0 ** -0.5
                lg = sb("r_lg", [128, 16], F32, es); B_lg = Buf()
                nlg = sb("r_nlg", [128, 16], F32, es)
                wkq = sb("r_wkq", [128, 3, 16], F32, es)
                pk = sb("r_pk", [128, 2, 16], F32, es); B_pk = Buf()
                kb.dma("sp", lg[:], r_decay[l:l + 1, :].partition_broadcast(128), writes=[B_lg])
                kb.dma("sp", pk[:, 0, :], posk, writes=[B_pk])
                kb.dma("sp", pk[:, 1, :], posq, writes=[B_pk])
                ACT(lg[:], lg[:], AF.Exp, [B_lg], [B_lg])
                TS("dve", nlg[:], lg[:], 1.0, None, ALU.mult, None, [B_lg], [B_lg])
                TS("dve", lg[:], lg[:], -1.0, None, ALU.mult, None, [B_lg], [B_lg])
                TT("dve", wkq[:, 0, :], pk[:, 0, :], lg[:], ALU.mult, [B_pk, B_lg], [B_lg])
                TT("dve", wkq[:, 1, :], pk[:, 1, :], lg[:], ALU.mult, [B_pk, B_lg], [B_lg])
                TS("dve", wkq[:, 2, :], lg[:], 128.0, None, ALU.mult, None, [B_lg], [B_lg])
                ACT(wkq[:], wkq[:], AF.Exp, [B_lg], [B_lg])
                TS("dve", wkq[:, 0, :], wkq[:, 0, :], RS, None, ALU.mult, None, [B_lg], [B_lg])
                wk_ = wkq[:, 0, :]; wq_ = wkq[:, 1, :]; cd_ = wkq[:, 2, :]
                DTm = sb("r_DT", [128, 8, 128], F32, es); B_DT = Buf()
                dtmp = sb("r_dtmp", [128, 128], F32, es); B_dtmp = Buf()
                for h in range(8):
                    ACT(DTm[:, h, :], rel, AF.Exp, [B_cst, B_lg], [B_DT], scale=lg[:, h:h + 1])
                    TT("dve", DTm[:, h, :], DTm[:, h, :], maskF, ALU.mult, [B_DT, B_cst], [B_DT])
                    ACT(dtmp[:], rel, AF.Exp, [B_cst, B_lg], [B_dtmp], scale=nlg[:, 8 + h:9 + h])
                    TT("dve", dtmp[:], dtmp[:], maskB, ALU.mult, [B_dtmp, B_cst], [B_dtmp])
                    STT("dve", DTm[:, h, :], DTm[:, h, :], 1.0, dtmp[:], ALU.mult, ALU.add, [B_DT, B_dtmp], [B_DT])
                TS("dve", DTm[:], DTm[:], RS, None, ALU.mult, None, [B_DT], [B_DT])
                gb, bgb = load_bcast(es, "r_g", r_norm_g[l:l + 1, :], 1024)
                S = [sb(f"r_S{d_}", [128, 8, 128], F32, es) for d_ in range(2)]
                B_S = [Buf(), Buf()]
                S16 = Rot(nc, es, "r_S16", [128, 8, 128], BF16, 3)
                ld = {n: Rot(nc, es, "r_ld" + n, [128, 8, 128], BF16, 2) for n in ("q", "k", "v", "g")}
                kwr = Rot(nc, es, "r_kw", [128, 8, 128], BF16, 2)
                qkT = Rot(nc, es, "r_qkT", [128, 8, 128], BF16, 4)
                sTm = Rot(nc, es, "r_sTm", [128, 8, 128], BF16, 2)
                t32 = Rot(nc, es, "r_t32", [128, 4, 128], F32, 4)
                o32 = Rot(nc, es, "r_o32", [128, 8, 128], F32, 2)
                ssr = Rot(nc, es, "r_ss", [128, 24], F32, 2)
                yb = Rot(nc, es, "r_y", [128, 1024], BF16, 2)
                ytp = Rot(nc, es, "r_yt", [128, 8, 128], BF16, 2)

                def r_load(name, src, c):
                    t_, b_ = ld[name].get()
                    kb.dma("sp", t_[:], src[c * 128:(c + 1) * 128, :].rearrange("p (h d) -> p h d", h=8),
                           reads=[bsc(src.name, c)], writes=[b_])
                    return t_, b_

                def r_state_update(d_, kt, bk, vt, bv):
                    kw, bkw = kwr.get()
                    TT("dve", kw[:], kt[:], wk_[:, d_ * 8:(d_ + 1) * 8].to_broadcast([128, 8, 128]) if False else
                       wkq[:, 0, d_ * 8:(d_ + 1) * 8].unsqueeze(2).to_broadcast([128, 8, 128]), ALU.mult, [bk, B_lg], [bkw])
                    TT("pool", S[d_][:], S[d_][:], wkq[:, 2, d_ * 8:(d_ + 1) * 8].unsqueeze(2).to_broadcast([128, 8, 128]),
                       ALU.mult, [B_S[d_], B_lg], [B_S[d_]])
                    for hh in range(2):
                        ps, bp = psF.get()
                        for h4 in range(4):
                            h = hh * 4 + h4
                            MM(ps[:, h4 * 128:(h4 + 1) * 128], kw[:, h, :], vt[:, h, :], True, True, [bkw, bv], [bp], sig=(h4 == 3))
                        TT("dve", S[d_][:, hh * 4:(hh + 1) * 4, :], S[d_][:, hh * 4:(hh + 1) * 4, :],
                           ps[:].rearrange("p (h e) -> p h e", h=4), ALU.add, [B_S[d_], bp], [B_S[d_]])

                def r_state_init(d_, c, first):
                    if first:
                        kb.dma("sp", S[d_][:], st_r[l, d_].rearrange("h d e -> d h e"), writes=[B_S[d_]])
                    else:
                        TS("dve", S[d_][:], S[d_][:], carry, None, ALU.mult, None, [B_S[d_], B_flg], [B_S[d_]])

                def r_state_out(d_, seg):
                    kb.dma("sp", o_r[seg, l, d_].rearrange("h d e -> d h e"), S[d_][:], reads=[B_S[d_]], writes=[Buf()])

                yield
                for c in range(NCH - 1, -1, -1):
                    kt, bk = r_load("k", sc_rk, c)
                    vt, bv = r_load("v", sc_rv, c)
                    if c == NCH - 1:
                        r_state_init(1, c, True)
                    elif c % 2 == 1:
                        r_state_init(1, c, False)
                    s16, bs16 = S16.get()
                    CP("act", s16[:], S[1][:], [B_S[1]], [bs16])
                    kb.dma("sp", sb_r[c].rearrange("h d e -> d h e"), s16[:], reads=[bs16], writes=[bsc("sb_r", c)])
                    r_state_update(1, kt, bk, vt, bv)
                    if c % 2 == 0:
                        r_state_out(1, c // 2)
                    yield
                for c in range(NCH):
                    qt, bq = r_load("q", sc_rq, c)
                    kt, bk = r_load("k", sc_rk, c)
                    vt, bv = r_load("v", sc_rv, c)
                    gt, bg = r_load("g", sc_rg, c)
                    sbk, bsbk = S16.get()
                    kb.dma("sp", sbk[:], sb_r[c].rearrange("h d e -> d h e"), reads=[bsc("sb_r", c)], writes=[bsbk])
                    if c == 0:
                        r_state_init(0, c, True)
                    elif c % 2 == 0:
                        r_state_init(0, c, False)
                    sfk, bsfk = S16.get()
                    CP("act", sfk[:], S[0][:], [B_S[0]], [bsfk])
                    qT, bqT = qkT.get(); kT, bkT = qkT.get()
                    for (src, bsrc, dstT, bdst) in ((qt, bq, qT, bqT), (kt, bk, kT, bkT)):
                        pt, bpt = psT.get()
                        for h in range(8):
                            TR(pt[:, h * 128:(h + 1) * 128], src[:, h, :], [bsrc], [bpt], sig=(h == 7))
                        CPrr(dstT[:], pt[:].rearrange("p (h t) -> p h t", h=8), [bpt], [bdst])
                    stm, bstm = sTm.get()
                    for hh in range(2):
                        ps, bp = psF.get()
                        for h4 in range(4):
                            h = hh * 4 + h4
                            MM(ps[:, h4 * 128:(h4 + 1) * 128], kT[:, h, :], qT[:, h, :], True, True, [bkT, bqT], [bp], sig=(h4 == 3))
                        TT("dve", stm[:, hh * 4:(hh + 1) * 4, :], ps[:].rearrange("p (h e) -> p h e", h=4),
                           DTm[:, hh * 4:(hh + 1) * 4, :], ALU.mult, [bp, B_DT], [bstm])
                    o, bo = o32.get()
                    for hh in range(2):
                        psI, bpI = psF.get(); psA, bpA = psF.get(); psB, bpB = psF.get()
                        for h4 in range(4):
                            h = hh * 4 + h4
                            sl = slice(h4 * 128, (h4 + 1) * 128)
                            MM(psI[:, sl], stm[:, h, :], vt[:, h, :], True, True, [bstm, bv], [bpI], sig=(h4 == 3))
                            MM(psA[:, sl], qT[:, h, :], sfk[:, h, :], True, True, [bqT, bsfk], [bpA], sig=(h4 == 3))
                            MM(psB[:, sl], qT[:, h, :], sbk[:, h, :], True, True, [bqT, bsbk], [bpB], sig=(h4 == 3))
                        t1, bt1 = t32.get(); t2, bt2 = t32.get()
                        r4 = lambda p_: p_[:].rearrange("p (h e) -> p h e", h=4)
                        TT("dve", t1[:], r4(psA), wkq[:, 1, hh * 4:hh * 4 + 4].unsqueeze(2).to_broadcast([128, 4, 128]),
                           ALU.mult, [bpA, B_lg], [bt1])
                        TT("dve", t2[:], r4(psB), wkq[:, 1, 8 + hh * 4:8 + hh * 4 + 4].unsqueeze(2).to_broadcast([128, 4, 128]),
                           ALU.mult, [bpB, B_lg], [bt2])
                        TT("pool", t1[:], t1[:], t2[:], ALU.add, [bt1, bt2], [bt1])
                        TT("dve", o[:, hh * 4:(hh + 1) * 4, :], r4(psI), t1[:], ALU.add, [bpI, bt1], [bo])
                    sq, bsq = o32.get()
                    TT("pool", sq[:], o[:], o[:], ALU.mult, [bo], [bsq])
                    ss, bss = ssr.get()
                    kb.op("dve", lambda g: g.tensor_reduce(out=ss[:, 0:8], in_=sq[:], axis=AX.X, op=ALU.add), [bsq], [bss])
                    RSTD(ss[:, 16:24], ss[:, 8:16], ss[:, 0:8], 1.0 / 128, bss)
                    TT("dve", o[:], o[:], ss[:, 16:24].unsqueeze(2).to_broadcast([128, 8, 128]), ALU.mult, [bo, bss], [bo])
                    TT("pool", o[:], o[:], gb[:].rearrange("p (h e) -> p h e", h=8), ALU.mult, [bo, bgb], [bo])
                    y, by = yb.get()
                    TT("dve", y[:].rearrange("p (h e) -> p h e", h=8), o[:], gt[:], ALU.mult, [bo, bg], [by])
                    yT_store(es, y, by, 1, c, 8, ytp)
                    r_state_update(0, kt, bk, vt, bv)
                    if c % 2 == 1:
                        r_state_out(0, c // 2)
                    yield

            if stop != "inproj":
                with Scope(kb) as es:
                    gens = []
                    if not (dbg and dbg.get("skip_mlstm")):
                        gens.append(gen_mlstm(es))
                    if not (dbg and dbg.get("skip_ret")):
                        gens.append(gen_ret(es))
                    while gens:
                        for g_ in list(gens):
                            try:
                                next(g_)
                            except StopIteration:
                                gens.remove(g_)
            if stop in ("r_full", "m_full", "m_gates"):
                break
            def gen_ssd(es):
                DTt = sb("s_DT", [128, NCH, 32], F32, es); B_DTt = Buf()
                for c_ in range(NCH):
                    kb.dma("sp", DTt[:, c_, :], sc_sdt[c_ * 128:(c_ + 1) * 128, :], reads=[bsc("sc_sdt", c_)], writes=[B_DTt])
                sbias = sb("s_bias", [128, 32], F32, es); B_sb = Buf()
                Arep = sb("s_A", [128, 32], F32, es)
                Dbc = sb("s_D", [128, 16], F32, es)
                kb.dma("sp", sbias[:], s_dt_bias[l:l + 1, :].partition_broadcast(128), writes=[B_sb])
                kb.dma("sp", Arep[:], s_a_log[l:l + 1, :].partition_broadcast(128), writes=[B_sb])
                kb.dma("sp", Dbc[:], s_d[l:l + 1, :].partition_broadcast(128), writes=[B_sb])
                ACT(Arep[:], Arep[:], AF.Exp, [B_sb], [B_sb])
                TS("dve", Arep[:], Arep[:], -1.0, None, ALU.mult, None, [B_sb], [B_sb])
                TT("dve", DTt[:], DTt[:], sbias[:].unsqueeze(1).to_broadcast([128, NCH, 32]), ALU.add, [B_DTt, B_sb], [B_DTt])
                ACT(DTt[:], DTt[:], AF.Exp, [B_DTt], [B_DTt])
                ACT(DTt[:], DTt[:], AF.Ln, [B_DTt], [B_DTt], bias=1.0)
                dA = sb("s_dA", [128, NCH, 32], F32, es); B_dA = Buf()
                TT("dve", dA[:], DTt[:], Arep[:].unsqueeze(1).to_broadcast([128, NCH, 32]), ALU.mult, [B_DTt, B_sb], [B_dA])
                CS = sb("s_CS", [128, NCH, 64], F32, es); B_CS = Buf()
                for hf in range(2):
                    ps, bp = psF.get()
                    for cc in range(8):
                        c = hf * 8 + cc
                        MM(ps[:, cc * 64:cc * 64 + 16], maskF, dA[:, c, 0:16], True, True, [B_cst, B_dA], [bp], sig=False)
                        MM(ps[:, cc * 64 + 16:cc * 64 + 32], maskB, dA[:, c, 16:32], True, True, [B_cst, B_dA], [bp], sig=False)
                        MM(ps[:, cc * 64 + 32:cc * 64 + 64], ones_f, dA[:, c, :], True, True, [B_cst, B_dA], [bp], sig=(cc == 7))
                    CP("dve", CS[:, hf * 8:(hf + 1) * 8, :], ps[:].rearrange("p (c g) -> p c g", c=8), [bp], [B_CS])
                ECS = sb("s_ECS", [128, NCH, 32], F32, es); WEND = sb("s_WEND", [128, NCH, 32], F32, es)
                ETOT = sb("s_ETOT", [128, NCH, 32], F32, es); B_E = Buf()
                ACT(ECS[:], CS[:, :, 0:32], AF.Exp, [B_CS], [B_E])
                TT("dve", WEND[:], CS[:, :, 32:64], CS[:, :, 0:32], ALU.subtract, [B_CS], [B_E])
                ACT(WEND[:], WEND[:], AF.Exp, [B_E], [B_E])
                ACT(ETOT[:], CS[:, :, 32:64], AF.Exp, [B_CS], [B_E])
                nm4 = sb("s_nm4", [128, 2, 4, 128], F32, es); B_nm4 = Buf()
                CP("dve", nm4[:, 0], negF.unsqueeze(1).to_broadcast([128, 4, 128]), [B_cst], [B_nm4])
                CP("dve", nm4[:, 1], negB.unsqueeze(1).to_broadcast([128, 4, 128]), [B_cst], [B_nm4])
                gb, bgb = load_bcast(es, "s_g", s_norm_g[l:l + 1, :], 1024)
                H = [sb(f"s_H{d_}", [128, 16, 64], F32, es) for d_ in range(2)]
                B_H = [Buf(), Buf()]
                H16 = Rot(nc, es, "s_H16", [128, 1024], BF16, 3)
                xTr = Rot(nc, es, "s_xT", [128, 8, 128], BF16, 2)
                xtk = Rot(nc, es, "s_xtok", [128, 16, 64], BF16, 2)
                bcr = Rot(nc, es, "s_BC", [128, 4, 128], BF16, 2)
                btk = Rot(nc, es, "s_Btok", [128, 2, 128], BF16, 2)
                zr = Rot(nc, es, "s_z", [128, 1024], BF16, 2)
                xdr = Rot(nc, es, "s_xdt", [128, 16, 64], BF16, 4)
                xwr = Rot(nc, es, "s_xw", [128, 16, 64], BF16, 2)
                dgr = Rot(nc, es, "s_dg", [128, 8, 128], F32, 2)
                rrr = Rot(nc, es, "s_rr", [128, 8, 128], F32, 2)
                dcr = Rot(nc, es, "s_dc", [128, 8, 128], F32, 2)
                Gr = Rot(nc, es, "s_G", [128, 16, 128], BF16, 3)
                cbr = Rot(nc, es, "s_cb", [128, 2, 128], F32, 2)
                t32 = Rot(nc, es, "s_t32", [128, 16, 64], F32, 3)
                yar = Rot(nc, es, "s_yacc", [128, 16, 64], F32, 2)
                ssr = Rot(nc, es, "s_ss", [128, 4], F32, 2)
                jkr = Rot(nc, es, "s_jk", [128, 1024], BF16, 1)
                yb = Rot(nc, es, "s_y", [128, 1024], BF16, 2)
                ytp = Rot(nc, es, "s_yt", [128, 8, 128], BF16, 2)
                stg = Rot(nc, es, "s_stg", [128, 8, 128], F32, 2)

                def s_load_x(c):
                    xT_, bxT = xTr.get()
                    kb.dma("sp", xT_[:], sc_sxT.rearrange("b p t -> p b t")[:, :, c * 128:(c + 1) * 128], reads=[bsc("sxT", 0)], writes=[bxT])
                    pt, bpt = psT.get()
                    for k in range(8):
                        TR(pt[:, k * 128:(k + 1) * 128], xT_[:, k, :], [bxT], [bpt], sig=(k == 7))
                    xt_, bxt = xtk.get()
                    CPrr(xt_[:].rearrange("p h d -> p (h d)"), pt[:], [bpt], [bxt])
                    return xt_, bxt

                def s_load_bc(c):
                    bc, bbc = bcr.get()
                    kb.dma("sp", bc[:, 0:2, :], sc_sBT.rearrange("g p t -> p g t")[:, :, c * 128:(c + 1) * 128], reads=[bsc("sBT", 0)], writes=[bbc])
                    kb.dma("sp", bc[:, 2:4, :], sc_sCT.rearrange("g p t -> p g t")[:, :, c * 128:(c + 1) * 128], reads=[bsc("sCT", 0)], writes=[bbc])
                    pt, bpt = psT.get()
                    for g in range(2):
                        TR(pt[:, g * 128:(g + 1) * 128], bc[:, g, :], [bbc], [bpt], sig=(g == 1))
                    bt_, bbt = btk.get()
                    CPrr(bt_[:].rearrange("p g s -> p (g s)"), pt[:, 0:256], [bpt], [bbt])
                    return bc, bbc, bt_, bbt

                def s_xdt(d_, c, xt_, bxt):
                    xd, bxd = xdr.get()
                    TT("pool", xd[:], xt_[:], DTt[:, c, d_ * 16:(d_ + 1) * 16].unsqueeze(2).to_broadcast([128, 16, 64]), ALU.mult, [bxt, B_DTt], [bxd])
                    return xd, bxd

                def s_state_init(d_, first):
                    if first:
                        st_, bst = stg.get()
                        kb.dma("sp", st_[:], st_s[l, d_].rearrange("h p s -> (h p) s").rearrange("(b r) s -> r b s", r=128), writes=[bst])
                        for hf in range(2):
                            ps, bp = psF.get()
                            for b4 in range(4):
                                b = hf * 4 + b4
                                MM(ps[:, b4 * 128:(b4 + 1) * 128], st_[:, b, :], ident_f, True, True, [bst, B_cst], [bp], sig=(b4 == 3))
                            CP("dve", H[d_][:, hf * 8:(hf + 1) * 8, :].rearrange("p h d -> p (h d)"), ps[:], [bp], [B_H[d_]])
                    else:
                        TS("dve", H[d_][:], H[d_][:], carry, None, ALU.mult, None, [B_H[d_], B_flg], [B_H[d_]])

                def s_state_out(d_, seg):
                    st_, bst = stg.get()
                    Hf = H[d_][:].rearrange("p h d -> p (h d)")
                    for hf in range(2):
                        ps, bp = psF.get()
                        for b4 in range(4):
                            b = hf * 4 + b4
                            MM(ps[:, b4 * 128:(b4 + 1) * 128], Hf[:, b * 128:(b + 1) * 128], ident_f, True, True, [B_H[d_], B_cst], [bp], sig=(b4 == 3))
                        CP("dve", st_[:, hf * 4:(hf + 1) * 4, :], ps[:].rearrange("p (b s) -> p b s", b=4), [bp], [bst])
                    kb.dma("sp", o_s[seg, l, d_].rearrange("h p s -> (h p) s").rearrange("(b r) s -> r b s", r=128), st_[:], reads=[bst], writes=[Buf()])

                def s_state_update(d_, c, xd, bxd, bt_, bbt):
                    xw, bxw = xwr.get()
                    TT("pool", xw[:], xd[:], WEND[:, c, d_ * 16:(d_ + 1) * 16].unsqueeze(2).to_broadcast([128, 16, 64]), ALU.mult, [bxd, B_E], [bxw])
                    TT("dve", H[d_][:], H[d_][:], ETOT[:, c, d_ * 16:(d_ + 1) * 16].unsqueeze(2).to_broadcast([128, 16, 64]), ALU.mult,
                       [B_H[d_], B_E], [B_H[d_]])
                    for g in range(2):
                        ps, bp = psF.get()
                        MM(ps[:], bt_[:, g, :], xw[:, g * 8:(g + 1) * 8, :].rearrange("p h d -> p (h d)"), True, True, [bbt, bxw], [bp])
                        TT("dve", H[d_][:, g * 8:(g + 1) * 8, :].rearrange("p h d -> p (h d)"),
                           H[d_][:, g * 8:(g + 1) * 8, :].rearrange("p h d -> p (h d)"), ps[:], ALU.add, [B_H[d_], bp], [B_H[d_]])

                for c in range(NCH - 1, -1, -1):
                    xt_, bxt = s_load_x(c)
                    bc, bbc, bt_, bbt = s_load_bc(c)
                    if c == NCH - 1:
                        s_state_init(1, True)
                    elif c % 2 == 1:
                        s_state_init(1, False)
                    h16, bh16 = H16.get()
                    CP("act", h16[:], H[1][:].rearrange("p h d -> p (h d)"), [B_H[1]], [bh16])
                    kb.dma("sp", sb_s[c].rearrange("g s n -> s g n"), h16[:].rearrange("p (g n) -> p g n", g=2), reads=[bh16], writes=[bsc("sb_s", c)])
                    xd, bxd = s_xdt(1, c, xt_, bxt)
                    s_state_update(1, c, xd, bxd, bt_, bbt)
                    if c % 2 == 0:
                        s_state_out(1, c // 2)
                    yield
                for c in range(NCH):
                    xt_, bxt = s_load_x(c)
                    bc, bbc, bt_, bbt = s_load_bc(c)
                    z_, bz = zr.get()
                    kb.dma("sp", z_[:], sc_sz[c * 128:(c + 1) * 128, :], reads=[bsc("sc_sz", c)], writes=[bz])
                    hb16, bhb16 = H16.get()
                    kb.dma("sp", hb16[:].rearrange("p (g n) -> p g n", g=2), sb_s[c].rearrange("g s n -> s g n"), reads=[bsc("sb_s", c)], writes=[bhb16])
                    if c == 0:
                        s_state_init(0, True)
                    elif c % 2 == 0:
                        s_state_init(0, False)
                    hf16, bhf16 = H16.get()
                    CP("act", hf16[:], H[0][:].rearrange("p h d -> p (h d)"), [B_H[0]], [bhf16])
                    psc, bpc = psF.get()
                    for g in range(2):
                        MM(psc[:, g * 128:(g + 1) * 128], bc[:, g, :], bc[:, 2 + g, :], True, True, [bbc], [bpc], sig=(g == 1))
                    cb, bcb = cbr.get()
                    CP("dve", cb[:].rearrange("p g i -> p (g i)"), psc[:, 0:256], [bpc], [bcb])
                    xds = []; Gs = []
                    for d_ in range(2):
                        xds.append(s_xdt(d_, c, xt_, bxt))
                        G_, bG = Gr.get()
                        st = []
                        for g in range(2):
                            cs_ = CS[:, c, d_ * 16 + g * 8:d_ * 16 + (g + 1) * 8]
                            dg, bdg = dgr.get(); rr, brr = rrr.get()
                            TT("dve", dg[:], cst[:, 0:1, :].to_broadcast([128, 8, 128]), cs_.unsqueeze(2).to_broadcast([128, 8, 128]),
                               ALU.mult, [B_cst, B_CS], [bdg])
                            CP("pool", rr[:], cs_.unsqueeze(2).to_broadcast([128, 8, 128]), [B_CS], [brr])
                            dc, bdc = dcr.get()
                            st.append((dg, bdg, rr, brr, dc, bdc))
                        pss_ = []
                        for g in range(2):
                            dg, bdg, rr, brr, dc, bdc = st[g]
                            for hf in range(2):
                                ps, bp = psF.get()
                                MM(ps[:], ones_f, dg[:, hf * 4:(hf + 1) * 4, :].rearrange("p h i -> p (h i)"), True, False, [B_cst, bdg], [bp], sig=False)
                                MM(ps[:], cst[:, 7, :], rr[:, hf * 4:(hf + 1) * 4, :].rearrange("p h i -> p (h i)"), False, True, [B_cst, brr], [bp])
                                pss_.append((ps, bp))
                        for g in range(2):
                            dc, bdc = st[g][4], st[g][5]
                            for hf in range(2):
                                ps, bp = pss_[g * 2 + hf]
                                dch = dc[:, hf * 4:(hf + 1) * 4, :].rearrange("p h i -> p (h i)")
                                STT("dve", dch, ps[:], 0.0, nm4[:, d_].rearrange("p h i -> p (h i)"), ALU.min, ALU.add, [bp, B_nm4], [bdc])
                        for g in range(2):
                            dc, bdc = st[g][4], st[g][5]
                            ACT(dc[:], dc[:], AF.Exp, [bdc], [bdc])
                        for g in range(2):
                            dc, bdc = st[g][4], st[g][5]
                            TT("dve", G_[:, g * 8:(g + 1) * 8, :], dc[:], cb[:, g:g + 1, :].to_broadcast([128, 8, 128]), ALU.mult, [bdc, bcb], [bG])
                        Gs.append((G_, bG))
                    yacc, bya = yar.get()
                    for g in range(2):
                        psI, bpI = psF.get(); psA, bpA = psF.get(); psB, bpB = psF.get()
                        for h8 in range(8):
                            h = g * 8 + h8
                            MM(psI[:, h8 * 64:(h8 + 1) * 64], Gs[0][0][:, h, :], xds[0][0][:, h, :], True, False, [Gs[0][1], xds[0][1]], [bpI], sig=False)
                            MM(psI[:, h8 * 64:(h8 + 1) * 64], Gs[1][0][:, h, :], xds[1][0][:, h, :], False, True, [Gs[1][1], xds[1][1]], [bpI], sig=(h8 == 7))
                        MM(psA[:], bc[:, 2 + g, :], hf16[:, g * 512:(g + 1) * 512], True, True, [bbc, bhf16], [bpA])
                        MM(psB[:], bc[:, 2 + g, :], hb16[:, g * 512:(g + 1) * 512], True, True, [bbc, bhb16], [bpB])
                        t1, bt1 = t32.get(); t2, bt2 = t32.get()
                        r8 = lambda p_: p_[:].rearrange("p (h d) -> p h d", h=8)
                        TT("dve", t1[:, 0:8, :], r8(psA), ECS[:, c, g * 8:(g + 1) * 8].unsqueeze(2).to_broadcast([128, 8, 64]), ALU.mult, [bpA, B_E], [bt1])
                        TT("dve", t2[:, 0:8, :], r8(psB), ECS[:, c, 16 + g * 8:16 + (g + 1) * 8].unsqueeze(2).to_broadcast([128, 8, 64]), ALU.mult, [bpB, B_E], [bt2])
                        TT("pool", t1[:, 0:8, :], t1[:, 0:8, :], t2[:, 0:8, :], ALU.add, [bt1, bt2], [bt1])
                        TT("dve", yacc[:, g * 8:(g + 1) * 8, :], r8(psI), t1[:, 0:8, :], ALU.add, [bpI, bt1], [bya])
                    xD, bxD = t32.get()
                    TT("pool", xD[:], xt_[:], Dbc[:].unsqueeze(2).to_broadcast([128, 16, 64]), ALU.mult, [bxt, B_sb], [bxD])
                    TT("pool", yacc[:], yacc[:], xD[:], ALU.add, [bya, bxD], [bya])
                    if l == 0 and c == 0:
                        dump("d_yacc", yacc[:], [bya]); dump("d_xtok", xt_[:], [bxt])
                    yf = yacc[:].rearrange("p h d -> p (h d)")
                    TT("dve", yf, yf, z_[:], ALU.mult, [bya, bz], [bya])
                    ss, bss = ssr.get(); jk, bjk = jkr.get()
                    ACT(jk[:], yf, AF.Square, [bya], [bjk, bss], accum=ss[:, 0:1])
                    RSTD(ss[:, 2:3], ss[:, 1:2], ss[:, 0:1], 1.0 / 1024, bss)
                    y, by = yb.get()
                    STT("dve", y[:], yf, ss[:, 2:3], gb[:], ALU.mult, ALU.mult, [bya, bss, bgb], [by])
                    yT_store(es, y, by, 2, c, 8, ytp)
                    s_state_update(0, c, xds[0][0], xds[0][1], bt_, bbt)
                    if c % 2 == 1:
                        s_state_out(0, c // 2)
                    yield
            if not (dbg and dbg.get("skip_ssd")):
                with Scope(kb) as es:
                    gens = [gen_ssd(es)]
                    if l + 1 < DEPTH:
                        gens.append(gen_mod(es, l + 1, 2))
                    while gens:
                        for g_ in list(gens):
                            try:
                                next(g_)
                            except StopIteration:
                                gens.remove(g_)
            if stop == "s_full":
                break
            if stop in ("inproj", "mix"):
                break
            with Scope(kb) as es:
                mT = sb("mT", [128, KC, T], BF16, es)
                B_mT = [Buf(f"mT{t_}") for t_ in range(4)]
                with Scope(kb) as es1:
                    yT = []
                    for b in range(3):
                        t_ = sb(f"yT{b}", [128, 8, T], BF16, es1); bb = Buf()
                        for hf in range(2):
                            kb.dma("sp", t_[:, hf * 4:(hf + 1) * 4, :],
                                   sc_yT[b].rearrange("(k p) t -> p k t", p=128)[:, hf * 4:(hf + 1) * 4, :],
                                   reads=[bsc(f"yT{b}", 0)], writes=[bb])
                        yT.append((t_, bb))
                    wbr = Rot(nc, es1, "wbr", [128, 8, 512], BF16, 3)
                    gt_ = Rot(nc, es1, "gtile", [128, 512], BF16, 3)
                    acc = Rot(nc, es1, "macc", [128, 512], F32, 3)
                    tm = Rot(nc, es1, "mtmp", [128, 512], F32, 3)
                    for cb4 in range(4):
                        wts = []
                        for b in range(3):
                            wt, bw = wbr.get()
                            for k4 in range(2):
                                kb.dma("pool", wt[:, k4 * 4:(k4 + 1) * 4, :], w_br[b][l].rearrange("(k p) n -> p k n", p=128)[:, k4 * 4:(k4 + 1) * 4, cb4 * 512:(cb4 + 1) * 512],
                                       writes=[bw])
                            wts.append((wt, bw))
                        for j in range(4):
                            kc_ = cb4 * 4 + j
                            for tt in range(4):
                                a, ba = acc.get()
                                for b in range(3):
                                    g_, bg_ = gt_.get()
                                    kb.dma("sp", g_[:], sc_gT[b, kc_ * 128:(kc_ + 1) * 128, tt * 512:(tt + 1) * 512],
                                           reads=[bsc("gT", 0)], writes=[bg_])
                                    ps, bp = psF.get()
                                    wt, bw = wts[b]
                                    for k in range(8):
                                        MM(ps[:], wt[:, k, j * 128:(j + 1) * 128], yT[b][0][:, k, tt * 512:(tt + 1) * 512],
                                           k == 0, k == 7, [bw, yT[b][1]], [bp])
                                    if b == 0:
                                        TT("dve", a[:], ps[:], g_[:], ALU.mult, [bp, bg_], [ba])
                                    else:
                                        t_, bt_ = tm.get()
                                        TT("dve", t_[:], ps[:], g_[:], ALU.mult, [bp, bg_], [bt_])
                                        if b == 1:
                                            TT("dve", a[:], a[:], t_[:], ALU.add, [ba, bt_], [ba])
                                        else:
                                            TT("dve", mT[:, kc_, tt * 512:(tt + 1) * 512], a[:], t_[:], ALU.add, [ba, bt_], [B_mT[tt]])
                with Scope(kb) as es1:
                    ga, bga = load_bcast(es1, "ga", modrow[l, 2:3, :], D)
                    wo = Rot(nc, es1, "wo", [128, KC, 512], BF16, 2)
                    xin = Rot(nc, es1, "xin", [128, 512], F32, 3)
                    xo = Rot(nc, es1, "xo", [128, 512], F32, 3)
                    for n4 in range(4):
                        wt, bw = wo.get()
                        for k4 in range(4):
                            kb.dma("pool", wt[:, k4 * 4:(k4 + 1) * 4, :], w_out[l].rearrange("(k p) n -> p k n", p=128)[:, k4 * 4:(k4 + 1) * 4, n4 * 512:(n4 + 1) * 512], writes=[bw])
                        for c in range(NCH):
                            xi, bxi = xin.get()
                            src = x_in if l == 0 else xres
                            kb.dma("sp", xi[:], src[c * 128:(c + 1) * 128, n4 * 512:(n4 + 1) * 512],
                                   reads=([] if l == 0 else [B_xres[c]]), writes=[bxi])
                            ps, bp = psF.get()
                            for k in range(KC):
                                MM(ps[:], mT[:, k, c * 128:(c + 1) * 128], wt[:, k, :], k == 0, k == KC - 1, [B_mT[c // 4], bw], [bp])
                            x2, bx2 = xo.get()
                            TT("dve", x2[:], ps[:], ga[:, n4 * 512:(n4 + 1) * 512], ALU.mult, [bp, bga], [bx2])
                            TT("dve", x2[:], x2[:], xi[:], ALU.add, [bx2, bxi], [bx2])
                            kb.dma("sp", xres[c * 128:(c + 1) * 128, n4 * 512:(n4 + 1) * 512], x2[:], reads=[bx2], writes=[B_xres[c]])
            if stop == "merge":
                break
            with Scope(kb) as es:
                h2T = sb("h2T", [128, KC, T], BF16, es)
                B_h2T = [Buf(f"h2T{c}") for c in range(NCH)]
                with Scope(kb) as es1:
                    gm, bgm = load_bcast(es1, "gm2", modrow[l, 3:4, :], D)
                    sh, bsh = load_bcast(es1, "sh2", modrow[l, 4:5, :], D)
                    wk = mk_norm_wk(es1)
                    xr = Rot(nc, es1, "xr2", [128, D], F32, 2)
                    for c in range(NCH):
                        xt, bx = xr.get()
                        kb.dma("sp", xt[:], xres[c * 128:(c + 1) * 128, :], reads=[B_xres[c]], writes=[bx])
                        norm_to_hT(xt, bx, gm, bgm, sh, bsh, h2T, B_h2T[c], c, wk)
                with Scope(kb) as es1:
                    gf, bgf = load_bcast(es1, "gf", modrow[l, 5:6, :], D)
                    uT = sb("uT", [128, KC, T], BF16, es1)
                    B_uT = [Buf(f"uT{t_}") for t_ in range(4)]
                    wf = Rot(nc, es1, "wf", [128, KC, 512], BF16, 3)
                    rl = Rot(nc, es1, "rl", [128, 512], F32, 2)
                    xin = Rot(nc, es1, "xin2", [128, 512], F32, 3)
                    xo = Rot(nc, es1, "xo2", [128, 512], F32, 3)
                    w1v = w_ff1[l].rearrange("(k p) n -> p k n", p=128)
                    w2v = w_ff2[l].rearrange("(q k p) n -> q p k n", p=128, k=16)
                    for hb in range(4):
                        for fb in range(4):
                            wt, bw = wf.get()
                            col0 = (hb * 4 + fb) * 512
                            for k4 in range(4):
                                kb.dma("pool", wt[:, k4 * 4:(k4 + 1) * 4, :], w1v[:, k4 * 4:(k4 + 1) * 4, col0:col0 + 512], writes=[bw])
                            for j in range(4):
                                for tt in range(4):
                                    ps, bp = psF.get()
                                    for k in range(KC):
                                        MM(ps[:], wt[:, k, j * 128:(j + 1) * 128], h2T[:, k, tt * 512:(tt + 1) * 512],
                                           k == 0, k == KC - 1, B_h2T[tt * 4:(tt + 1) * 4] + [bw], [bp])
                                    r_, br_ = rl.get()
                                    ACT(r_[:], ps[:], AF.Relu, [bp], [br_])
                                    TT("dve", uT[:, fb * 4 + j, tt * 512:(tt + 1) * 512], r_[:], r_[:], ALU.mult, [br_], [B_uT[tt]])
                        for n4 in range(4):
                            wt, bw = wf.get()
                            for k4 in range(4):
                                kb.dma("pool", wt[:, k4 * 4:(k4 + 1) * 4, :], w2v[hb][:, k4 * 4:(k4 + 1) * 4, n4 * 512:(n4 + 1) * 512], writes=[bw])
                            for c in range(NCH):
                                rs_, cs_ = slice(c * 128, (c + 1) * 128), slice(n4 * 512, (n4 + 1) * 512)
                                if hb > 0:
                                    xi, bxi = xin.get()
                                    kb.dma("sp", xi[:], ffacc[rs_, cs_], reads=[B_ffacc[c]], writes=[bxi])
                                if hb == 3:
                                    xr_, bxr_ = xin.get()
                                    kb.dma("sp", xr_[:], xres[rs_, cs_], reads=[B_xres[c]], writes=[bxr_])
                                ps, bp = psF.get()
                                for k in range(KC):
                                    MM(ps[:], uT[:, k, c * 128:(c + 1) * 128], wt[:, k, :], k == 0, k == KC - 1, [B_uT[c // 4], bw], [bp])
                                x2, bx2 = xo.get()
                                if hb == 0:
                                    CPrr(x2[:], ps[:], [bp], [bx2])
                                else:
                                    TT("dve", x2[:], ps[:], xi[:], ALU.add, [bp, bxi], [bx2])
                                if hb < 3:
                                    kb.dma("sp", ffacc[rs_, cs_], x2[:], reads=[bx2], writes=[B_ffacc[c]])
                                else:
                                    TT("dve", x2[:], x2[:], gf[:, cs_], ALU.mult, [bx2, bgf], [bx2])
                                    TT("dve", x2[:], x2[:], xr_[:], ALU.add, [bx2, bxr_], [bx2])
                                    kb.dma("sp", xres[rs_, cs_], x2[:], reads=[bx2], writes=[B_xres[c]])

        if stop is None or stop == "final":
            with Scope(kb) as es:
                fg = sb("fg", [128, D], F32, es); bfg = Buf()
                kb.dma("sp", fg[:], final_g.partition_broadcast(128), writes=[bfg])
                xr = Rot(nc, es, "xr3", [128, D], F32, 2)
                junk = Rot(nc, es, "fj", [128, D], BF16, 1)
                ssr = Rot(nc, es, "fss", [128, 4], F32, 2)
                yo = Rot(nc, es, "fy", [128, D], F32, 2)
                for c in range(NCH):
                    xt, bx = xr.get()
                    kb.dma("sp", xt[:], xres[c * 128:(c + 1) * 128, :], reads=[B_xres[c]], writes=[bx])
                    jk, bj = junk.get(); ss, bss = ssr.get()
                    ACT(jk[:], xt[:], AF.Square, [bx], [bj, bss], accum=ss[:, 0:1])
                    RSTD(ss[:, 2:3], ss[:, 1:2], ss[:, 0:1], 1.0 / D, bss)
                    y_, by_ = yo.get()
                    STT("dve", y_[:], xt[:], ss[:, 2:3], fg[:], ALU.mult, ALU.mult, [bx, bss, bfg], [by_])
                    kb.dma("sp", y_out[c * 128:(c + 1) * 128, :], y_[:], reads=[by_], writes=[Buf()])

        kb.finish()
    return nc, kb


def _consts():
    j = np.arange(128)[:, None].astype(np.float32)
    i = np.arange(128)[None, :].astype(np.float32)
    c = np.zeros((128, 8, 128), np.float32)
    c[:, 0] = (j == i)
    c[:, 1] = (j <= i)
    c[:, 2] = (j >= i)
    c[:, 3] = 1.0
    c[:, 4] = i - j
    c[:, 5] = np.where(j <= i, 0.0, -1000.0)
    c[:, 6] = np.where(j >= i, 0.0, -1000.0)
    c[:, 7] = -(j == i).astype(np.float32)
    pos = np.arange(128, dtype=np.float32)[:, None]
    posk = np.concatenate([np.repeat(127.0 - pos, 8, 1), np.repeat(pos, 8, 1)], 1)
    posq = np.concatenate([np.repeat(pos + 1.0, 8, 1), np.repeat(128.0 - pos, 8, 1)], 1)
    return c, posk.astype(np.float32), posq.astype(np.float32)


def _rope_tables(L):
    GRID_W = 64
    n_rows = L // GRID_W
    rows = np.repeat(np.arange(n_rows, dtype=np.float32), GRID_W)
    cols = np.tile(np.arange(GRID_W, dtype=np.float32), n_rows)
    inv = (np.float32(10000.0) ** (-np.arange(32, dtype=np.float32) / np.float32(32))).astype(np.float32)
    ang = np.concatenate([rows[:, None] * inv, cols[:, None] * inv], -1).astype(np.float32)
    return np.cos(ang).astype(np.float32), np.sin(ang).astype(np.float32)


def make_in_maps(inp):
    f = lambda a: np.ascontiguousarray(np.asarray(a, dtype=np.float32))
    c3, posk, posq = _consts()
    rc, rs = _rope_tables(T)
    shared = {
        "consts": c3, "posk": posk, "posq": posq,
        "w_mod": f(inp["w_mod"]), "b_mod": f(inp["b_mod"]),
        "norm_mix_g": f(inp["norm_mix_g"]), "norm_mlp_g": f(inp["norm_mlp_g"]),
        "w_in": f(inp["w_in"]),
        "m_igate_b": f(inp["m_igate_b"]).reshape(DEPTH, 8), "m_fgate_b": f(inp["m_fgate_b"]).reshape(DEPTH, 8),
        "m_norm_g": f(inp["m_norm_g"]), "r_decay": f(inp["r_decay"]).reshape(DEPTH, 16),
        "r_norm_g": f(inp["r_norm_g"]), "s_conv_w": f(inp["s_conv_w"]), "s_conv_b": f(inp["s_conv_b"]),
        "s_dt_bias": f(inp["s_dt_bias"]).reshape(DEPTH, 32), "s_a_log": f(inp["s_a_log"]).reshape(DEPTH, 32),
        "s_d": f(inp["s_d"]), "s_norm_g": f(inp["s_norm_g"]),
        "w_br_m": f(inp["w_br_m"]), "w_br_r": f(inp["w_br_r"]), "w_br_s": f(inp["w_br_s"]),
        "w_out": f(inp["w_out"]), "w_ff1": f(inp["w_ff1"]), "w_ff2": f(inp["w_ff2"]),
        "final_norm_g": f(inp["final_norm_g"]),
    }
    maps = []
    xs = f(inp["x_sample"]); xp = f(inp["x_prompt"])
    for core in range(8):
        m = dict(shared)
        if core < 4:
            b = core
            m["x"] = xs[b]
            m["cvec"] = f(inp["c"])[b]
            m["st_mC"] = f(inp["state_mlstm_C"])[b]
            m["st_mn"] = f(inp["state_mlstm_n"])[b]
            m["st_mm"] = f(inp["state_mlstm_m"])[b]
            m["st_r"] = f(inp["state_ret"])[b]
            m["st_s"] = f(inp["state_ssd"])[b]
            fl = np.zeros((128, 2), np.float32); fl[:, 0] = 1.0
            m["flags"] = fl
            m["rope_c"] = rc; m["rope_s"] = rs
        else:
            p = core - 4
            m["x"] = np.ascontiguousarray(xp[8 * p:8 * p + 8].reshape(T, D))
            m["cvec"] = f(inp["c_ctx"])
            m["st_mC"] = np.zeros((DEPTH, 2, 4, 256, 256), np.float32)
            m["st_mn"] = np.zeros((DEPTH, 2, 4, 256), np.float32)
            m["st_mm"] = np.zeros((DEPTH, 2, 4), np.float32)
            m["st_r"] = np.zeros((DEPTH, 2, 8, 128, 128), np.float32)
            m["st_s"] = np.zeros((DEPTH, 2, 16, 64, 128), np.float32)
            fl = np.zeros((128, 2), np.float32); fl[:, 1] = 1.0
            m["flags"] = fl
            m["rope_c"] = np.ones((T, 64), np.float32); m["rope_s"] = np.zeros((T, 64), np.float32)
        maps.append(m)
    return maps


def kernel(**inputs):
    nc, kb = build_program()
    maps = make_in_maps(inputs)
    res = run_bass_kernel_spmd(nc, maps, core_ids=list(range(8)))
    r = res.results
    y_sample = np.stack([r[b]["y"] for b in range(4)], 0)
    y_prompt = np.concatenate([r[4 + p]["y"].reshape(8, 256, D) for p in range(4)], 0)
    cat = lambda k: np.concatenate([r[4 + p][k] for p in range(4)], 0)
    return (y_prompt.astype(np.float32), y_sample.astype(np.float32), cat("o_mC"), cat("o_mn"), cat("o_mm"),
            cat("o_r"), cat("o_s"))
```
